# Optimizing a Trainium2 kernel written in Bass

```python
import jax
import jax.numpy as jnp
from jax import lax
import numpy as np

D_MODEL = 2048
BATCH = 4
SEQ = 2048
DEPTH = 1

HEAD_DIM = 128
NSA_HEADS = 8
NSA_KV_HEADS = 2
NSA_GROUP = NSA_HEADS // NSA_KV_HEADS
CMP_BLOCK = 32
CMP_STRIDE = 16
CMP_HIDDEN = 256
SEL_BLOCK = 64
SEL_TOP = 8
WINDOW = 512
MLA_HEADS = 8
MLA_Q_RANK = 384
MLA_KV_RANK = 256
MLA_NOPE_DIM = 128
MLA_ROPE_DIM = 64
MLA_V_DIM = 128
ROPE_THETA = 10000.0
D_FF = 5632
Q_BLOCK = 128
EPS = 1e-6
NEG = -1e30
FORCE_SCORE = 1e4

NSA_WIDTH = NSA_HEADS * HEAD_DIM
MLA_WIDTH = MLA_HEADS * MLA_V_DIM
MIX_WIDTH = NSA_WIDTH + MLA_WIDTH
NSA_KV_WIDTH = NSA_KV_HEADS * HEAD_DIM
MLA_QK_DIM = MLA_NOPE_DIM + MLA_ROPE_DIM
IN_SPLITS = (NSA_WIDTH,) + (NSA_KV_WIDTH,) * 6 + (NSA_HEADS * 3, MLA_Q_RANK, MLA_KV_RANK, MLA_ROPE_DIM)
IN_DIM = sum(IN_SPLITS)

kernel_name = "hybrid_nsa_mla_macaron_layer"


def rms_norm(x, g):
    xf = x.astype(jnp.float32)
    y = xf * lax.rsqrt(jnp.mean(xf * xf, axis=-1, keepdims=True) + EPS)
    return (y * g.astype(jnp.float32)).astype(x.dtype)


def swiglu(x, w_gate, w_up, w_down):
    return (jax.nn.silu(x @ w_gate) * (x @ w_up)) @ w_down


def masked_softmax(scores, mask):
    s = jnp.where(mask, scores.astype(jnp.float32), NEG)
    p = jax.nn.softmax(s, axis=-1)
    return jnp.where(mask, p, 0.0)


def alibi_slopes(n):
    return 2.0 ** (-8.0 * jnp.arange(1, n + 1, dtype=jnp.float32) / n)


def rope(x, pos):
    half = x.shape[-1] // 2
    inv = ROPE_THETA ** (-jnp.arange(half, dtype=jnp.float32) / half)
    ang = pos.astype(jnp.float32)[:, None] * inv[None, :]
    cos = jnp.cos(ang)[None, :, None, :]
    sin = jnp.sin(ang)[None, :, None, :]
    x1 = x[..., :half].astype(jnp.float32)
    x2 = x[..., half:].astype(jnp.float32)
    return jnp.concatenate([x1 * cos - x2 * sin, x2 * cos + x1 * sin], axis=-1).astype(x.dtype)


def compress_blocks(kv, cmp_idx, pos_emb, w1, w2):
    B = kv.shape[0]
    n_cmp = cmp_idx.shape[0]
    blocks = kv[:, cmp_idx] + pos_emb[None, None, :, None, :]
    flat = blocks.transpose(0, 1, 3, 2, 4).reshape(B, n_cmp, NSA_KV_HEADS, CMP_BLOCK * HEAD_DIM)
    return jax.nn.gelu(flat @ w1) @ w2


def nsa_mixer(q, k_cmp, v_cmp, k_sel, v_sel, k_win, v_win, gates, q_gain, k_gains,
              cmp_pos_k, cmp_w1_k, cmp_w2_k, cmp_pos_v, cmp_w1_v, cmp_w2_v):
    B, S = q.shape[0], q.shape[1]
    H, G, R, D = NSA_HEADS, NSA_KV_HEADS, NSA_GROUP, HEAD_DIM
    slopes = alibi_slopes(H).reshape(G, R)
    scale = D ** -0.5
    q = rms_norm(q.reshape(B, S, H, D), q_gain).reshape(B, S, G, R, D)
    gates = gates.reshape(B, S, G, R, 3)

    n_cmp = (S - CMP_BLOCK) // CMP_STRIDE + 1
    cmp_idx = np.arange(n_cmp)[:, None] * CMP_STRIDE + np.arange(CMP_BLOCK)[None, :]
    kc = rms_norm(compress_blocks(k_cmp.reshape(B, S, G, D), cmp_idx, cmp_pos_k, cmp_w1_k, cmp_w2_k), k_gains[0])
    vc = compress_blocks(v_cmp.reshape(B, S, G, D), cmp_idx, cmp_pos_v, cmp_w1_v, cmp_w2_v)
    cmp_end = jnp.asarray(cmp_idx[:, -1], jnp.int32)
    cmp_centre = jnp.asarray(cmp_idx[:, 0] + 0.5 * (CMP_BLOCK - 1), jnp.float32)

    n_sel = S // SEL_BLOCK
    top = min(SEL_TOP, n_sel)
    sel_start_np = np.arange(n_sel) * SEL_BLOCK
    overlap = jnp.asarray((cmp_idx[:, :1] < sel_start_np[None, :] + SEL_BLOCK)
                          & (cmp_idx[:, -1:] >= sel_start_np[None, :]), jnp.float32)
    sel_start = jnp.asarray(sel_start_np, jnp.int32)
    ks = rms_norm(k_sel.reshape(B, S, G, D), k_gains[1])
    ks_blocks = ks.reshape(B, n_sel, SEL_BLOCK, G, D).transpose(0, 3, 1, 2, 4)
    vs_blocks = v_sel.reshape(B, n_sel, SEL_BLOCK, G, D).transpose(0, 3, 1, 2, 4)
    bi = jnp.arange(B)[:, None, None, None]
    gi = jnp.arange(G)[None, :, None, None]

    kw = rms_norm(k_win.reshape(B, S, G, D), k_gains[2])
    kw_pad = jnp.pad(kw, ((0, 0), (WINDOW, 0), (0, 0), (0, 0)))
    vw_pad = jnp.pad(v_win.reshape(B, S, G, D), ((0, 0), (WINDOW, 0), (0, 0), (0, 0)))
    n_win = WINDOW + Q_BLOCK

    def block(qb):
        q0 = qb * Q_BLOCK
        t = q0 + jnp.arange(Q_BLOCK)
        qq = lax.dynamic_slice_in_dim(q, q0, Q_BLOCK, axis=1)
        gb = lax.dynamic_slice_in_dim(gates, q0, Q_BLOCK, axis=1)

        dist_c = t[:, None].astype(jnp.float32) - cmp_centre[None, :]
        s_c = jnp.einsum('bqgrd,bngd->bgrqn', qq, kc) * scale - slopes[:, :, None, None] * dist_c
        p_c = masked_softmax(s_c, cmp_end[None, :] <= t[:, None])
        o_c = jnp.einsum('bgrqn,bngd->bqgrd', p_c.astype(vc.dtype), vc)

        imp = jnp.einsum('bgrqn,ns->bgqs', p_c, overlap)
        blk = jnp.arange(n_sel)
        forced = (blk[None, :] == 0) | (blk[None, :] == t[:, None] // SEL_BLOCK)
        future = sel_start[None, :] > t[:, None]
        imp = jnp.where(future, NEG, jnp.where(forced, FORCE_SCORE, imp))
        _, idx = lax.top_k(imp, top)
        kg = ks_blocks[bi, gi, idx].reshape(B, G, Q_BLOCK, top * SEL_BLOCK, D)
        vg = vs_blocks[bi, gi, idx].reshape(B, G, Q_BLOCK, top * SEL_BLOCK, D)
        s_pos = (idx[..., None] * SEL_BLOCK + jnp.arange(SEL_BLOCK)).reshape(B, G, Q_BLOCK, top * SEL_BLOCK)
        dist_s = t[None, None, :, None] - s_pos
        s_s = (jnp.einsum('bqgrd,bgqkd->bgrqk', qq, kg) * scale
               - slopes[None, :, :, None, None] * dist_s[:, :, None].astype(jnp.float32))
        p_s = masked_softmax(s_s, (dist_s >= 0)[:, :, None])
        o_s = jnp.einsum('bgrqk,bgqkd->bqgrd', p_s.astype(vg.dtype), vg)

        kwb = lax.dynamic_slice_in_dim(kw_pad, q0, n_win, axis=1)
        vwb = lax.dynamic_slice_in_dim(vw_pad, q0, n_win, axis=1)
        w_pos = q0 - WINDOW + jnp.arange(n_win)
        dist_w = t[:, None] - w_pos[None, :]
        m_w = (w_pos[None, :] >= 0) & (dist_w >= 0) & (dist_w < WINDOW)
        s_w = (jnp.einsum('bqgrd,bkgd->bgrqk', qq, kwb) * scale
               - slopes[:, :, None, None] * dist_w.astype(jnp.float32))
        p_w = masked_softmax(s_w, m_w)
        o_w = jnp.einsum('bgrqk,bkgd->bqgrd', p_w.astype(vwb.dtype), vwb)

        return gb[..., 0:1] * o_c + gb[..., 1:2] * o_s + gb[..., 2:3] * o_w

    out = lax.map(block, jnp.arange(S // Q_BLOCK))
    return out.transpose(1, 0, 2, 3, 4, 5).reshape(B, S, NSA_WIDTH)


def mla_mixer(c_q, c_kv, k_rope, q_a_gain, w_uq, kv_a_gain, w_ukv, q_gain, k_gain):
    B, S = c_q.shape[0], c_q.shape[1]
    Hm = MLA_HEADS
    pos = jnp.arange(S)
    q = (rms_norm(c_q, q_a_gain) @ w_uq).reshape(B, S, Hm, MLA_QK_DIM)
    kv = (rms_norm(c_kv, kv_a_gain) @ w_ukv).reshape(B, S, Hm, MLA_NOPE_DIM + MLA_V_DIM)
    k_nope, v = kv[..., :MLA_NOPE_DIM], kv[..., MLA_NOPE_DIM:]
    k = jnp.concatenate([k_nope, jnp.broadcast_to(k_rope[:, :, None, :], (B, S, Hm, MLA_ROPE_DIM))], axis=-1)
    q = rms_norm(q, q_gain)
    k = rms_norm(k, k_gain)
    q = jnp.concatenate([q[..., :MLA_NOPE_DIM], rope(q[..., MLA_NOPE_DIM:], pos)], axis=-1)
    k = jnp.concatenate([k[..., :MLA_NOPE_DIM], rope(k[..., MLA_NOPE_DIM:], pos)], axis=-1)
    scale = MLA_QK_DIM ** -0.5

    def block(qb):
        q0 = qb * Q_BLOCK
        t = q0 + jnp.arange(Q_BLOCK)
        qq = lax.dynamic_slice_in_dim(q, q0, Q_BLOCK, axis=1)
        s = jnp.einsum('bqhd,bkhd->bhqk', qq, k) * scale
        p = masked_softmax(s, pos[None, :] <= t[:, None])
        return jnp.einsum('bhqk,bkhd->bqhd', p.astype(v.dtype), v)

    out = lax.map(block, jnp.arange(S // Q_BLOCK))
    return out.transpose(1, 0, 2, 3, 4).reshape(B, S, MLA_WIDTH)


def setup_inputs(seed: int = 0) -> dict:
    key = jax.random.key(seed)
    keys = iter(jax.random.split(key, 40))

    def dense(shape, fan_in):
        return jax.random.normal(next(keys), (DEPTH,) + shape, jnp.float32) * fan_in ** -0.5

    def gain(shape):
        return 1.0 + 0.02 * jax.random.normal(next(keys), (DEPTH,) + shape, jnp.float32)

    def small(shape):
        return 0.1 * jax.random.normal(next(keys), (DEPTH,) + shape, jnp.float32)

    x = jax.random.normal(next(keys), (BATCH, SEQ, D_MODEL), jnp.float32)
    return {
        "x": x,
        "ffn1_norm": gain((D_MODEL,)),
        "ffn1_w_gate": dense((D_MODEL, D_FF), D_MODEL),
        "ffn1_w_up": dense((D_MODEL, D_FF), D_MODEL),
        "ffn1_w_down": dense((D_FF, D_MODEL), D_FF),
        "mix_norm": gain((D_MODEL,)),
        "w_in": dense((D_MODEL, IN_DIM), D_MODEL),
        "nsa_q_norm": gain((HEAD_DIM,)),
        "nsa_k_norm": gain((3, HEAD_DIM)),
        "nsa_cmp_pos_k": small((CMP_BLOCK, HEAD_DIM)),
        "nsa_cmp_w1_k": dense((CMP_BLOCK * HEAD_DIM, CMP_HIDDEN), CMP_BLOCK * HEAD_DIM),
        "nsa_cmp_w2_k": dense((CMP_HIDDEN, HEAD_DIM), CMP_HIDDEN),
        "nsa_cmp_pos_v": small((CMP_BLOCK, HEAD_DIM)),
        "nsa_cmp_w1_v": dense((CMP_BLOCK * HEAD_DIM, CMP_HIDDEN), CMP_BLOCK * HEAD_DIM),
        "nsa_cmp_w2_v": dense((CMP_HIDDEN, HEAD_DIM), CMP_HIDDEN),
        "mla_q_a_norm": gain((MLA_Q_RANK,)),
        "mla_w_uq": dense((MLA_Q_RANK, MLA_HEADS * MLA_QK_DIM), MLA_Q_RANK),
        "mla_kv_a_norm": gain((MLA_KV_RANK,)),
        "mla_w_ukv": dense((MLA_KV_RANK, MLA_HEADS * (MLA_NOPE_DIM + MLA_V_DIM)), MLA_KV_RANK),
        "mla_q_norm": gain((MLA_QK_DIM,)),
        "mla_k_norm": gain((MLA_QK_DIM,)),
        "out_norm_nsa": gain((NSA_WIDTH,)),
        "out_norm_mla": gain((MLA_WIDTH,)),
        "w_out": dense((MIX_WIDTH, D_MODEL), MIX_WIDTH),
        "ffn2_norm": gain((D_MODEL,)),
        "ffn2_w_gate": dense((D_MODEL, D_FF), D_MODEL),
        "ffn2_w_up": dense((D_MODEL, D_FF), D_MODEL),
        "ffn2_w_down": dense((D_FF, D_MODEL), D_FF),
    }


def reference(x, ffn1_norm, ffn1_w_gate, ffn1_w_up, ffn1_w_down, mix_norm, w_in,
              nsa_q_norm, nsa_k_norm, nsa_cmp_pos_k, nsa_cmp_w1_k, nsa_cmp_w2_k,
              nsa_cmp_pos_v, nsa_cmp_w1_v, nsa_cmp_w2_v,
              mla_q_a_norm, mla_w_uq, mla_kv_a_norm, mla_w_ukv, mla_q_norm, mla_k_norm,
              out_norm_nsa, out_norm_mla, w_out,
              ffn2_norm, ffn2_w_gate, ffn2_w_up, ffn2_w_down):
    offsets = [int(v) for v in np.cumsum(IN_SPLITS)[:-1]]
    for l in range(DEPTH):
        h = rms_norm(x, ffn1_norm[l])
        x = x + 0.5 * swiglu(h, ffn1_w_gate[l], ffn1_w_up[l], ffn1_w_down[l])

        h = rms_norm(x, mix_norm[l])
        proj = h @ w_in[l]
        (q_a, k_cmp, v_cmp, k_sel, v_sel, k_win, v_win, g_a,
         c_q, c_kv, k_rope) = jnp.split(proj, offsets, axis=-1)
        gates = jax.nn.sigmoid(g_a.astype(jnp.float32)).astype(x.dtype)
        o_a = nsa_mixer(q_a, k_cmp, v_cmp, k_sel, v_sel, k_win, v_win, gates,
                        nsa_q_norm[l], nsa_k_norm[l],
                        nsa_cmp_pos_k[l], nsa_cmp_w1_k[l], nsa_cmp_w2_k[l],
                        nsa_cmp_pos_v[l], nsa_cmp_w1_v[l], nsa_cmp_w2_v[l])
        o_b = mla_mixer(c_q, c_kv, k_rope, mla_q_a_norm[l], mla_w_uq[l],
                        mla_kv_a_norm[l], mla_w_ukv[l], mla_q_norm[l], mla_k_norm[l])
        mixed = jnp.concatenate([rms_norm(o_a, out_norm_nsa[l]), rms_norm(o_b, out_norm_mla[l])], axis=-1)
        x = x + mixed @ w_out[l]

        h = rms_norm(x, ffn2_norm[l])
        x = x + 0.5 * swiglu(h, ffn2_w_gate[l], ffn2_w_up[l], ffn2_w_down[l])
    return x
```

```python
import numpy as np
from contextlib import ExitStack
import concourse.bass as bass
import concourse.mybir as mybir
from concourse.bass_utils import run_bass_kernel_spmd

F32 = mybir.dt.float32
BF16 = mybir.dt.bfloat16
AF = mybir.ActivationFunctionType
ALU = mybir.AluOpType
AX = mybir.AxisListType

DM = 2048
KC = 16
IN_DIM = 3288
EPS = 1e-6
BIG = 30000.0
NEG = -1e30
GCH = 11
NTAB = 96
DEBUG = False
LAST = {}


class Op:
    __slots__ = ("eng", "fn", "deps", "signal", "val", "dma", "idx")


class Sched:
    CE = ("pe", "act", "dve", "pool")
    ALL = ("pe", "act", "dve", "pool", "sp")

    def __init__(self, nc):
        self.nc = nc
        self.ops = {e: [] for e in self.ALL}
        self.lastw = {}
        self.readers = {}
        self.dcount = {}
        self.last = {}
        self.n = 0

    def add(self, eng, fn, reads=(), writes=(), dma=None):
        op = Op()
        op.eng, op.fn, op.signal, op.dma, op.val = eng, fn, False, dma, None
        op.idx = self.n
        self.n += 1
        deps = {}

        def dep(d):
            if d.dma is None and d.eng == "pe" and eng == "pe":
                return
            k = ("d", d.dma) if d.dma is not None else ("e", d.eng)
            c = deps.get(k)
            if c is None or c.idx < d.idx:
                deps[k] = d

        for k in reads:
            w = self.lastw.get(k)
            if w is not None:
                dep(w)
        for k in writes:
            w = self.lastw.get(k)
            if w is not None:
                dep(w)
            for r in self.readers.get(k, ()):
                dep(r)
        op.deps = list(deps.values())
        for d in op.deps:
            d.signal = True
        for k in reads:
            self.readers.setdefault(k, []).append(op)
        for k in writes:
            self.lastw[k] = op
            self.readers[k] = []
        if dma is not None:
            self.dcount[dma] = self.dcount.get(dma, 0) + 16
            op.val = (dma, self.dcount[dma])
            self.last[("d", dma)] = op
        else:
            self.last[("e", eng)] = op
        self.ops[eng].append(op)
        return op

    def barrier(self):
        lasts = list(self.last.values())
        for e in self.ALL:
            op = Op()
            op.eng, op.fn, op.signal, op.dma, op.val = e, None, False, None, None
            op.idx = self.n
            self.n += 1
            op.deps = [d for d in lasts if not (d.dma is None and d.eng == e)]
            for d in op.deps:
                d.signal = True
            self.ops[e].append(op)
        self.lastw = {}
        self.readers = {}

    def emit(self, es):
        nc = self.nc
        for e in self.CE:
            c = 0
            for op in self.ops[e]:
                if op.fn is not None and op.dma is None and op.signal:
                    c += 1
                    op.val = ("prog_" + e, c)
        names = ["prog_" + e for e in self.CE] + sorted(self.dcount.keys())
        sems = {n: es.enter_context(nc.semaphore(n)) for n in names}
        block = es.enter_context(nc.Block())

        def mk(e):
            def body(eng):
                known = {}
                for op in self.ops[e]:
                    need = {}
                    for d in op.deps:
                        n, v = d.val
                        if need.get(n, 0) < v:
                            need[n] = v
                    for n, v in need.items():
                        if known.get(n, 0) < v:
                            eng.wait_ge(sems[n], v)
                            known[n] = v
                    if op.fn is None:
                        continue
                    ins = op.fn(eng)
                    if op.dma is not None:
                        ins.then_inc(sems[op.dma], 16)
                    elif op.signal:
                        ins.then_inc(sems["prog_" + e], 1)
            return body

        block.tensor(mk("pe"))
        block.scalar(mk("act"))
        block.vector(mk("dve"))
        block.gpsimd(mk("pool"))
        block.sync(mk("sp"))


class Stream:
    def __init__(self, kb, name, bufs, slabs, live=1):
        self.kb, self.name, self.bufs, self.slabs = kb, name, bufs, slabs
        self.nxt = 0
        self.live = live

    def _issue(self, i):
        dstf, src = self.slabs[i]
        b = i % len(self.bufs)
        self.kb.dma("pool", dstf(self.bufs[b]), src, [], [(self.name, b)], sem=f"{self.name}{b}")

    def use(self, i):
        nb = len(self.bufs)
        while self.nxt < len(self.slabs) and self.nxt <= i + nb - self.live:
            self._issue(self.nxt)
            self.nxt += 1
        b = i % nb
        return self.slabs[i][0](self.bufs[b]), (self.name, b)


class KB:
    def __init__(self, nc, S, DFF):
        self.nc = nc
        self.S, self.DFF = S, DFF
        self.NB = S // 128
        self.NJ = self.NB // 2
        self.SO = S // 2
        self.FC = DFF // 128
        self.NG = self.FC // GCH
        assert self.FC % GCH == 0 and S % 512 == 0 and self.SO % 512 == 0
        self.s = Sched(nc)
        self.es = ExitStack()
        self.rotc = {}
        self.bankc = 0
        self.ssi = 0

    def setup_mem(self):
        nc = self.nc
        self.AW = 52600
        self.arena = self.es.enter_context(nc.sbuf_tensor("arena", [128, self.AW], F32))
        self.pst = self.es.enter_context(nc.psum_tensor("ps", [128, 8 * 512], F32))
        self.ps = [self.pst[:, b * 512:(b + 1) * 512] for b in range(8)]
        self.top = 0

    def take(self, shape, dt):
        n = int(np.prod(shape))
        nbytes = n * (4 if dt == F32 else 2)
        n32 = (nbytes + 3) // 4
        n32 = (n32 + 7) // 8 * 8
        off = self.top
        self.top += n32
        assert self.top <= self.AW, f"arena overflow {self.top}"
        ap = self.arena[:, off:off + n32]
        if dt != F32:
            ap = ap.bitcast(dt)
        ap = ap[:, 0:n]
        if len(shape) == 2:
            ap = ap.rearrange("p (a b) -> p a b", b=shape[1])
        elif len(shape) == 3:
            ap = ap.rearrange("p (a b c) -> p a b c", b=shape[1], c=shape[2])
        return ap

    def bank(self):
        b = self.bankc % 8
        self.bankc += 1
        return b

    def mm(self, out, lhsT, rhs, start, stop, r, w, sgc=False):
        if sgc:
            self.s.add("pe", lambda e: e.matmul(out, lhsT, rhs, start=start, stop=stop, skip_group_check=True), r, w)
        else:
            self.s.add("pe", lambda e: e.matmul(out, lhsT, rhs, start=start, stop=stop), r, w)

    def tr(self, out, in_, ident, r, w):
        self.s.add("pe", lambda e: e.transpose(out, in_, ident), r, w)

    def act(self, out, in_, func, r, w, bias=None, scale=None):
        kw = {}
        if bias is not None:
            kw["bias"] = bias
        if scale is not None:
            kw["scale"] = scale
        self.s.add("act", lambda e: e.activation(out, in_, func, **kw), r, w)

    def ts(self, out, in0, s1, s2, op0, op1, r, w, eng="dve"):
        if op1 is None:
            self.s.add(eng, lambda e: e.tensor_scalar(out, in0, s1, None, op0), r, w)
        else:
            self.s.add(eng, lambda e: e.tensor_scalar(out, in0, s1, s2, op0, op1), r, w)

    def tt(self, out, in0, in1, op, r, w, eng="dve"):
        self.s.add(eng, lambda e: e.tensor_tensor(out, in0, in1, op), r, w)

    def stt(self, out, in0, sc, in1, op0, op1, r, w):
        self.s.add("dve", lambda e: e.scalar_tensor_tensor(out, in0, sc, in1, op0, op1), r, w)

    def cp(self, out, in_, r, w, eng="dve"):
        self.s.add(eng, lambda e: e.tensor_copy(out, in_), r, w)

    def red(self, out, in_, r, w):
        self.s.add("dve", lambda e: e.tensor_reduce(out, in_, AX.X, ALU.add), r, w)

    def recip(self, out, in_, r, w):
        self.s.add("dve", lambda e: e.reciprocal(out, in_), r, w)

    def mset(self, ap, val, w, eng="dve"):
        self.s.add(eng, lambda e: e.memset(ap, val), [], w)

    def dma(self, eng, out, in_, r, w, sem, cap=False):
        sem = ("g_" if eng == "pool" else "h_") + sem
        if cap:
            self.s.add(eng, lambda e: e.dma_start(out=out, in_=in_, max_dma_last_dim=4096), r, w, dma=sem)
        else:
            self.s.add(eng, lambda e: e.dma_start(out=out, in_=in_), r, w, dma=sem)

    def declare(self):
        nc, S, DFF, SO, NJ, NB = self.nc, self.S, self.DFF, self.SO, self.NJ, self.NB
        D = {}

        def inp(name, shape):
            D[name] = nc.dram_tensor(name, list(shape), F32, kind="ExternalInput").ap()

        def scr(name, shape, dt):
            D[name] = nc.dram_tensor(name, list(shape), dt, kind=("ExternalOutput" if DEBUG else "Internal")).ap()

        inp("x", [S, DM])
        for f in ("f1", "f2"):
            inp(f + "g", [DM, DFF]); inp(f + "u", [DM, DFF]); inp(f + "d", [DFF, DM])
        inp("w_in", [DM, IN_DIM]); inp("w_out", [DM, DM])
        inp("cw1k", [4096, 256]); inp("cw1v", [4096, 256]); inp("cw2k", [256, 128]); inp("cw2v", [256, 128])
        inp("w_uq", [384, 1536]); inp("w_ukv", [256, 2048])
        inp("tab", [128, NTAB]); inp("posk", [128, 32]); inp("posv", [128, 32]); inp("ident", [128, 128])
        inp("mAB", [128, 2 * 512]); inp("mW", [128, 6 * 512]); inp("mC", [128, NJ * 512])
        inp("Ltab", [36, NB * 128]); inp("Rs", [68, NJ * 2 * 512]); inp("Lc", [68, 128])
        inp("keep", [128, NJ * 32]); inp("addt", [128, NJ * 32]); inp("ov", [128, 32])
        inp("cosk", [64, S]); inp("sink", [64, S]); inp("cosq", [64, SO]); inp("sinq", [64, SO])
        D["out"] = nc.dram_tensor("out", [SO, DM], F32, kind="ExternalOutput").ap()
        scr("x1own", [SO, DM], F32)
        scr("qT_s", [128, NJ * 8 * 128], BF16)
        scr("kcmpT_s", [128, 2 * S], BF16); scr("vcmpT_s", [128, 2 * S], BF16)
        scr("kselT_s", [128, 2 * S], BF16); scr("kwinT_s", [128, 2 * S], BF16)
        scr("vsel_s", [S, 256], BF16); scr("vwin_s", [S, 256], BF16)
        scr("gates_s", [SO, 24], F32)
        scr("cqnT_s", [128, 3 * SO], BF16); scr("ckvnT_s", [128, 2 * S], BF16)
        scr("kraw_s", [64, S], BF16); scr("krot_s", [64, S], F32)
        scr("oa_s", [SO, 1024], F32); scr("ob_s", [SO, 1024], F32)
        scr("mixT_s", [128, 16 * SO], BF16)
        self.D = D

    def consts(self):
        D = self.D
        self.tab = self.take([NTAB], F32)
        self.identf = self.take([128], F32)
        self.identb = self.take([128], BF16)
        self.onesb = self.take([128], BF16)
        self.dma("sp", self.tab, D["tab"], [], ["tab"], sem="c0")
        self.dma("sp", self.identf, D["ident"], [], ["identf"], sem="c1")
        self.cp(self.identb, self.identf, ["identf"], ["identb"])
        self.mset(self.onesb, 1.0, ["onesb"])
        self.cbase = self.top

    def tcol(self, c, n=1, p=128):
        return self.tab[0:p, c:c + n]

    def norm_T(self, src, srckey, F, gcol, dst, dstkey, xn, tb0, xnkey="xn"):
        nk = F // 128
        si = self.ssi % 4
        self.ssi += 1
        ss = self.ssb[:, 4 * si:4 * si + 4]
        k0_, k1_, k2_ = ("ss", si, 0), ("ss", si, 1), ("ss", si, 2)
        jk = self.junk
        self.act(jk[:, 0:F], src, AF.Square, [srckey], ["jk"])
        self.red(ss[:, 0:1], jk[:, 0:F], ["jk"], [k0_])
        self.act(ss[:, 1:2], ss[:, 0:1], AF.Sqrt, [k0_], [k1_], bias=EPS, scale=1.0 / F)
        self.recip(ss[:, 2:3], ss[:, 1:2], [k1_], [k2_])
        self.ts(xn[:, 0:F], src, ss[:, 2:3], None, ALU.mult, None, [srckey, k2_], [xnkey])
        for k0 in range(0, nk, 4):
            b = tb0 + (k0 // 4) % 2
            for kk in range(4):
                k = k0 + kk
                self.tr(self.ps[b][:, kk * 128:(kk + 1) * 128], xn[:, k * 128:(k + 1) * 128], self.identf,
                        [xnkey, "identf"], [("ps", b)])
            for kk in range(4):
                k = k0 + kk
                self.ts(dst(k), self.ps[b][:, kk * 128:(kk + 1) * 128], self.tcol(gcol + k), None,
                        ALU.mult, None, [("ps", b), "tab"], [dstkey(k)])

    def ffn_slabs(self, wg, wu, wd):
        A, B = [], []
        wgv = wg.rearrange("(k p) c -> p k c", p=128)
        wuv = wu.rearrange("(k p) c -> p k c", p=128)
        wdv = wd.rearrange("(c p) n -> p c n", p=128)
        pieces = []
        for g in range(self.NG):
            c = 0
            while c < GCH:
                n = min(4, GCH - c)
                pieces.append((g, c, n))
                c += n
        for (g, c, n) in pieces:
            c0 = (g * GCH + c) * 128
            for wv in (wgv, wuv):
                A.append(((lambda buf, n=n: buf[:, :, 0:n * 128]), wv[:, :, c0:c0 + n * 128]))
        for g in range(self.NG):
            for dc in range(4):
                B.append(((lambda buf: buf), wdv[:, g * GCH:(g + 1) * GCH, dc * 512:(dc + 1) * 512]))
        return pieces, A, B

    def ffn(self, stA, a0, stB, b0, pieces, gcol):
        xres, hT, actT = self.xres, self.hT, self.actT
        for c in range(4):
            self.norm_T(xres[:, c, :], ("xres", c), DM, gcol,
                        (lambda k, c=c: hT[:, k, c * 128:(c + 1) * 128]), (lambda k: ("hT", k)), self.xn, 4)
        ai = a0
        bi = b0
        cc = 0
        for g in range(self.NG):
            for (pg, c, n) in pieces:
                if pg != g:
                    continue
                bufG, keyG = stA.use(ai)
                bufU, keyU = stA.use(ai + 1)
                ai += 2
                for i in range(n):
                    ci = c + i
                    bg, bu = 2 * (cc % 2), 2 * (cc % 2) + 1
                    for (bnk, buf, key) in ((bg, bufG, keyG), (bu, bufU, keyU)):
                        for k in range(KC):
                            self.mm(self.ps[bnk], buf[:, k, i * 128:(i + 1) * 128], hT[:, k, :],
                                    k == 0, k == KC - 1, [key, ("hT", k)], [("ps", bnk)])
                    sg = self.sg[cc % 2]
                    self.act(sg, self.ps[bg], AF.Silu, [("ps", bg)], [("sg", cc % 2)])
                    self.tt(actT[:, ci, :], sg, self.ps[bu], ALU.mult, [("sg", cc % 2), ("ps", bu)], [("act", ci)])
                    cc += 1
            for dc in range(4):
                bufD, keyD = stB.use(bi)
                bi += 1
                for tc in range(4):
                    bnk = 4 + tc
                    for ci in range(GCH):
                        self.mm(self.ps[bnk], actT[:, ci, tc * 128:(tc + 1) * 128], bufD[:, ci, :],
                                ci == 0, ci == GCH - 1, [("act", ci), keyD], [("ps", bnk)])
                    xs = xres[:, tc, dc * 512:(dc + 1) * 512]
                    self.stt(xs, self.ps[bnk], 0.5, xs, ALU.mult, ALU.add, [("ps", bnk), ("xres", tc)], [("xres", tc)])
        return ai, bi

    def fm_rstd(self, parts, N, c1, c2, rs, rskey):
        bss = self.bank()
        sqs = []
        for i, (ap, key, P) in enumerate(parts):
            sq, sqk = self.rot("sq", 3)
            self.act(sq[0:P, 0:N], ap, AF.Square, [key], [sqk])
            sqs.append((sq, sqk, P))
        for i, (sq, sqk, P) in enumerate(sqs):
            self.mm(self.ps[bss][:, 0:N], self.onesb[0:P, :], sq[0:P, 0:N], i == 0, i == len(sqs) - 1,
                    [sqk, "onesb"], [("ps", bss)])
        self.act(rs[:, 0:N], self.ps[bss][:, 0:N], AF.Sqrt, [("ps", bss)], [rskey], bias=c2, scale=c1)
        self.recip(rs[:, 0:N], rs[:, 0:N], [rskey], [rskey])

    def ksem(self, key):
        return "s_%s%d" % (key[0], key[1])

    def rot(self, name, n):
        i = self.rotc.get(name, 0)
        self.rotc[name] = i + 1
        return self.rbuf[name][i % n], (name, i % n)

    def phase1(self):
        D, S = self.D, self.S
        NT = S // 512
        self.top = self.cbase
        self.xres = self.take([4, DM], F32)
        self.hT = self.take([KC, 512], BF16)
        self.actT = self.take([GCH, 512], BF16)
        self.WA = [self.take([KC, 512], BF16) for _ in range(4)]
        self.WB = [self.take([GCH, 512], BF16) for _ in range(3)]
        self.xn = self.take([DM], F32)
        self.sg = [self.take([512], F32) for _ in range(2)]
        self.ssb = self.take([16], F32)
        self.junk = self.take([DM], BF16)
        self.p13_top = self.top
        self.hoT = self.take([KC, 256], BF16)
        self.rbuf = {
            "sq": [self.take([512], BF16) for _ in range(3)],
            "rs": [self.take([512], F32) for _ in range(2)],
            "stb": [self.take([512], BF16) for _ in range(4)],
            "stf": [self.take([512], F32) for _ in range(3)],
        }
        cosb = self.take([512], F32)
        sinb = self.take([512], F32)
        self.p1save = (self.ssb, self.rbuf, self.junk)

        pieces, A1, B1 = self.ffn_slabs(D["f1g"], D["f1u"], D["f1d"])
        wv = D["w_in"].rearrange("(k p) c -> p k c", p=128)
        wslabs = [(0, 512), (512, 1024), (1024, 1536), (1536, 2048), (2048, 2560), (2560, 2968), (2968, 3288)]
        Aw = [((lambda buf, n=(b - a): buf[:, :, 0:n]), wv[:, :, a:b]) for (a, b) in wslabs]
        A = []
        Bl = []
        for t in range(NT):
            A += A1 + Aw
            Bl += B1
        stA = Stream(self, "WA", self.WA, A, live=2)
        stB = Stream(self, "WB", self.WB, Bl)
        nA1 = len(A1)
        m0, m1 = self.tcol(79), self.tcol(80)
        hT, hoT, xres = self.hT, self.hoT, self.xres
        scale_n = 128.0 ** -0.5

        for t in range(NT):
            a0 = t * (nA1 + len(Aw))
            b0 = t * len(B1)
            if t == 0:
                for c in range(4):
                    self.dma("sp", xres[:, c, :], D["x"][c * 128:(c + 1) * 128, :], [], [("xres", c)], sem=f"xr{c}")
            self.ffn(stA, a0, stB, b0, pieces, 0)
            for i in range(2):
                self.ts(self.xn, xres[:, 2 * i + 1, :], m1, None, ALU.mult, None, [("xres", 2 * i + 1), "tab"], ["xn"])
                self.stt(self.xn, xres[:, 2 * i, :], m0, self.xn, ALU.mult, ALU.add, [("xres", 2 * i), "tab", "xn"], ["xn"])
                r0 = (2 * t + i) * 128
                self.dma("sp", D["x1own"][r0:r0 + 128, :], self.xn, ["xn"], [("x1own", 2 * t + i)], sem="x1o")
            for c in range(4):
                self.norm_T(xres[:, c, :], ("xres", c), DM, 16,
                            (lambda k, c=c: hT[:, k, c * 128:(c + 1) * 128]), (lambda k: ("hT", k)), self.xn, 4)
            if t + 1 < NT:
                for c in range(4):
                    r0 = (4 * (t + 1) + c) * 128
                    self.dma("sp", xres[:, c, :], D["x"][r0:r0 + 128, :], [], [("xres", c)], sem=f"xr{c}")
            tmp = self.xn.bitcast(BF16)[:, 0:KC * 128].rearrange("p (k n) -> p k n", n=128)
            for i in range(2):
                ev = hT[:, :, (2 * i) * 128:(2 * i + 1) * 128]
                od = hT[:, :, (2 * i + 1) * 128:(2 * i + 2) * 128]
                self.ts(tmp, od, m1, None, ALU.mult, None, [("hT", kk_) for kk_ in range(KC)] + ["tab"], ["xn"])
                self.stt(hoT[:, :, i * 128:(i + 1) * 128], ev, m0, tmp, ALU.mult, ALU.add, [("hT", kk_) for kk_ in range(KC)] + ["tab", "xn"], ["hoT"])
            ai = a0 + nA1
            tc0 = t * 512
            oc0 = t * 256
            qv = D["qT_s"].rearrange("p (j h n) -> p j h n", h=8, n=128)
            for sl in range(2):
                buf, key = stA.use(ai); ai += 1
                for hh in range(4):
                    h = sl * 4 + hh
                    bq = self.bank()
                    for k in range(KC):
                        self.mm(self.ps[bq][:, 0:256], buf[:, k, hh * 128:(hh + 1) * 128], hoT[:, k, :],
                                k == 0, k == KC - 1, [key, "hoT"], [("ps", bq)])
                    rs, rsk = self.rot("rs", 2)
                    self.fm_rstd([(self.ps[bq][:, 0:256], ("ps", bq), 128)], 256, 1.0, EPS * 128.0, rs, rsk)
                    st, stk = self.rot("stb", 4)
                    self.stt(st[:, 0:256], self.ps[bq][:, 0:256], self.tcol(64), rs[:, 0:256], ALU.mult, ALU.mult,
                             [("ps", bq), "tab", rsk], [stk])
                    self.dma("sp", qv[:, 2 * t:2 * t + 2, h, :], st[:, 0:256].rearrange("p (j n) -> p j n", n=128),
                             [stk], [("qT_s", t, h)], sem=self.ksem(stk))
            buf, key = stA.use(ai); ai += 1
            for idx, name in enumerate(("kcmpT_s", "kcmpT_s", "vcmpT_s", "vcmpT_s")):
                g = idx % 2
                bq = self.bank()
                for k in range(KC):
                    self.mm(self.ps[bq], buf[:, k, idx * 128:(idx + 1) * 128], hT[:, k, :], k == 0, k == KC - 1,
                            [key, ("hT", k)], [("ps", bq)])
                st, stk = self.rot("stb", 4)
                self.act(st, self.ps[bq], AF.Copy, [("ps", bq)], [stk])
                self.dma("sp", D[name][:, g * S + tc0:g * S + tc0 + 512], st, [stk], [(name, g, t)], sem=self.ksem(stk))
            for (kname, vname, gc) in (("kselT_s", "vsel_s", 66), ("kwinT_s", "vwin_s", 67)):
                buf, key = stA.use(ai); ai += 1
                for g in range(2):
                    bq = self.bank()
                    for k in range(KC):
                        self.mm(self.ps[bq], buf[:, k, g * 128:(g + 1) * 128], hT[:, k, :], k == 0, k == KC - 1,
                                [key, ("hT", k)], [("ps", bq)])
                    rs, rsk = self.rot("rs", 2)
                    self.fm_rstd([(self.ps[bq], ("ps", bq), 128)], 512, 1.0 / 128.0, EPS, rs, rsk)
                    st, stk = self.rot("stb", 4)
                    self.stt(st, self.ps[bq], self.tcol(gc), rs, ALU.mult, ALU.mult, [("ps", bq), "tab", rsk], [stk])
                    self.dma("sp", D[kname][:, g * S + tc0:g * S + tc0 + 512], st, [stk], [(kname, g, t)], sem=self.ksem(stk))
                for tc in range(4):
                    bq = self.bank()
                    for k in range(KC):
                        self.mm(self.ps[bq][:, 0:256], hT[:, k, tc * 128:(tc + 1) * 128], buf[:, k, 256:512],
                                k == 0, k == KC - 1, [key, ("hT", k)], [("ps", bq)])
                    st, stk = self.rot("stb", 4)
                    self.act(st[:, 0:256], self.ps[bq][:, 0:256], AF.Copy, [("ps", bq)], [stk])
                    r0 = tc0 + tc * 128
                    self.dma("sp", D[vname][r0:r0 + 128, :], st[:, 0:256], [stk], [(vname, 4 * t + tc)], sem=self.ksem(stk))
            buf, key = stA.use(ai); ai += 1
            for oc in range(2):
                bq = self.bank()
                for k in range(KC):
                    self.mm(self.ps[bq][:, 0:24], hoT[:, k, oc * 128:(oc + 1) * 128], buf[:, k, 0:24],
                            k == 0, k == KC - 1, [key, "hoT"], [("ps", bq)])
                st, stk = self.rot("stf", 3)
                self.act(st[:, 0:24], self.ps[bq][:, 0:24], AF.Sigmoid, [("ps", bq)], [stk])
                r0 = oc0 + oc * 128
                self.dma("sp", D["gates_s"][r0:r0 + 128, :], st[:, 0:24], [stk], [("gates_s", 2 * t + oc)], sem=self.ksem(stk))
            bqs = []
            for kq in range(3):
                bq = self.bank()
                bqs.append(bq)
                for k in range(KC):
                    self.mm(self.ps[bq][:, 0:256], buf[:, k, 24 + kq * 128:24 + (kq + 1) * 128], hoT[:, k, :],
                            k == 0, k == KC - 1, [key, "hoT"], [("ps", bq)])
            rs, rsk = self.rot("rs", 2)
            self.fm_rstd([(self.ps[b][:, 0:256], ("ps", b), 128) for b in bqs], 256, 1.0 / 384.0, EPS, rs, rsk)
            for kq in range(3):
                st, stk = self.rot("stb", 4)
                self.stt(st[:, 0:256], self.ps[bqs[kq]][:, 0:256], self.tcol(68 + kq), rs[:, 0:256], ALU.mult, ALU.mult,
                         [("ps", bqs[kq]), "tab", rsk], [stk])
                self.dma("sp", D["cqnT_s"][:, kq * self.SO + oc0:kq * self.SO + oc0 + 256], st[:, 0:256], [stk],
                         [("cqnT_s", kq, t)], sem=self.ksem(stk))
            buf, key = stA.use(ai); ai += 1
            bqs = []
            for kq in range(2):
                bq = self.bank()
                bqs.append(bq)
                for k in range(KC):
                    self.mm(self.ps[bq], buf[:, k, kq * 128:(kq + 1) * 128], hT[:, k, :], k == 0, k == KC - 1,
                            [key, ("hT", k)], [("ps", bq)])
            rs, rsk = self.rot("rs", 2)
            self.fm_rstd([(self.ps[b], ("ps", b), 128) for b in bqs], 512, 1.0 / 256.0, EPS, rs, rsk)
            for kq in range(2):
                st, stk = self.rot("stb", 4)
                self.stt(st, self.ps[bqs[kq]], self.tcol(71 + kq), rs, ALU.mult, ALU.mult,
                         [("ps", bqs[kq]), "tab", rsk], [stk])
                self.dma("sp", D["ckvnT_s"][:, kq * S + tc0:kq * S + tc0 + 512], st, [stk], [("ckvnT_s", kq, t)], sem=self.ksem(stk))
            br = self.bank()
            bs = self.bank()
            for k in range(KC):
                self.mm(self.ps[br][0:64, :], buf[:, k, 256:320], hT[:, k, :], k == 0, k == KC - 1, [key, ("hT", k)], [("ps", br)])
            for k in range(KC):
                self.mm(self.ps[bs][0:32, :], buf[:, k, 288:320], hT[:, k, :], k == 0, k == KC - 1, [key, ("hT", k)], [("ps", bs)])
            for k in range(KC):
                self.mm(self.ps[bs][32:64, :], buf[:, k, 256:288], hT[:, k, :], k == 0, k == KC - 1, [key, ("hT", k)], [("ps", bs)])
            st, stk = self.rot("stb", 4)
            self.act(st[0:64, :], self.ps[br][0:64, :], AF.Copy, [("ps", br)], [stk])
            self.dma("sp", D["kraw_s"][:, tc0:tc0 + 512], st[0:64, :], [stk], [("kraw_s", t)], sem=self.ksem(stk))
            self.dma("sp", cosb[0:64, :], D["cosk"][:, tc0:tc0 + 512], [], ["cosb"], sem="cs0")
            self.dma("sp", sinb[0:64, :], D["sink"][:, tc0:tc0 + 512], [], ["sinb"], sem="cs1")
            t1, t1k = self.rot("stf", 3)
            t2, t2k = self.rot("stf", 3)
            self.stt(t1[0:64, :], self.ps[br][0:64, :], self.tcol(77, 1, 64), cosb[0:64, :], ALU.mult, ALU.mult,
                     [("ps", br), "tab", "cosb"], [t1k])
            self.stt(t2[0:64, :], self.ps[bs][0:64, :], self.tcol(78, 1, 64), sinb[0:64, :], ALU.mult, ALU.mult,
                     [("ps", bs), "tab", "sinb"], [t2k])
            self.tt(t1[0:64, :], t1[0:64, :], t2[0:64, :], ALU.add, [t1k, t2k], [t1k])
            self.dma("sp", D["krot_s"][:, tc0:tc0 + 512], t1[0:64, :], [t1k], [("krot_s", t)], sem=self.ksem(t1k))

    def phase2(self):
        D, S, SO, NJ, NB = self.D, self.S, self.SO, self.NJ, self.NB
        self.s.barrier()
        self.top = self.cbase
        self.bankc = 0
        self.ssb = self.take([16], F32)
        self.junk = self.take([1024], BF16)
        self.rbuf = {
            "sq": [self.take([512], BF16) for _ in range(3)],
            "rs": [self.take([512], F32) for _ in range(2)],
            "stb": [self.take([512], BF16) for _ in range(4)],
            "stf": [self.take([512], F32) for _ in range(3)],
            "pt": [self.take([512], BF16) for _ in range(4)],
            "of": [self.take([1024], F32) for _ in range(2)],
            "sm": [self.take([64], F32) for _ in range(4)],
        }
        kcT = self.take([2, 128], BF16)
        vcaug = self.take([2, 162], BF16)
        qnT = self.take([8, SO], BF16)
        qrT = self.take([8, SO], BF16)
        mAB = self.take([2, 512], BF16)
        self.dma("pool", mAB, D["mAB"].rearrange("p (a n) -> p a n", n=512), [], ["mAB"], sem="t0")
        p2base = self.top

        kcA = self.take([2, S], BF16)
        vcA = self.take([2, S], BF16)
        w1k = self.take([32, 256], BF16)
        w1v = self.take([32, 256], BF16)
        w2k = self.take([2, 128], BF16)
        w2v = self.take([2, 128], BF16)
        posk = self.take([32], BF16)
        posv = self.take([32], BF16)
        ovb = self.take([32], F32)
        hid = [self.take([128], BF16) for _ in range(2)]
        self.dma("sp", kcA, D["kcmpT_s"].rearrange("p (g n) -> p g n", n=S), [("kcmpT_s", g, t) for g in range(2) for t in range(S // 512)], ["kcA"], sem="l0")
        self.dma("sp", vcA, D["vcmpT_s"].rearrange("p (g n) -> p g n", n=S), [("vcmpT_s", g, t) for g in range(2) for t in range(S // 512)], ["vcA"], sem="l1")
        self.dma("pool", w1k, D["cw1k"].rearrange("(l d) h -> d l h", d=128), [], ["w1k"], sem="l2")
        self.dma("pool", w1v, D["cw1v"].rearrange("(l d) h -> d l h", d=128), [], ["w1v"], sem="l3")
        self.dma("pool", w2k, D["cw2k"].rearrange("(c p) d -> p c d", p=128), [], ["w2k"], sem="l4")
        self.dma("pool", w2v, D["cw2v"].rearrange("(c p) d -> p c d", p=128), [], ["w2v"], sem="l5")
        self.dma("pool", posk, D["posk"], [], ["posk"], sem="l6")
        self.dma("pool", posv, D["posv"], [], ["posv"], sem="l7")
        self.dma("sp", ovb, D["ov"], [], ["ovb"], sem="l8")
        self.mset(vcaug, 0.0, ["vcaug"])
        self.mset(kcT, 0.0, ["kcT"])
        NC_ = (S - 32) // 16 + 1
        GA = 2.0 * (2.0 / np.pi) ** 0.5
        for (isv, srcA, srck, w1, w1key, w2, w2key, pos, poskey) in (
                (0, kcA, "kcA", w1k, "w1k", w2k, "w2k", posk, "posk"),
                (1, vcA, "vcA", w1v, "w1v", w2v, "w2v", posv, "posv")):
            sv = srcA.rearrange("p g (n r) -> p g n r", r=16)
            for g in range(2):
                for hc in range(2):
                    bh = self.bank()
                    bb = self.bank()
                    for l in range(32):
                        self.mm(self.ps[bh][:, 0:NC_], w1[:, l, hc * 128:(hc + 1) * 128],
                                sv[:, g, (l // 16):(l // 16) + NC_, l % 16], l == 0, l == 31,
                                [w1key, srck], [("ps", bh)])
                    for l in range(32):
                        self.mm(self.ps[bb][:, 0:1], w1[:, l, hc * 128:(hc + 1) * 128], pos[:, l:l + 1], l == 0, l == 31,
                                [w1key, poskey], [("ps", bb)])
                    sm, smk = self.rot("sm", 4)
                    self.cp(sm[:, 0:1], self.ps[bb][:, 0:1], [("ps", bb)], [smk])
                    u, uk = self.rot("stf", 3)
                    w_, wk = self.rot("stf", 3)
                    self.ts(u[:, 0:NC_], self.ps[bh][:, 0:NC_], sm[:, 0:1], None, ALU.add, None, [("ps", bh), smk], [uk])
                    self.tt(w_[:, 0:NC_], u[:, 0:NC_], u[:, 0:NC_], ALU.mult, [uk], [wk])
                    self.ts(w_[:, 0:NC_], w_[:, 0:NC_], 0.044715, 1.0, ALU.mult, ALU.add, [wk], [wk])
                    self.tt(w_[:, 0:NC_], w_[:, 0:NC_], u[:, 0:NC_], ALU.mult, [wk, uk], [wk])
                    self.act(w_[:, 0:NC_], w_[:, 0:NC_], AF.Sigmoid, [wk], [wk], scale=GA)
                    self.tt(hid[hc][:, 0:NC_], u[:, 0:NC_], w_[:, 0:NC_], ALU.mult, [uk, wk], [("hid", hc)])
                bo = self.bank()
                if not isv:
                    for hc in range(2):
                        self.mm(self.ps[bo][:, 0:NC_], w2[:, hc, :], hid[hc][:, 0:NC_], hc == 0, hc == 1,
                                [w2key, ("hid", hc)], [("ps", bo)])
                    rs, rsk = self.rot("rs", 2)
                    self.fm_rstd([(self.ps[bo][:, 0:NC_], ("ps", bo), 128)], NC_, 1.0 / 128.0, EPS, rs, rsk)
                    self.stt(kcT[:, g, 0:NC_], self.ps[bo][:, 0:NC_], self.tcol(65), rs[:, 0:NC_], ALU.mult, ALU.mult,
                             [("ps", bo), "tab", rsk], ["kcT"])
                else:
                    for hc in range(2):
                        self.mm(self.ps[bo][0:NC_, 0:128], hid[hc][:, 0:NC_], w2[:, hc, :], hc == 0, hc == 1,
                                [w2key, ("hid", hc)], [("ps", bo)])
                    self.cp(vcaug[0:NC_, g, 0:128], self.ps[bo][0:NC_, 0:128], [("ps", bo)], ["vcaug"])
                    self.mset(vcaug[0:NC_, g, 128:129], 1.0, ["vcaug"])
                    self.cp(vcaug[0:NC_, g, 129:161], ovb[0:NC_, :], ["ovb"], ["vcaug"])

        cqn = self.take([3, SO], BF16)
        wuq = self.take([3, 1536], BF16)
        cosq = self.take([SO], F32)
        sinq = self.take([SO], F32)
        self.dma("sp", cqn, D["cqnT_s"].rearrange("p (k n) -> p k n", n=SO), [], ["cqn"], sem="q0")
        self.dma("pool", wuq, D["w_uq"].rearrange("(k p) c -> p k c", p=128), [], ["wuq"], sem="q2")
        self.dma("sp", cosq[0:64, :], D["cosq"], [], ["cosq"], sem="q1")
        self.dma("sp", sinq[0:64, :], D["sinq"], [], ["sinq"], sem="q8")
        for h in range(8):
            for nb in range(SO // 512):
                cs = slice(nb * 512, (nb + 1) * 512)
                c0 = h * 192
                bn, br, bs = self.bank(), self.bank(), self.bank()
                for k in range(3):
                    self.mm(self.ps[bn], wuq[:, k, c0:c0 + 128], cqn[:, k, cs], k == 0, k == 2, ["wuq", "cqn"], [("ps", bn)])
                for k in range(3):
                    self.mm(self.ps[br][0:64, :], wuq[:, k, c0 + 128:c0 + 192], cqn[:, k, cs], k == 0, k == 2, ["wuq", "cqn"], [("ps", br)])
                for k in range(3):
                    self.mm(self.ps[bs][0:32, :], wuq[:, k, c0 + 160:c0 + 192], cqn[:, k, cs], k == 0, k == 2, ["wuq", "cqn"], [("ps", bs)])
                for k in range(3):
                    self.mm(self.ps[bs][32:64, :], wuq[:, k, c0 + 128:c0 + 160], cqn[:, k, cs], k == 0, k == 2, ["wuq", "cqn"], [("ps", bs)])
                rs, rsk = self.rot("rs", 2)
                self.fm_rstd([(self.ps[bn], ("ps", bn), 128), (self.ps[br][0:64, :], ("ps", br), 64)], 512,
                             1.0, EPS * 192.0, rs, rsk)
                self.stt(qnT[:, h, cs], self.ps[bn], self.tcol(73), rs, ALU.mult, ALU.mult, [("ps", bn), "tab", rsk], [("qnT", h)])
                t1, t1k = self.rot("stf", 3)
                t2, t2k = self.rot("stf", 3)
                self.stt(t1[0:64, :], self.ps[br][0:64, :], self.tcol(75, 1, 64), cosq[0:64, cs], ALU.mult, ALU.mult,
                         [("ps", br), "tab", "cosq"], [t1k])
                self.stt(t2[0:64, :], self.ps[bs][0:64, :], self.tcol(76, 1, 64), sinq[0:64, cs], ALU.mult, ALU.mult,
                         [("ps", bs), "tab", "sinq"], [t2k])
                self.tt(t1[0:64, :], t1[0:64, :], t2[0:64, :], ALU.add, [t1k, t2k], [t1k])
                self.tt(qrT[0:64, h, cs], t1[0:64, :], rs[0:64, :], ALU.mult, [t1k, rsk], [("qrT", h)])

        self.nsa(kcT, vcaug, mAB, p2base)
        self.mla(qnT, qrT, mAB, p2base)
        self.outnorm(p2base)

    def att_exp_pv(self, sb, nk, pvs, first, last):
        pt, ptk = self.rot("pt", 4)
        self.act(pt[0:nk, :], self.ps[sb][0:nk, :], AF.Exp, [("ps", sb)], [ptk])
        seen = set()
        for (oap, ob, h, rhs, rkeys) in pvs:
            st = first and (ob not in seen)
            seen.add(ob)
            self.mm(oap, pt[0:nk, h * 128:(h + 1) * 128], rhs, st, bool(last), [ptk] + rkeys, [("ps", ob)], sgc=True)

    def run_pipeline(self, tasks, skew):
        n = len(tasks)
        for i in range(n + skew):
            if i < n:
                tasks[i][0]()
            if i - skew >= 0:
                tasks[i - skew][1]()

    def fin_norm(self, oapf, okeys):
        sm, smk = self.rot("sm", 4)
        for i in range(4):
            self.ts(sm[:, i:i + 1], oapf(i)[:, 128:129], 1e-30, None, ALU.add, None, okeys, [smk])
        self.recip(sm[:, 0:4], sm[:, 0:4], [smk], [smk])
        return sm, smk

    def nsa(self, kcT, vcaug, mAB, base):
        D, S, SO, NJ, NB = self.D, self.S, self.SO, self.NJ, self.NB
        self.s.barrier()
        self.top = base
        kselT = self.take([2, S], BF16)
        kwinT = self.take([2, S], BF16)
        vsel = self.take([NB, 2, 130], BF16)
        vwin = self.take([NB, 2, 130], BF16)
        qT = self.take([NJ, 8, 128], BF16)
        gates = self.take([NJ, 24], F32)
        mW = self.take([6, 512], BF16)
        mC = self.take([NJ, 512], BF16)
        Ltab = self.take([NB, 128], BF16)
        Rs = self.take([NJ, 2, 512], BF16)
        Lc = self.take([128], BF16)
        keep = self.take([NJ, 32], F32)
        addt = self.take([NJ, 32], F32)
        oa_all = self.take([NJ, 1024], F32)
        self.dma("sp", kselT, D["kselT_s"].rearrange("p (g n) -> p g n", n=S), [], ["kselT"], sem="l0")
        self.dma("sp", kwinT, D["kwinT_s"].rearrange("p (g n) -> p g n", n=S), [], ["kwinT"], sem="l1")
        self.mset(vsel, 1.0, ["vsel"])
        self.mset(vwin, 1.0, ["vwin"])
        for g in range(2):
            self.dma("sp", vsel[:, :, g, 0:128], D["vsel_s"][:, g * 128:(g + 1) * 128].rearrange("(c p) d -> p c d", p=128),
                     [], ["vsel"], sem="l2")
            self.dma("sp", vwin[:, :, g, 0:128], D["vwin_s"][:, g * 128:(g + 1) * 128].rearrange("(c p) d -> p c d", p=128),
                     [], ["vwin"], sem="l3")
        self.dma("sp", qT, D["qT_s"].rearrange("p (j h n) -> p j h n", h=8, n=128), [], ["qT"], sem="l4")
        self.dma("sp", gates, D["gates_s"].rearrange("(j p) c -> p j c", p=128), [], ["gates"], sem="l5")
        self.dma("pool", mW.rearrange("p a n -> p (a n)"), D["mW"], [], ["mW"], sem="l6", cap=True)
        self.dma("pool", mC.rearrange("p a n -> p (a n)"), D["mC"], [], ["mC"], sem="l7", cap=True)
        self.dma("pool", Ltab[0:36], D["Ltab"].rearrange("p (a n) -> p a n", n=128), [], ["Ltab"], sem="l8")
        self.dma("pool", Rs[0:68].rearrange("p j g n -> p (j g n)"), D["Rs"], [], ["Rs"], sem="t1", cap=True)
        self.dma("pool", Lc[0:68], D["Lc"], [], ["Lc"], sem="t2")
        self.dma("sp", keep, D["keep"].rearrange("p (j n) -> p j n", n=32), [], ["keep"], sem="t3")
        self.dma("sp", addt, D["addt"].rearrange("p (j n) -> p j n", n=32), [], ["addt"], sem="t4")
        NC_ = (S - 32) // 16 + 1
        SB4 = (0, 1, 6, 7)

        def mk_oap(obase, ncol):
            return lambda i: self.ps[obase + i // 2][:, (i % 2) * 256:(i % 2) * 256 + ncol]

        def gate_update(j, g, br, oapf, okeys, sm, smk):
            gv = gates[:, j, :].rearrange("p (h b) -> p h b", b=3)[:, 4 * g:4 * g + 4, br]
            self.tt(sm[:, 4:8], sm[:, 0:4], gv, ALU.mult, [smk, "gates"], [smk])
            for i in range(4):
                h = 4 * g + i
                dst = oa_all[:, j, h * 128:(h + 1) * 128]
                if br == 0:
                    self.ts(dst, oapf(i)[:, 0:128], sm[:, 4 + i:5 + i], None, ALU.mult, None, okeys + [smk], [("oa", j, g)])
                else:
                    self.stt(dst, oapf(i)[:, 0:128], sm[:, 4 + i:5 + i], dst, ALU.mult, ALU.add,
                             okeys + [smk, ("oa", j, g)], [("oa", j, g)])

        def cmp_task(j, g, sb, obase, bt):
            qr = qT[:, j, 4 * g:4 * g + 4, :]
            oapf = mk_oap(obase, 161)
            okeys = [("ps", obase), ("ps", obase + 1)]
            nk = NC_

            def score():
                S_ = self.ps[sb]
                self.mm(S_[0:nk, :], kcT[:, g, 0:nk], qr, True, False, ["kcT", "qT"], [("ps", sb)])
                self.mm(S_[0:nk, :], Lc[64:68, 0:nk], Rs[64:68, j, g, :], False, False, ["Lc", "Rs"], [("ps", sb)])
                self.mm(S_[0:nk, :], self.identb[0:nk, 0:nk], mC[0:nk, j, :], False, True, ["identb", "mC"], [("ps", sb)])

            def finish():
                pvs = [(oapf(i), obase + i // 2, i, vcaug[0:nk, g, 0:161], ["vcaug"]) for i in range(4)]
                self.att_exp_pv(sb, nk, pvs, True, True)
                sm, smk = self.fin_norm(oapf, okeys)
                gate_update(j, g, 0, oapf, okeys, sm, smk)
                imp, impk = self.rot("sm", 4)
                self.ts(imp[:, 0:32], oapf(0)[:, 129:161], sm[:, 0:1], None, ALU.mult, None, okeys + [smk], [impk])
                for i in range(1, 4):
                    self.stt(imp[:, 0:32], oapf(i)[:, 129:161], sm[:, i:i + 1], imp[:, 0:32], ALU.mult, ALU.add,
                             okeys + [smk, impk], [impk])
                self.tt(imp[:, 0:32], imp[:, 0:32], keep[:, j, :], ALU.mult, [impk, "keep"], [impk])
                self.tt(imp[:, 0:32], imp[:, 0:32], addt[:, j, :], ALU.add, [impk, "addt"], [impk])
                self.s.add("dve", (lambda e, o_=imp[:, 32:40], i_=imp[:, 0:32]: e.max(o_, i_)), [impk], [impk])
                self.ts(imp[:, 0:32], imp[:, 0:32], imp[:, 39:40], -BIG, ALU.is_lt, ALU.mult, [impk], [impk])
                self.tr(self.ps[bt][0:32, 0:128], imp[:, 0:32], self.identf, [impk, "identf"], [("ps", bt)])
                for i in range(4):
                    self.cp(Rs[0:32, j, g, i * 128:(i + 1) * 128], self.ps[bt][0:32, 0:128], [("ps", bt), "Rs"], [("Rsel", j, g)])
            return score, finish

        tasks = []
        for j in range(NJ):
            for g in range(2):
                n = len(tasks)
                tasks.append(cmp_task(j, g, n % 2, 2 + 2 * (n % 2), 6 + n % 2))
        self.run_pipeline(tasks, 1)

        def sw_task(j, g, br, kb, sb, obase, first, last, store):
            qr = qT[:, j, 4 * g:4 * g + 4, :]
            oapf = mk_oap(obase, 129)
            okeys = [("ps", obase), ("ps", obase + 1)]

            def score():
                S_ = self.ps[sb]
                if br == 1:
                    self.mm(S_, kselT[:, g, kb * 128:(kb + 1) * 128], qr, True, False, ["kselT", "qT"], [("ps", sb)])
                    msk = kb >= 2 * j
                    self.mm(S_, Ltab[0:36, kb, :], Rs[0:36, j, g, :], False, not msk, ["Ltab", "Rs", ("Rsel", j, g)], [("ps", sb)])
                    if msk:
                        self.mm(S_, self.identb, mAB[:, kb - 2 * j, :], False, True, ["identb", "mAB"], [("ps", sb)])
                else:
                    o = kb - (2 * j - 4)
                    self.mm(S_, kwinT[:, g, kb * 128:(kb + 1) * 128], qr, True, False, ["kwinT", "qT"], [("ps", sb)])
                    self.mm(S_, Ltab[32:36, kb, :], Rs[32:36, j, g, :], False, False, ["Ltab", "Rs"], [("ps", sb)])
                    self.mm(S_, self.identb, mW[:, o, :], False, True, ["identb", "mW"], [("ps", sb)])

            def finish():
                vt, vk = (vsel, "vsel") if br == 1 else (vwin, "vwin")
                pvs = [(oapf(i), obase + i // 2, i, vt[:, kb, g, 0:129], [vk]) for i in range(4)]
                self.att_exp_pv(sb, 128, pvs, first, last)
                if last:
                    sm, smk = self.fin_norm(oapf, okeys)
                    gate_update(j, g, br, oapf, okeys, sm, smk)
                if store:
                    self.dma("sp", D["oa_s"][j * 128:(j + 1) * 128, :], oa_all[:, j, :], [("oa", j, 0), ("oa", j, 1)],
                             [("oa_s", j)], sem="so%d" % (j % 2))
            return score, finish

        tasks = []
        grp = 0
        for j in range(NJ):
            for g in range(2):
                for br in (1, 2):
                    if br == 1:
                        kbs = list(range(2 * j + 2))
                    else:
                        kbs = [2 * j - 4 + o for o in range(6) if 2 * j - 4 + o >= 0]
                    obase = 2 + 2 * (grp % 2)
                    grp += 1
                    for ki, kb in enumerate(kbs):
                        last = ki == len(kbs) - 1
                        tasks.append(sw_task(j, g, br, kb, SB4[len(tasks) % 4], obase, ki == 0, last,
                                             last and g == 1 and br == 2))
        self.run_pipeline(tasks, 2)

    def mla(self, qnT, qrT, mAB, base):
        D, S, SO, NJ, NB = self.D, self.S, self.SO, self.NJ, self.NB
        self.s.barrier()
        self.top = base
        ckvn = self.take([2, S], BF16)
        wukv = self.take([2, 2048], BF16)
        kraw = self.take([S], BF16)
        krot = self.take([S], F32)
        knT = self.take([4, S], BF16)
        krT = self.take([4, S], BF16)
        vaug = self.take([NB, 4, 130], BF16)
        self.dma("sp", ckvn, D["ckvnT_s"].rearrange("p (k n) -> p k n", n=S), [], ["ckvn"], sem="l0")
        self.dma("pool", wukv, D["w_ukv"].rearrange("(k p) c -> p k c", p=128), [], ["wukv"], sem="l2")
        self.dma("sp", kraw[0:64, :], D["kraw_s"], [], ["kraw"], sem="l1")
        self.dma("sp", krot[0:64, :], D["krot_s"], [], ["krot"], sem="l3")
        self.mset(vaug, 1.0, ["vaug"])
        self.act(kraw[0:64, :], kraw[0:64, :], AF.Square, ["kraw"], ["kraw"])
        SB4 = (0, 1, 6, 7)
        wv = wukv.rearrange("p k (h c) -> p k h c", c=256)
        grp = 0
        for hg in range(2):
            self.bankc = 2
            for i in range(4):
                h = 4 * hg + i
                for nb in range(S // 512):
                    cs = slice(nb * 512, (nb + 1) * 512)
                    bn = self.bank()
                    for k in range(2):
                        self.mm(self.ps[bn], wukv[:, k, h * 256:h * 256 + 128], ckvn[:, k, cs], k == 0, k == 1,
                                ["wukv", "ckvn"], [("ps", bn)])
                    sq, sqk = self.rot("sq", 3)
                    self.act(sq, self.ps[bn], AF.Square, [("ps", bn)], [sqk])
                    bss = self.bank()
                    self.mm(self.ps[bss], self.onesb, sq, True, False, [sqk, "onesb"], [("ps", bss)])
                    self.mm(self.ps[bss], self.onesb[0:64, :], kraw[0:64, cs], False, True, ["kraw", "onesb"], [("ps", bss)])
                    rs, rsk = self.rot("rs", 2)
                    self.act(rs, self.ps[bss], AF.Sqrt, [("ps", bss)], [rsk], bias=EPS, scale=1.0 / 192.0)
                    self.recip(rs, rs, [rsk], [rsk])
                    self.stt(knT[:, i, cs], self.ps[bn], self.tcol(74), rs, ALU.mult, ALU.mult, [("ps", bn), "tab", rsk], [("knT", i)])
                    self.tt(krT[0:64, i, cs], krot[0:64, cs], rs[0:64, :], ALU.mult, ["krot", rsk], [("krT", i)])
            for c in range(NB):
                bv = self.bank()
                for k in range(2):
                    self.mm(self.ps[bv], ckvn[:, k, c * 128:(c + 1) * 128], wv[:, k, 4 * hg:4 * hg + 4, 128:256], k == 0, k == 1,
                            ["wukv", "ckvn"], [("ps", bv)])
                self.cp(vaug[:, c, :, 0:128], self.ps[bv].rearrange("p (h d) -> p h d", d=128), [("ps", bv)], ["vaug"])

            def mla_task(j, kb, sb, obase, first, last):
                oapf = lambda i: self.ps[obase + i // 2][:, (i % 2) * 256:(i % 2) * 256 + 129]
                okeys = [("ps", obase), ("ps", obase + 1)]

                def score():
                    S_ = self.ps[sb]
                    msk = kb >= 2 * j
                    for i in range(4):
                        h = 4 * hg + i
                        self.mm(S_[:, i * 128:(i + 1) * 128], knT[:, i, kb * 128:(kb + 1) * 128], qnT[:, h, j * 128:(j + 1) * 128],
                                i == 0, False, [("knT", i), ("qnT", h)], [("ps", sb)], sgc=True)
                        self.mm(S_[:, i * 128:(i + 1) * 128], krT[0:64, i, kb * 128:(kb + 1) * 128], qrT[0:64, h, j * 128:(j + 1) * 128],
                                False, (i == 3) and not msk, [("krT", i), ("qrT", h)], [("ps", sb)], sgc=True)
                    if msk:
                        self.mm(S_, self.identb, mAB[:, kb - 2 * j, :], False, True, ["identb", "mAB"], [("ps", sb)], sgc=True)

                def finish():
                    pvs = [(oapf(i), obase + i // 2, i, vaug[:, kb, i, 0:129], ["vaug"]) for i in range(4)]
                    self.att_exp_pv(sb, 128, pvs, first, last)
                    if last:
                        sm, smk = self.fin_norm(oapf, okeys)
                        ob, obk = self.rot("of", 2)
                        for i in range(4):
                            self.ts(ob[:, i * 128:(i + 1) * 128], oapf(i)[:, 0:128], sm[:, i:i + 1], None, ALU.mult, None,
                                    okeys + [smk], [obk])
                        self.dma("sp", D["ob_s"][j * 128:(j + 1) * 128, hg * 512:(hg + 1) * 512], ob[:, 0:512], [obk],
                                 [("ob_s", j, hg)], sem=self.ksem(obk))
                return score, finish

            tasks = []
            for j in range(NJ):
                obase = 2 + 2 * (grp % 2)
                grp += 1
                nkb = 2 * j + 2
                for kb in range(nkb):
                    tasks.append(mla_task(j, kb, SB4[len(tasks) % 4], obase, kb == 0, kb == nkb - 1))
            self.run_pipeline(tasks, 2)

    def outnorm(self, base):
        D, SO, NJ = self.D, self.SO, self.NJ
        self.s.barrier()
        self.top = base
        xns = [self.take([1024], F32) for _ in range(2)]
        oin = [self.take([1024], F32) for _ in range(4)]
        mx = [self.take([16, 128], BF16) for _ in range(2)]
        mv = D["mixT_s"].rearrange("p (k n) -> p k n", n=SO)
        for j in range(NJ):
            m = mx[j % 2]
            mk = ("mx", j % 2)
            for half, (name, gcol) in enumerate((("oa_s", 48), ("ob_s", 56))):
                oi = 2 * (j % 2) + half
                o = oin[oi]
                ok = ("oin", oi)
                self.dma("sp", o, D[name][j * 128:(j + 1) * 128, :], [], [ok], sem=f"lo{oi}")
                self.norm_T(o, ok, 1024, gcol, (lambda k, m=m, half=half: m[:, half * 8 + k, :]),
                            (lambda k, j=j, half=half: ("mx", j % 2, half * 8 + k)), xns[half], 6 if half == 0 else 0,
                            xnkey=("xn", half))
            self.dma("sp", mv[:, :, j * 128:(j + 1) * 128], m, [("mx", j % 2, kk_) for kk_ in range(16)], [("mixT_s", j)],
                     sem=self.ksem(mk))

    def phase3(self):
        D, SO = self.D, self.SO
        self.s.barrier()
        self.top = self.p13_top
        self.bankc = 0
        self.ssb, self.rbuf, self.junk = self.p1save
        mixT = self.take([KC, 512], BF16)
        NT3 = SO // 512
        pieces, A2, B2 = self.ffn_slabs(D["f2g"], D["f2u"], D["f2d"])
        wov = D["w_out"].rearrange("(k p) c -> p k c", p=128)
        Ao = [((lambda buf: buf), wov[:, :, dc * 512:(dc + 1) * 512]) for dc in range(4)]
        A, Bl = [], []
        for t in range(NT3):
            A += Ao + A2
            Bl += B2
        stA = Stream(self, "WA", self.WA, A, live=2)
        stB = Stream(self, "WB", self.WB, Bl)
        xres = self.xres
        mv = D["mixT_s"].rearrange("p (k n) -> p k n", n=SO)
        for t in range(NT3):
            a0 = t * (len(Ao) + len(A2))
            b0 = t * len(B2)
            for c in range(4):
                r0 = (4 * t + c) * 128
                self.dma("sp", xres[:, c, :], D["x1own"][r0:r0 + 128, :], [], [("xres", c)], sem=f"xr{c}")
            self.dma("sp", mixT, mv[:, :, t * 512:(t + 1) * 512], [], ["mixT"], sem="lm")
            for dc in range(4):
                buf, key = stA.use(a0 + dc)
                for tc in range(4):
                    bnk = 4 + tc
                    for k in range(KC):
                        self.mm(self.ps[bnk], mixT[:, k, tc * 128:(tc + 1) * 128], buf[:, k, :], k == 0, k == KC - 1,
                                ["mixT", key], [("ps", bnk)])
                    xs = xres[:, tc, dc * 512:(dc + 1) * 512]
                    self.tt(xs, self.ps[bnk], xs, ALU.add, [("ps", bnk), ("xres", tc)], [("xres", tc)])
            self.ffn(stA, a0 + 4, stB, b0, pieces, 32)
            for c in range(4):
                r0 = (4 * t + c) * 128
                self.dma("sp", D["out"][r0:r0 + 128, :], xres[:, c, :], [("xres", c)], [("out", t, c)], sem=f"o{c}")
        self.s.barrier()


def build_nc(S, DFF):
    nc = bass.Bass("TRN2", target_bir_lowering=False)
    kb = KB(nc, S, DFF)
    kb.declare()
    kb.setup_mem()
    kb.consts()
    kb.phase1()
    kb.phase2()
    kb.phase3()
    kb.s.emit(kb.es)
    kb.es.close()
    return nc


def host_tables(S, p):
    NB = S // 128
    NJ = NB // 2
    f = np.float32
    slopes = (2.0 ** (-8.0 * np.arange(1, 9, dtype=np.float64) / 8)).astype(f)
    ql = np.arange(128)
    kl = np.arange(128)
    diag = np.where(kl[:, None] <= ql[None, :], 0.0, -BIG).astype(f)
    far = np.where(kl[:, None] > ql[None, :], 0.0, -BIG).astype(f)
    full = np.zeros((128, 128), f)
    empty = np.full((128, 128), -BIG, f)
    rep4 = lambda m: np.tile(m, (1, 4))
    mAB = np.stack([rep4(diag if p == 0 else full), rep4(empty if p == 0 else diag)], 1).reshape(128, 2 * 512)
    mw = []
    for o in range(6):
        d = p + 4 - o
        m = far if d == 4 else (full if 1 <= d <= 3 else (diag if d == 0 else empty))
        mw.append(rep4(m))
    mW = np.stack(mw, 1).reshape(128, 6 * 512)
    n = np.arange(128)
    mc = []
    for j in range(NJ):
        qb = 2 * j + p
        vis = (16 * n[:, None] + 31 <= 128 * qb + ql[None, :]) & (n[:, None] < 127)
        mc.append(rep4(np.where(vis, 0.0, -BIG).astype(f)))
    mC = np.stack(mc, 1).reshape(128, NJ * 512)
    Ltab = np.zeros((36, NB, 128), f)
    for kb in range(NB):
        for s in range(32):
            Ltab[s, kb, :] = (s == 2 * kb + kl // 64)
        Ltab[32, kb, :] = 1.0
        Ltab[33, kb, :] = 1.0
        Ltab[34, kb, :] = kl
        Ltab[35, kb, :] = 128.0 * kb
    Rs = np.zeros((68, NJ, 2, 512), f)
    col = np.arange(512)
    for j in range(NJ):
        qb = 2 * j + p
        for g in range(2):
            sl = slopes[4 * g + col // 128]
            qq = (col % 128).astype(f)
            Rs[32, j, g] = -sl * qq
            Rs[33, j, g] = -sl * f(128.0 * qb)
            Rs[34, j, g] = sl
            Rs[35, j, g] = sl
            Rs[64, j, g] = -sl * qq
            Rs[65, j, g] = -sl * f(128.0 * qb)
            Rs[66, j, g] = sl
            Rs[67, j, g] = f(15.5) * sl
    Lc = np.zeros((68, 128), f)
    Lc[64] = 1.0
    Lc[65] = 1.0
    Lc[66] = 16.0 * n
    Lc[67] = 1.0
    keep = np.zeros((128, NJ, 32), f)
    addt = np.zeros((128, NJ, 32), f)
    s = np.arange(32)
    for j in range(NJ):
        t = 128 * (2 * j + p) + ql
        cur = t // 64
        forced = (s[None, :] == 0) | (s[None, :] == cur[:, None])
        future = 64 * s[None, :] > t[:, None]
        keep[:, j, :] = 1.0 - forced - future
        addt[:, j, :] = 1e4 * forced + NEG * future
    ov = np.zeros((128, 32), f)
    ncmp = (S - 32) // 16 + 1
    for nn in range(ncmp):
        ov[nn] = (16 * nn < 64 * s + 64) & (16 * nn + 31 >= 64 * s)
    ov[:, S // 64:] = 0.0
    inv = (np.float32(10000.0) ** (-np.arange(32, dtype=f) / f(32))).astype(f)
    pos = np.arange(S).astype(f)
    ang = (pos[None, :] * inv[:, None]).astype(f)
    cs, sn = np.cos(ang).astype(f), np.sin(ang).astype(f)
    cosk = np.concatenate([cs, cs], 0)
    sink = np.concatenate([-sn, sn], 0)
    own = np.concatenate([128 * (2 * j + p) + ql for j in range(NJ)])
    cosq = cosk[:, own]
    sinq = sink[:, own]
    c = np.ascontiguousarray
    return dict(mAB=c(mAB), mW=c(mW), mC=c(mC), Ltab=c(Ltab.reshape(36, -1)), Rs=c(Rs.reshape(68, -1)), Lc=c(Lc),
                keep=c(keep.reshape(128, -1)), addt=c(addt.reshape(128, -1)), ov=c(ov),
                cosk=c(cosk), sink=c(sink), cosq=c(cosq), sinq=c(sinq))


def run(inputs, S, DFF, B):
    f = np.float32
    c = np.ascontiguousarray
    g = lambda k: np.asarray(inputs[k], dtype=f)[0]
    tab = np.zeros((128, NTAB), f)
    tab[:, 0:16] = g("ffn1_norm").reshape(16, 128).T
    tab[:, 16:32] = g("mix_norm").reshape(16, 128).T
    tab[:, 32:48] = g("ffn2_norm").reshape(16, 128).T
    tab[:, 48:56] = g("out_norm_nsa").reshape(8, 128).T
    tab[:, 56:64] = g("out_norm_mla").reshape(8, 128).T
    tab[:, 64] = g("nsa_q_norm")
    tab[:, 65:68] = g("nsa_k_norm").T
    tab[:, 68:71] = g("mla_q_a_norm").reshape(3, 128).T
    tab[:, 71:73] = g("mla_kv_a_norm").reshape(2, 128).T
    qn, kn = g("mla_q_norm"), g("mla_k_norm")
    tab[:, 73] = qn[0:128]
    tab[:, 74] = kn[0:128]
    tab[0:64, 75] = qn[128:192]
    tab[0:64, 76] = np.concatenate([qn[160:192], qn[128:160]])
    tab[0:64, 77] = kn[128:192]
    tab[0:64, 78] = np.concatenate([kn[160:192], kn[128:160]])
    shared = dict(
        f1g=c(g("ffn1_w_gate")), f1u=c(g("ffn1_w_up")), f1d=c(g("ffn1_w_down")),
        f2g=c(g("ffn2_w_gate")), f2u=c(g("ffn2_w_up")), f2d=c(g("ffn2_w_down")),
        w_in=c(g("w_in")), w_out=c(g("w_out")),
        cw1k=c(g("nsa_cmp_w1_k")), cw1v=c(g("nsa_cmp_w1_v")), cw2k=c(g("nsa_cmp_w2_k")), cw2v=c(g("nsa_cmp_w2_v")),
        w_uq=c(g("mla_w_uq")), w_ukv=c(g("mla_w_ukv")),
        posk=c(g("nsa_cmp_pos_k").T), posv=c(g("nsa_cmp_pos_v").T),
        ident=np.eye(128, dtype=f),
    )
    x = np.asarray(inputs["x"], dtype=f)
    tables = [host_tables(S, p) for p in range(2)]
    in_maps = []
    for core in range(2 * B):
        b, p = core // 2, core % 2
        m = dict(shared)
        m.update(tables[p])
        tb = tab.copy()
        tb[:, 79] = 1.0 - p
        tb[:, 80] = float(p)
        m["tab"] = tb
        m["x"] = c(x[b])
        in_maps.append(m)
    nc = build_nc(S, DFF)
    res = run_bass_kernel_spmd(nc, in_maps, core_ids=list(range(2 * B)))
    if DEBUG:
        LAST["res"] = res.results
    out = np.zeros((B, S, DM), f)
    NJ = S // 256
    for core in range(2 * B):
        b, p = core // 2, core % 2
        o = np.asarray(res.results[core]["out"]).reshape(NJ, 128, DM)
        for j in range(NJ):
            qb = 2 * j + p
            out[b, qb * 128:(qb + 1) * 128] = o[j]
    return out


def kernel(**inputs):
    return run(inputs, 2048, 5632, 4)
```

```python
import numpy as np
from contextlib import ExitStack
import concourse.bass as bass
import concourse.mybir as mybir
from concourse.bass_utils import run_bass_kernel_spmd

F32 = mybir.dt.float32
BF16 = mybir.dt.bfloat16
AF = mybir.ActivationFunctionType
ALU = mybir.AluOpType
AX = mybir.AxisListType

DM = 2048
KC = 16
IN_DIM = 3288
EPS = 1e-6
BIG = 30000.0
NEG = -1e30
GCH = 11
NTAB = 96
DEBUG = False
LAST = {}


class Op:
    __slots__ = ("eng", "fn", "deps", "signal", "val", "dma", "idx")


class Sched:
    CE = ("pe", "act", "dve", "pool")
    ALL = ("pe", "act", "dve", "pool", "sp")

    def __init__(self, nc):
        self.nc = nc
        self.ops = {e: [] for e in self.ALL}
        self.lastw = {}
        self.readers = {}
        self.dcount = {}
        self.last = {}
        self.n = 0

    def add(self, eng, fn, reads=(), writes=(), dma=None):
        op = Op()
        op.eng, op.fn, op.signal, op.dma, op.val = eng, fn, False, dma, None
        op.idx = self.n
        self.n += 1
        deps = {}

        def dep(d):
            if d.dma is None and d.eng == "pe" and eng == "pe":
                return
            k = ("d", d.dma) if d.dma is not None else ("e", d.eng)
            c = deps.get(k)
            if c is None or c.idx < d.idx:
                deps[k] = d

        for k in reads:
            w = self.lastw.get(k)
            if w is not None:
                dep(w)
        for k in writes:
            w = self.lastw.get(k)
            if w is not None:
                dep(w)
            for r in self.readers.get(k, ()):
                dep(r)
        op.deps = list(deps.values())
        for d in op.deps:
            d.signal = True
        for k in reads:
            self.readers.setdefault(k, []).append(op)
        for k in writes:
            self.lastw[k] = op
            self.readers[k] = []
        if dma is not None:
            self.dcount[dma] = self.dcount.get(dma, 0) + 16
            op.val = (dma, self.dcount[dma])
            self.last[("d", dma)] = op
        else:
            self.last[("e", eng)] = op
        self.ops[eng].append(op)
        return op

    def barrier(self):
        lasts = list(self.last.values())
        for e in self.ALL:
            op = Op()
            op.eng, op.fn, op.signal, op.dma, op.val = e, None, False, None, None
            op.idx = self.n
            self.n += 1
            op.deps = [d for d in lasts if not (d.dma is None and d.eng == e)]
            for d in op.deps:
                d.signal = True
            self.ops[e].append(op)
        self.lastw = {}
        self.readers = {}

    def emit(self, es):
        nc = self.nc
        for e in self.CE:
            c = 0
            for op in self.ops[e]:
                if op.fn is not None and op.dma is None and op.signal:
                    c += 1
                    op.val = ("prog_" + e, c)
        names = ["prog_" + e for e in self.CE] + sorted(self.dcount.keys())
        sems = {n: es.enter_context(nc.semaphore(n)) for n in names}
        block = es.enter_context(nc.Block())

        def mk(e):
            def body(eng):
                known = {}
                for op in self.ops[e]:
                    need = {}
                    for d in op.deps:
                        n, v = d.val
                        if need.get(n, 0) < v:
                            need[n] = v
                    for n, v in need.items():
                        if known.get(n, 0) < v:
                            eng.wait_ge(sems[n], v)
                            known[n] = v
                    if op.fn is None:
                        continue
                    ins = op.fn(eng)
                    if op.dma is not None:
                        ins.then_inc(sems[op.dma], 16)
                    elif op.signal:
                        ins.then_inc(sems["prog_" + e], 1)
            return body

        block.tensor(mk("pe"))
        block.scalar(mk("act"))
        block.vector(mk("dve"))
        block.gpsimd(mk("pool"))
        block.sync(mk("sp"))


class Stream:
    def __init__(self, kb, name, bufs, slabs, live=1):
        self.kb, self.name, self.bufs, self.slabs = kb, name, bufs, slabs
        self.nxt = 0
        self.live = live

    def _issue(self, i):
        dstf, src = self.slabs[i]
        b = i % len(self.bufs)
        self.kb.dma("pool", dstf(self.bufs[b]), src, [], [(self.name, b)], sem=f"{self.name}{b}")

    def use(self, i):
        nb = len(self.bufs)
        while self.nxt < len(self.slabs) and self.nxt <= i + nb - self.live:
            self._issue(self.nxt)
            self.nxt += 1
        b = i % nb
        return self.slabs[i][0](self.bufs[b]), (self.name, b)


class KB:
    def __init__(self, nc, S, DFF):
        self.nc = nc
        self.S, self.DFF = S, DFF
        self.NB = S // 128
        self.NJ = self.NB // 2
        self.SO = S // 2
        self.FC = DFF // 128
        self.NG = self.FC // GCH
        assert self.FC % GCH == 0 and S % 512 == 0 and self.SO % 512 == 0
        self.s = Sched(nc)
        self.es = ExitStack()
        self.rotc = {}
        self.bankc = 0
        self.ssi = 0

    def setup_mem(self):
        nc = self.nc
        self.AW = 52600
        self.arena = self.es.enter_context(nc.sbuf_tensor("arena", [128, self.AW], F32))
        self.pst = self.es.enter_context(nc.psum_tensor("ps", [128, 8 * 512], F32))
        self.ps = [self.pst[:, b * 512:(b + 1) * 512] for b in range(8)]
        self.top = 0

    def take(self, shape, dt):
        n = int(np.prod(shape))
        nbytes = n * (4 if dt == F32 else 2)
        n32 = (nbytes + 3) // 4
        n32 = (n32 + 7) // 8 * 8
        off = self.top
        self.top += n32
        assert self.top <= self.AW, f"arena overflow {self.top}"
        ap = self.arena[:, off:off + n32]
        if dt != F32:
            ap = ap.bitcast(dt)
        ap = ap[:, 0:n]
        if len(shape) == 2:
            ap = ap.rearrange("p (a b) -> p a b", b=shape[1])
        elif len(shape) == 3:
            ap = ap.rearrange("p (a b c) -> p a b c", b=shape[1], c=shape[2])
        return ap

    def bank(self):
        b = self.bankc % 8
        self.bankc += 1
        return b

    def mm(self, out, lhsT, rhs, start, stop, r, w, sgc=False):
        if sgc:
            self.s.add("pe", lambda e: e.matmul(out, lhsT, rhs, start=start, stop=stop, skip_group_check=True), r, w)
        else:
            self.s.add("pe", lambda e: e.matmul(out, lhsT, rhs, start=start, stop=stop), r, w)

    def tr(self, out, in_, ident, r, w):
        self.s.add("pe", lambda e: e.transpose(out, in_, ident), r, w)

    def act(self, out, in_, func, r, w, bias=None, scale=None):
        kw = {}
        if bias is not None:
            kw["bias"] = bias
        if scale is not None:
            kw["scale"] = scale
        self.s.add("act", lambda e: e.activation(out, in_, func, **kw), r, w)

    def ts(self, out, in0, s1, s2, op0, op1, r, w, eng="dve"):
        if op1 is None:
            self.s.add(eng, lambda e: e.tensor_scalar(out, in0, s1, None, op0), r, w)
        else:
            self.s.add(eng, lambda e: e.tensor_scalar(out, in0, s1, s2, op0, op1), r, w)

    def tt(self, out, in0, in1, op, r, w, eng="dve"):
        self.s.add(eng, lambda e: e.tensor_tensor(out, in0, in1, op), r, w)

    def stt(self, out, in0, sc, in1, op0, op1, r, w):
        self.s.add("dve", lambda e: e.scalar_tensor_tensor(out, in0, sc, in1, op0, op1), r, w)

    def cp(self, out, in_, r, w, eng="dve"):
        self.s.add(eng, lambda e: e.tensor_copy(out, in_), r, w)

    def red(self, out, in_, r, w):
        self.s.add("dve", lambda e: e.tensor_reduce(out, in_, AX.X, ALU.add), r, w)

    def recip(self, out, in_, r, w):
        self.s.add("dve", lambda e: e.reciprocal(out, in_), r, w)

    def mset(self, ap, val, w, eng="dve"):
        self.s.add(eng, lambda e: e.memset(ap, val), [], w)

    def dma(self, eng, out, in_, r, w, sem, cap=False):
        sem = ("g_" if eng == "pool" else "h_") + sem
        if cap:
            self.s.add(eng, lambda e: e.dma_start(out=out, in_=in_, max_dma_last_dim=4096), r, w, dma=sem)
        else:
            self.s.add(eng, lambda e: e.dma_start(out=out, in_=in_), r, w, dma=sem)

    def declare(self):
        nc, S, DFF, SO, NJ, NB = self.nc, self.S, self.DFF, self.SO, self.NJ, self.NB
        D = {}

        def inp(name, shape):
            D[name] = nc.dram_tensor(name, list(shape), F32, kind="ExternalInput").ap()

        def scr(name, shape, dt):
            D[name] = nc.dram_tensor(name, list(shape), dt, kind=("ExternalOutput" if DEBUG else "Internal")).ap()

        inp("x", [S, DM])
        for f in ("f1", "f2"):
            inp(f + "g", [DM, DFF]); inp(f + "u", [DM, DFF]); inp(f + "d", [DFF, DM])
        inp("w_in", [DM, IN_DIM]); inp("w_out", [DM, DM])
        inp("cw1k", [4096, 256]); inp("cw1v", [4096, 256]); inp("cw2k", [256, 128]); inp("cw2v", [256, 128])
        inp("w_uq", [384, 1536]); inp("w_ukv", [256, 2048])
        inp("tab", [128, NTAB]); inp("posk", [128, 32]); inp("posv", [128, 32]); inp("ident", [128, 128])
        inp("mAB", [128, 2 * 512]); inp("mW", [128, 6 * 512]); inp("mC", [128, NJ * 512])
        inp("Ltab", [36, NB * 128]); inp("Rs", [68, NJ * 2 * 512]); inp("Lc", [68, 128])
        inp("keep", [128, NJ * 32]); inp("addt", [128, NJ * 32]); inp("ov", [128, 32])
        inp("cosk", [64, S]); inp("sink", [64, S]); inp("cosq", [64, SO]); inp("sinq", [64, SO])
        D["out"] = nc.dram_tensor("out", [SO, DM], F32, kind="ExternalOutput").ap()
        scr("x1own", [SO, DM], F32)
        scr("qT_s", [128, NJ * 8 * 128], BF16)
        scr("kcmpT_s", [128, 2 * S], BF16); scr("vcmpT_s", [128, 2 * S], BF16)
        scr("kselT_s", [128, 2 * S], BF16); scr("kwinT_s", [128, 2 * S], BF16)
        scr("vsel_s", [S, 256], BF16); scr("vwin_s", [S, 256], BF16)
        scr("gates_s", [SO, 24], F32)
        scr("cqnT_s", [128, 3 * SO], BF16); scr("ckvnT_s", [128, 2 * S], BF16)
        scr("kraw_s", [64, S], BF16); scr("krot_s", [64, S], F32)
        scr("oa_s", [SO, 1024], F32); scr("ob_s", [SO, 1024], F32)
        scr("mixT_s", [128, 16 * SO], BF16)
        self.D = D

    def consts(self):
        D = self.D
        self.tab = self.take([NTAB], F32)
        self.identf = self.take([128], F32)
        self.identb = self.take([128], BF16)
        self.onesb = self.take([128], BF16)
        self.dma("sp", self.tab, D["tab"], [], ["tab"], sem="c0")
        self.dma("sp", self.identf, D["ident"], [], ["identf"], sem="c1")
        self.cp(self.identb, self.identf, ["identf"], ["identb"])
        self.mset(self.onesb, 1.0, ["onesb"])
        self.cbase = self.top

    def tcol(self, c, n=1, p=128):
        return self.tab[0:p, c:c + n]

    def norm_T(self, src, srckey, F, gcol, dst, dstkey, xn, tb0, xnkey="xn", dst4=None):
        nk = F // 128
        si = self.ssi % 4
        self.ssi += 1
        ss = self.ssb[:, 4 * si:4 * si + 4]
        k0_, k1_, k2_ = ("ss", si, 0), ("ss", si, 1), ("ss", si, 2)
        jk = self.junk
        self.act(jk[:, 0:F], src, AF.Square, [srckey], ["jk"])
        self.red(ss[:, 0:1], jk[:, 0:F], ["jk"], [k0_])
        self.act(ss[:, 1:2], ss[:, 0:1], AF.Sqrt, [k0_], [k1_], bias=EPS, scale=1.0 / F)
        self.recip(ss[:, 2:3], ss[:, 1:2], [k1_], [k2_])
        self.ts(xn[:, 0:F], src, ss[:, 2:3], None, ALU.mult, None, [srckey, k2_], [xnkey])
        for k0 in range(0, nk, 4):
            b = tb0 + (k0 // 4) % 2
            for kk in range(4):
                k = k0 + kk
                self.tr(self.ps[b][:, kk * 128:(kk + 1) * 128], xn[:, k * 128:(k + 1) * 128], self.identf,
                        [xnkey, "identf"], [("ps", b)])
            if dst4 is not None:
                gb = self.tab[:, gcol + k0:gcol + k0 + 4].to_broadcast([128, 4, 128]) if False else \
                    self.tab[:, gcol + k0:gcol + k0 + 4].unsqueeze(2).to_broadcast([128, 4, 128])
                self.tt(dst4(k0), self.ps[b].rearrange("p (a n) -> p a n", n=128), gb, ALU.mult,
                        [("ps", b), "tab"], [dstkey(k0 + kk) for kk in range(4)])
            else:
                for kk in range(4):
                    k = k0 + kk
                    self.ts(dst(k), self.ps[b][:, kk * 128:(kk + 1) * 128], self.tcol(gcol + k), None,
                            ALU.mult, None, [("ps", b), "tab"], [dstkey(k)])

    def ffn_slabs(self, wg, wu, wd):
        A, B = [], []
        wgv = wg.rearrange("(k p) c -> p k c", p=128)
        wuv = wu.rearrange("(k p) c -> p k c", p=128)
        wdv = wd.rearrange("(c p) n -> p c n", p=128)
        pieces = []
        for g in range(self.NG):
            c = 0
            while c < GCH:
                n = min(4, GCH - c)
                pieces.append((g, c, n))
                c += n
        for (g, c, n) in pieces:
            c0 = (g * GCH + c) * 128
            for wv in (wgv, wuv):
                A.append(((lambda buf, n=n: buf[:, :, 0:n * 128]), wv[:, :, c0:c0 + n * 128]))
        for g in range(self.NG):
            for dc in range(4):
                B.append(((lambda buf: buf), wdv[:, g * GCH:(g + 1) * GCH, dc * 512:(dc + 1) * 512]))
        return pieces, A, B

    def ffn(self, stA, a0, stB, b0, pieces, gcol):
        xres, hT, actT = self.xres, self.hT, self.actT
        for c in range(4):
            self.norm_T(xres[:, c, :], ("xres", c), DM, gcol,
                        (lambda k, c=c: hT[:, k, c * 128:(c + 1) * 128]), (lambda k: ("hT", k)), self.xn, 4,
                        dst4=(lambda k0, c=c: hT[:, k0:k0 + 4, c * 128:(c + 1) * 128]))
        ai = a0
        bi = b0
        cc = 0
        for g in range(self.NG):
            for (pg, c, n) in pieces:
                if pg != g:
                    continue
                bufG, keyG = stA.use(ai)
                bufU, keyU = stA.use(ai + 1)
                ai += 2
                for i in range(n):
                    ci = c + i
                    bg, bu = 2 * (cc % 2), 2 * (cc % 2) + 1
                    for (bnk, buf, key) in ((bg, bufG, keyG), (bu, bufU, keyU)):
                        for k in range(KC):
                            self.mm(self.ps[bnk], buf[:, k, i * 128:(i + 1) * 128], hT[:, k, :],
                                    k == 0, k == KC - 1, [key, ("hT", k)], [("ps", bnk)])
                    sg = self.sg[cc % 2]
                    self.act(sg, self.ps[bg], AF.Silu, [("ps", bg)], [("sg", cc % 2)])
                    self.tt(actT[:, ci, :], sg, self.ps[bu], ALU.mult, [("sg", cc % 2), ("ps", bu)], [("act", ci)])
                    cc += 1
            for dc in range(4):
                bufD, keyD = stB.use(bi)
                bi += 1
                for tc in range(4):
                    bnk = 4 + tc
                    for ci in range(GCH):
                        self.mm(self.ps[bnk], actT[:, ci, tc * 128:(tc + 1) * 128], bufD[:, ci, :],
                                ci == 0, ci == GCH - 1, [("act", ci), keyD], [("ps", bnk)])
                    xs = xres[:, tc, dc * 512:(dc + 1) * 512]
                    self.stt(xs, self.ps[bnk], 0.5, xs, ALU.mult, ALU.add, [("ps", bnk), ("xres", tc)], [("xres", tc)])
        return ai, bi

    def fm_rstd(self, parts, N, c1, c2, rs, rskey):
        bss = self.bank()
        sqs = []
        for i, (ap, key, P) in enumerate(parts):
            sq, sqk = self.rot("sq", 3)
            self.act(sq[0:P, 0:N], ap, AF.Square, [key], [sqk])
            sqs.append((sq, sqk, P))
        for i, (sq, sqk, P) in enumerate(sqs):
            self.mm(self.ps[bss][:, 0:N], self.onesb[0:P, :], sq[0:P, 0:N], i == 0, i == len(sqs) - 1,
                    [sqk, "onesb"], [("ps", bss)])
        self.act(rs[:, 0:N], self.ps[bss][:, 0:N], AF.Sqrt, [("ps", bss)], [rskey], bias=c2, scale=c1)
        self.recip(rs[:, 0:N], rs[:, 0:N], [rskey], [rskey])

    def ksem(self, key):
        return "s_%s%d" % (key[0], key[1])

    def rot(self, name, n):
        i = self.rotc.get(name, 0)
        self.rotc[name] = i + 1
        return self.rbuf[name][i % n], (name, i % n)

    def phase1(self):
        D, S = self.D, self.S
        NT = S // 512
        self.top = self.cbase
        self.xres = self.take([4, DM], F32)
        self.hT = self.take([KC, 512], BF16)
        self.actT = self.take([GCH, 512], BF16)
        self.WA = [self.take([KC, 512], BF16) for _ in range(4)]
        self.WB = [self.take([GCH, 512], BF16) for _ in range(3)]
        self.xn = self.take([DM], F32)
        self.sg = [self.take([512], F32) for _ in range(2)]
        self.ssb = self.take([16], F32)
        self.junk = self.take([DM], BF16)
        self.p13_top = self.top
        self.hoT = self.take([KC, 256], BF16)
        self.rbuf = {
            "sq": [self.take([512], BF16) for _ in range(3)],
            "rs": [self.take([512], F32) for _ in range(2)],
            "stb": [self.take([512], BF16) for _ in range(4)],
            "stf": [self.take([512], F32) for _ in range(3)],
        }
        cosb = self.take([512], F32)
        sinb = self.take([512], F32)
        self.p1save = (self.ssb, self.rbuf, self.junk)

        pieces, A1, B1 = self.ffn_slabs(D["f1g"], D["f1u"], D["f1d"])
        wv = D["w_in"].rearrange("(k p) c -> p k c", p=128)
        wslabs = [(0, 512), (512, 1024), (1024, 1536), (1536, 2048), (2048, 2560), (2560, 2968), (2968, 3288)]
        Aw = [((lambda buf, n=(b - a): buf[:, :, 0:n]), wv[:, :, a:b]) for (a, b) in wslabs]
        A = []
        Bl = []
        for t in range(NT):
            A += A1 + Aw
            Bl += B1
        stA = Stream(self, "WA", self.WA, A, live=2)
        stB = Stream(self, "WB", self.WB, Bl)
        nA1 = len(A1)
        m0, m1 = self.tcol(79), self.tcol(80)
        hT, hoT, xres = self.hT, self.hoT, self.xres
        scale_n = 128.0 ** -0.5

        for t in range(NT):
            a0 = t * (nA1 + len(Aw))
            b0 = t * len(B1)
            if t == 0:
                for c in range(4):
                    self.dma("sp", xres[:, c, :], D["x"][c * 128:(c + 1) * 128, :], [], [("xres", c)], sem=f"xr{c}")
            self.ffn(stA, a0, stB, b0, pieces, 0)
            for i in range(2):
                self.ts(self.xn, xres[:, 2 * i + 1, :], m1, None, ALU.mult, None, [("xres", 2 * i + 1), "tab"], ["xn"])
                self.stt(self.xn, xres[:, 2 * i, :], m0, self.xn, ALU.mult, ALU.add, [("xres", 2 * i), "tab", "xn"], ["xn"])
                r0 = (2 * t + i) * 128
                self.dma("sp", D["x1own"][r0:r0 + 128, :], self.xn, ["xn"], [("x1own", 2 * t + i)], sem="x1o")
            for c in range(4):
                self.norm_T(xres[:, c, :], ("xres", c), DM, 16,
                            (lambda k, c=c: hT[:, k, c * 128:(c + 1) * 128]), (lambda k: ("hT", k)), self.xn, 4,
                        dst4=(lambda k0, c=c: hT[:, k0:k0 + 4, c * 128:(c + 1) * 128]))
            if t + 1 < NT:
                for c in range(4):
                    r0 = (4 * (t + 1) + c) * 128
                    self.dma("sp", xres[:, c, :], D["x"][r0:r0 + 128, :], [], [("xres", c)], sem=f"xr{c}")
            tmp = self.xn.bitcast(BF16)[:, 0:KC * 128].rearrange("p (k n) -> p k n", n=128)
            for i in range(2):
                ev = hT[:, :, (2 * i) * 128:(2 * i + 1) * 128]
                od = hT[:, :, (2 * i + 1) * 128:(2 * i + 2) * 128]
                self.ts(tmp, od, m1, None, ALU.mult, None, [("hT", kk_) for kk_ in range(KC)] + ["tab"], ["xn"])
                self.stt(hoT[:, :, i * 128:(i + 1) * 128], ev, m0, tmp, ALU.mult, ALU.add, [("hT", kk_) for kk_ in range(KC)] + ["tab", "xn"], ["hoT"])
            ai = a0 + nA1
            tc0 = t * 512
            oc0 = t * 256
            qv = D["qT_s"].rearrange("p (j h n) -> p j h n", h=8, n=128)
            for sl in range(2):
                buf, key = stA.use(ai); ai += 1
                for hh in range(4):
                    h = sl * 4 + hh
                    bq = self.bank()
                    for k in range(KC):
                        self.mm(self.ps[bq][:, 0:256], buf[:, k, hh * 128:(hh + 1) * 128], hoT[:, k, :],
                                k == 0, k == KC - 1, [key, "hoT"], [("ps", bq)])
                    rs, rsk = self.rot("rs", 2)
                    self.fm_rstd([(self.ps[bq][:, 0:256], ("ps", bq), 128)], 256, 1.0, EPS * 128.0, rs, rsk)
                    st, stk = self.rot("stb", 4)
                    self.stt(st[:, 0:256], self.ps[bq][:, 0:256], self.tcol(64), rs[:, 0:256], ALU.mult, ALU.mult,
                             [("ps", bq), "tab", rsk], [stk])
                    self.dma("sp", qv[:, 2 * t:2 * t + 2, h, :], st[:, 0:256].rearrange("p (j n) -> p j n", n=128),
                             [stk], [("qT_s", t, h)], sem=self.ksem(stk))
            buf, key = stA.use(ai); ai += 1
            for idx, name in enumerate(("kcmpT_s", "kcmpT_s", "vcmpT_s", "vcmpT_s")):
                g = idx % 2
                bq = self.bank()
                for k in range(KC):
                    self.mm(self.ps[bq], buf[:, k, idx * 128:(idx + 1) * 128], hT[:, k, :], k == 0, k == KC - 1,
                            [key, ("hT", k)], [("ps", bq)])
                st, stk = self.rot("stb", 4)
                self.act(st, self.ps[bq], AF.Copy, [("ps", bq)], [stk])
                self.dma("sp", D[name][:, g * S + tc0:g * S + tc0 + 512], st, [stk], [(name, g, t)], sem=self.ksem(stk))
            for (kname, vname, gc) in (("kselT_s", "vsel_s", 66), ("kwinT_s", "vwin_s", 67)):
                buf, key = stA.use(ai); ai += 1
                for g in range(2):
                    bq = self.bank()
                    for k in range(KC):
                        self.mm(self.ps[bq], buf[:, k, g * 128:(g + 1) * 128], hT[:, k, :], k == 0, k == KC - 1,
                                [key, ("hT", k)], [("ps", bq)])
                    rs, rsk = self.rot("rs", 2)
                    self.fm_rstd([(self.ps[bq], ("ps", bq), 128)], 512, 1.0 / 128.0, EPS, rs, rsk)
                    st, stk = self.rot("stb", 4)
                    self.stt(st, self.ps[bq], self.tcol(gc), rs, ALU.mult, ALU.mult, [("ps", bq), "tab", rsk], [stk])
                    self.dma("sp", D[kname][:, g * S + tc0:g * S + tc0 + 512], st, [stk], [(kname, g, t)], sem=self.ksem(stk))
                for tc in range(4):
                    bq = self.bank()
                    for k in range(KC):
                        self.mm(self.ps[bq][:, 0:256], hT[:, k, tc * 128:(tc + 1) * 128], buf[:, k, 256:512],
                                k == 0, k == KC - 1, [key, ("hT", k)], [("ps", bq)])
                    st, stk = self.rot("stb", 4)
                    self.act(st[:, 0:256], self.ps[bq][:, 0:256], AF.Copy, [("ps", bq)], [stk])
                    r0 = tc0 + tc * 128
                    self.dma("sp", D[vname][r0:r0 + 128, :], st[:, 0:256], [stk], [(vname, 4 * t + tc)], sem=self.ksem(stk))
            buf, key = stA.use(ai); ai += 1
            for oc in range(2):
                bq = self.bank()
                for k in range(KC):
                    self.mm(self.ps[bq][:, 0:24], hoT[:, k, oc * 128:(oc + 1) * 128], buf[:, k, 0:24],
                            k == 0, k == KC - 1, [key, "hoT"], [("ps", bq)])
                st, stk = self.rot("stf", 3)
                self.act(st[:, 0:24], self.ps[bq][:, 0:24], AF.Sigmoid, [("ps", bq)], [stk])
                r0 = oc0 + oc * 128
                self.dma("sp", D["gates_s"][r0:r0 + 128, :], st[:, 0:24], [stk], [("gates_s", 2 * t + oc)], sem=self.ksem(stk))
            bqs = []
            for kq in range(3):
                bq = self.bank()
                bqs.append(bq)
                for k in range(KC):
                    self.mm(self.ps[bq][:, 0:256], buf[:, k, 24 + kq * 128:24 + (kq + 1) * 128], hoT[:, k, :],
                            k == 0, k == KC - 1, [key, "hoT"], [("ps", bq)])
            rs, rsk = self.rot("rs", 2)
            self.fm_rstd([(self.ps[b][:, 0:256], ("ps", b), 128) for b in bqs], 256, 1.0 / 384.0, EPS, rs, rsk)
            for kq in range(3):
                st, stk = self.rot("stb", 4)
                self.stt(st[:, 0:256], self.ps[bqs[kq]][:, 0:256], self.tcol(68 + kq), rs[:, 0:256], ALU.mult, ALU.mult,
                         [("ps", bqs[kq]), "tab", rsk], [stk])
                self.dma("sp", D["cqnT_s"][:, kq * self.SO + oc0:kq * self.SO + oc0 + 256], st[:, 0:256], [stk],
                         [("cqnT_s", kq, t)], sem=self.ksem(stk))
            buf, key = stA.use(ai); ai += 1
            bqs = []
            for kq in range(2):
                bq = self.bank()
                bqs.append(bq)
                for k in range(KC):
                    self.mm(self.ps[bq], buf[:, k, kq * 128:(kq + 1) * 128], hT[:, k, :], k == 0, k == KC - 1,
                            [key, ("hT", k)], [("ps", bq)])
            rs, rsk = self.rot("rs", 2)
            self.fm_rstd([(self.ps[b], ("ps", b), 128) for b in bqs], 512, 1.0 / 256.0, EPS, rs, rsk)
            for kq in range(2):
                st, stk = self.rot("stb", 4)
                self.stt(st, self.ps[bqs[kq]], self.tcol(71 + kq), rs, ALU.mult, ALU.mult,
                         [("ps", bqs[kq]), "tab", rsk], [stk])
                self.dma("sp", D["ckvnT_s"][:, kq * S + tc0:kq * S + tc0 + 512], st, [stk], [("ckvnT_s", kq, t)], sem=self.ksem(stk))
            br = self.bank()
            bs = self.bank()
            for k in range(KC):
                self.mm(self.ps[br][0:64, :], buf[:, k, 256:320], hT[:, k, :], k == 0, k == KC - 1, [key, ("hT", k)], [("ps", br)])
            for k in range(KC):
                self.mm(self.ps[bs][0:32, :], buf[:, k, 288:320], hT[:, k, :], k == 0, k == KC - 1, [key, ("hT", k)], [("ps", bs)])
            for k in range(KC):
                self.mm(self.ps[bs][32:64, :], buf[:, k, 256:288], hT[:, k, :], k == 0, k == KC - 1, [key, ("hT", k)], [("ps", bs)])
            st, stk = self.rot("stb", 4)
            self.act(st[0:64, :], self.ps[br][0:64, :], AF.Copy, [("ps", br)], [stk])
            self.dma("sp", D["kraw_s"][:, tc0:tc0 + 512], st[0:64, :], [stk], [("kraw_s", t)], sem=self.ksem(stk))
            self.dma("sp", cosb[0:64, :], D["cosk"][:, tc0:tc0 + 512], [], ["cosb"], sem="cs0")
            self.dma("sp", sinb[0:64, :], D["sink"][:, tc0:tc0 + 512], [], ["sinb"], sem="cs1")
            t1, t1k = self.rot("stf", 3)
            t2, t2k = self.rot("stf", 3)
            self.stt(t1[0:64, :], self.ps[br][0:64, :], self.tcol(77, 1, 64), cosb[0:64, :], ALU.mult, ALU.mult,
                     [("ps", br), "tab", "cosb"], [t1k])
            self.stt(t2[0:64, :], self.ps[bs][0:64, :], self.tcol(78, 1, 64), sinb[0:64, :], ALU.mult, ALU.mult,
                     [("ps", bs), "tab", "sinb"], [t2k])
            self.tt(t1[0:64, :], t1[0:64, :], t2[0:64, :], ALU.add, [t1k, t2k], [t1k])
            self.dma("sp", D["krot_s"][:, tc0:tc0 + 512], t1[0:64, :], [t1k], [("krot_s", t)], sem=self.ksem(t1k))

    def phase2(self):
        D, S, SO, NJ, NB = self.D, self.S, self.SO, self.NJ, self.NB
        self.s.barrier()
        self.top = self.cbase
        self.bankc = 0
        self.ssb = self.take([16], F32)
        self.junk = self.take([1024], BF16)
        self.rbuf = {
            "sq": [self.take([512], BF16) for _ in range(3)],
            "rs": [self.take([512], F32) for _ in range(2)],
            "stb": [self.take([512], BF16) for _ in range(4)],
            "stf": [self.take([512], F32) for _ in range(3)],
            "pt": [self.take([512], BF16) for _ in range(4)],
            "of": [self.take([1024], F32) for _ in range(2)],
            "sm": [self.take([64], F32) for _ in range(4)],
        }
        kcT = self.take([2, 128], BF16)
        vcaug = self.take([2, 162], BF16)
        qnT = self.take([8, SO], BF16)
        qrT = self.take([8, SO], BF16)
        mAB = self.take([2, 512], BF16)
        self.dma("pool", mAB, D["mAB"].rearrange("p (a n) -> p a n", n=512), [], ["mAB"], sem="t0")
        p2base = self.top

        kcA = self.take([2, S], BF16)
        vcA = self.take([2, S], BF16)
        w1k = self.take([32, 256], BF16)
        w1v = self.take([32, 256], BF16)
        w2k = self.take([2, 128], BF16)
        w2v = self.take([2, 128], BF16)
        posk = self.take([32], BF16)
        posv = self.take([32], BF16)
        ovb = self.take([32], F32)
        hid = [self.take([128], BF16) for _ in range(2)]
        self.dma("sp", kcA, D["kcmpT_s"].rearrange("p (g n) -> p g n", n=S), [("kcmpT_s", g, t) for g in range(2) for t in range(S // 512)], ["kcA"], sem="l0")
        self.dma("sp", vcA, D["vcmpT_s"].rearrange("p (g n) -> p g n", n=S), [("vcmpT_s", g, t) for g in range(2) for t in range(S // 512)], ["vcA"], sem="l1")
        self.dma("pool", w1k, D["cw1k"].rearrange("(l d) h -> d l h", d=128), [], ["w1k"], sem="l2")
        self.dma("pool", w1v, D["cw1v"].rearrange("(l d) h -> d l h", d=128), [], ["w1v"], sem="l3")
        self.dma("pool", w2k, D["cw2k"].rearrange("(c p) d -> p c d", p=128), [], ["w2k"], sem="l4")
        self.dma("pool", w2v, D["cw2v"].rearrange("(c p) d -> p c d", p=128), [], ["w2v"], sem="l5")
        self.dma("pool", posk, D["posk"], [], ["posk"], sem="l6")
        self.dma("pool", posv, D["posv"], [], ["posv"], sem="l7")
        self.dma("sp", ovb, D["ov"], [], ["ovb"], sem="l8")
        self.mset(vcaug, 0.0, ["vcaug"])
        self.mset(kcT, 0.0, ["kcT"])
        NC_ = (S - 32) // 16 + 1
        GA = 2.0 * (2.0 / np.pi) ** 0.5
        for (isv, srcA, srck, w1, w1key, w2, w2key, pos, poskey) in (
                (0, kcA, "kcA", w1k, "w1k", w2k, "w2k", posk, "posk"),
                (1, vcA, "vcA", w1v, "w1v", w2v, "w2v", posv, "posv")):
            sv = srcA.rearrange("p g (n r) -> p g n r", r=16)
            for g in range(2):
                for hc in range(2):
                    bh = self.bank()
                    bb = self.bank()
                    for l in range(32):
                        self.mm(self.ps[bh][:, 0:NC_], w1[:, l, hc * 128:(hc + 1) * 128],
                                sv[:, g, (l // 16):(l // 16) + NC_, l % 16], l == 0, l == 31,
                                [w1key, srck], [("ps", bh)])
                    for l in range(32):
                        self.mm(self.ps[bb][:, 0:1], w1[:, l, hc * 128:(hc + 1) * 128], pos[:, l:l + 1], l == 0, l == 31,
                                [w1key, poskey], [("ps", bb)])
                    sm, smk = self.rot("sm", 4)
                    self.cp(sm[:, 0:1], self.ps[bb][:, 0:1], [("ps", bb)], [smk])
                    u, uk = self.rot("stf", 3)
                    w_, wk = self.rot("stf", 3)
                    self.ts(u[:, 0:NC_], self.ps[bh][:, 0:NC_], sm[:, 0:1], None, ALU.add, None, [("ps", bh), smk], [uk])
                    self.tt(w_[:, 0:NC_], u[:, 0:NC_], u[:, 0:NC_], ALU.mult, [uk], [wk])
                    self.ts(w_[:, 0:NC_], w_[:, 0:NC_], 0.044715, 1.0, ALU.mult, ALU.add, [wk], [wk])
                    self.tt(w_[:, 0:NC_], w_[:, 0:NC_], u[:, 0:NC_], ALU.mult, [wk, uk], [wk])
                    self.act(w_[:, 0:NC_], w_[:, 0:NC_], AF.Sigmoid, [wk], [wk], scale=GA)
                    self.tt(hid[hc][:, 0:NC_], u[:, 0:NC_], w_[:, 0:NC_], ALU.mult, [uk, wk], [("hid", hc)])
                bo = self.bank()
                if not isv:
                    for hc in range(2):
                        self.mm(self.ps[bo][:, 0:NC_], w2[:, hc, :], hid[hc][:, 0:NC_], hc == 0, hc == 1,
                                [w2key, ("hid", hc)], [("ps", bo)])
                    rs, rsk = self.rot("rs", 2)
                    self.fm_rstd([(self.ps[bo][:, 0:NC_], ("ps", bo), 128)], NC_, 1.0 / 128.0, EPS, rs, rsk)
                    self.stt(kcT[:, g, 0:NC_], self.ps[bo][:, 0:NC_], self.tcol(65), rs[:, 0:NC_], ALU.mult, ALU.mult,
                             [("ps", bo), "tab", rsk], ["kcT"])
                else:
                    for hc in range(2):
                        self.mm(self.ps[bo][0:NC_, 0:128], hid[hc][:, 0:NC_], w2[:, hc, :], hc == 0, hc == 1,
                                [w2key, ("hid", hc)], [("ps", bo)])
                    self.cp(vcaug[0:NC_, g, 0:128], self.ps[bo][0:NC_, 0:128], [("ps", bo)], ["vcaug"])
                    self.mset(vcaug[0:NC_, g, 128:129], 1.0, ["vcaug"])
                    self.cp(vcaug[0:NC_, g, 129:161], ovb[0:NC_, :], ["ovb"], ["vcaug"])

        cqn = self.take([3, SO], BF16)
        wuq = self.take([3, 1536], BF16)
        cosq = self.take([SO], F32)
        sinq = self.take([SO], F32)
        self.dma("sp", cqn, D["cqnT_s"].rearrange("p (k n) -> p k n", n=SO), [], ["cqn"], sem="q0")
        self.dma("pool", wuq, D["w_uq"].rearrange("(k p) c -> p k c", p=128), [], ["wuq"], sem="q2")
        self.dma("sp", cosq[0:64, :], D["cosq"], [], ["cosq"], sem="q1")
        self.dma("sp", sinq[0:64, :], D["sinq"], [], ["sinq"], sem="q8")
        for h in range(8):
            for nb in range(SO // 512):
                cs = slice(nb * 512, (nb + 1) * 512)
                c0 = h * 192
                bn, br, bs = self.bank(), self.bank(), self.bank()
                for k in range(3):
                    self.mm(self.ps[bn], wuq[:, k, c0:c0 + 128], cqn[:, k, cs], k == 0, k == 2, ["wuq", "cqn"], [("ps", bn)])
                for k in range(3):
                    self.mm(self.ps[br][0:64, :], wuq[:, k, c0 + 128:c0 + 192], cqn[:, k, cs], k == 0, k == 2, ["wuq", "cqn"], [("ps", br)])
                for k in range(3):
                    self.mm(self.ps[bs][0:32, :], wuq[:, k, c0 + 160:c0 + 192], cqn[:, k, cs], k == 0, k == 2, ["wuq", "cqn"], [("ps", bs)])
                for k in range(3):
                    self.mm(self.ps[bs][32:64, :], wuq[:, k, c0 + 128:c0 + 160], cqn[:, k, cs], k == 0, k == 2, ["wuq", "cqn"], [("ps", bs)])
                rs, rsk = self.rot("rs", 2)
                self.fm_rstd([(self.ps[bn], ("ps", bn), 128), (self.ps[br][0:64, :], ("ps", br), 64)], 512,
                             1.0, EPS * 192.0, rs, rsk)
                self.stt(qnT[:, h, cs], self.ps[bn], self.tcol(73), rs, ALU.mult, ALU.mult, [("ps", bn), "tab", rsk], [("qnT", h)])
                t1, t1k = self.rot("stf", 3)
                t2, t2k = self.rot("stf", 3)
                self.stt(t1[0:64, :], self.ps[br][0:64, :], self.tcol(75, 1, 64), cosq[0:64, cs], ALU.mult, ALU.mult,
                         [("ps", br), "tab", "cosq"], [t1k])
                self.stt(t2[0:64, :], self.ps[bs][0:64, :], self.tcol(76, 1, 64), sinq[0:64, cs], ALU.mult, ALU.mult,
                         [("ps", bs), "tab", "sinq"], [t2k])
                self.tt(t1[0:64, :], t1[0:64, :], t2[0:64, :], ALU.add, [t1k, t2k], [t1k])
                self.tt(qrT[0:64, h, cs], t1[0:64, :], rs[0:64, :], ALU.mult, [t1k, rsk], [("qrT", h)])

        self.nsa(kcT, vcaug, mAB, p2base)
        self.mla(qnT, qrT, mAB, p2base)
        self.outnorm(p2base)

    def att_exp_pv(self, sb, nk, pvs, first, last):
        pt, ptk = self.rot("pt", 4)
        self.act(pt[0:nk, :], self.ps[sb][0:nk, :], AF.Exp, [("ps", sb)], [ptk])
        seen = set()
        for (oap, ob, h, rhs, rkeys) in pvs:
            st = first and (ob not in seen)
            seen.add(ob)
            self.mm(oap, pt[0:nk, h * 128:(h + 1) * 128], rhs, st, bool(last), [ptk] + rkeys, [("ps", ob)], sgc=True)

    def run_pipeline(self, tasks, skew):
        n = len(tasks)
        for i in range(n + skew):
            if i < n:
                tasks[i][0]()
            if i - skew >= 0:
                tasks[i - skew][1]()

    def fin_norm(self, oapf, okeys):
        sm, smk = self.rot("sm", 4)
        for i in range(4):
            self.ts(sm[:, i:i + 1], oapf(i)[:, 128:129], 1e-30, None, ALU.add, None, okeys, [smk])
        self.recip(sm[:, 0:4], sm[:, 0:4], [smk], [smk])
        return sm, smk

    def nsa(self, kcT, vcaug, mAB, base):
        D, S, SO, NJ, NB = self.D, self.S, self.SO, self.NJ, self.NB
        self.s.barrier()
        self.top = base
        kselT = self.take([2, S], BF16)
        kwinT = self.take([2, S], BF16)
        vsel = self.take([NB, 2, 130], BF16)
        vwin = self.take([NB, 2, 130], BF16)
        qT = self.take([NJ, 8, 128], BF16)
        gates = self.take([NJ, 24], F32)
        mW = self.take([6, 512], BF16)
        mC = self.take([NJ, 512], BF16)
        Ltab = self.take([NB, 128], BF16)
        Rs = self.take([NJ, 2, 512], BF16)
        Lc = self.take([128], BF16)
        keep = self.take([NJ, 32], F32)
        addt = self.take([NJ, 32], F32)
        oa_all = self.take([NJ, 1024], F32)
        self.dma("sp", kselT, D["kselT_s"].rearrange("p (g n) -> p g n", n=S), [], ["kselT"], sem="l0")
        self.dma("sp", kwinT, D["kwinT_s"].rearrange("p (g n) -> p g n", n=S), [], ["kwinT"], sem="l1")
        self.mset(vsel, 1.0, ["vsel"])
        self.mset(vwin, 1.0, ["vwin"])
        for g in range(2):
            self.dma("sp", vsel[:, :, g, 0:128], D["vsel_s"][:, g * 128:(g + 1) * 128].rearrange("(c p) d -> p c d", p=128),
                     [], ["vsel"], sem="l2")
            self.dma("sp", vwin[:, :, g, 0:128], D["vwin_s"][:, g * 128:(g + 1) * 128].rearrange("(c p) d -> p c d", p=128),
                     [], ["vwin"], sem="l3")
        self.dma("sp", qT, D["qT_s"].rearrange("p (j h n) -> p j h n", h=8, n=128), [], ["qT"], sem="l4")
        self.dma("sp", gates, D["gates_s"].rearrange("(j p) c -> p j c", p=128), [], ["gates"], sem="l5")
        self.dma("pool", mW.rearrange("p a n -> p (a n)"), D["mW"], [], ["mW"], sem="l6", cap=True)
        self.dma("pool", mC.rearrange("p a n -> p (a n)"), D["mC"], [], ["mC"], sem="l7", cap=True)
        self.dma("pool", Ltab[0:36], D["Ltab"].rearrange("p (a n) -> p a n", n=128), [], ["Ltab"], sem="l8")
        self.dma("pool", Rs[0:68].rearrange("p j g n -> p (j g n)"), D["Rs"], [], ["Rs"], sem="t1", cap=True)
        self.dma("pool", Lc[0:68], D["Lc"], [], ["Lc"], sem="t2")
        self.dma("sp", keep, D["keep"].rearrange("p (j n) -> p j n", n=32), [], ["keep"], sem="t3")
        self.dma("sp", addt, D["addt"].rearrange("p (j n) -> p j n", n=32), [], ["addt"], sem="t4")
        NC_ = (S - 32) // 16 + 1
        SB4 = (0, 1, 6, 7)

        def mk_oap(obase, ncol):
            return lambda i: self.ps[obase + i // 2][:, (i % 2) * 256:(i % 2) * 256 + ncol]

        def gate_update(j, g, br, oapf, okeys, sm, smk):
            gv = gates[:, j, :].rearrange("p (h b) -> p h b", b=3)[:, 4 * g:4 * g + 4, br]
            self.tt(sm[:, 4:8], sm[:, 0:4], gv, ALU.mult, [smk, "gates"], [smk])
            for i in range(4):
                h = 4 * g + i
                dst = oa_all[:, j, h * 128:(h + 1) * 128]
                if br == 0:
                    self.ts(dst, oapf(i)[:, 0:128], sm[:, 4 + i:5 + i], None, ALU.mult, None, okeys + [smk], [("oa", j, g)])
                else:
                    self.stt(dst, oapf(i)[:, 0:128], sm[:, 4 + i:5 + i], dst, ALU.mult, ALU.add,
                             okeys + [smk, ("oa", j, g)], [("oa", j, g)])

        def cmp_task(j, g, sb, obase, bt):
            qr = qT[:, j, 4 * g:4 * g + 4, :]
            oapf = mk_oap(obase, 161)
            okeys = [("ps", obase), ("ps", obase + 1)]
            nk = NC_

            def score():
                S_ = self.ps[sb]
                self.mm(S_[0:nk, :], kcT[:, g, 0:nk], qr, True, False, ["kcT", "qT"], [("ps", sb)])
                self.mm(S_[0:nk, :], Lc[64:68, 0:nk], Rs[64:68, j, g, :], False, False, ["Lc", "Rs"], [("ps", sb)])
                self.mm(S_[0:nk, :], self.identb[0:nk, 0:nk], mC[0:nk, j, :], False, True, ["identb", "mC"], [("ps", sb)])

            def finish():
                pvs = [(oapf(i), obase + i // 2, i, vcaug[0:nk, g, 0:161], ["vcaug"]) for i in range(4)]
                self.att_exp_pv(sb, nk, pvs, True, True)
                sm, smk = self.fin_norm(oapf, okeys)
                gate_update(j, g, 0, oapf, okeys, sm, smk)
                imp, impk = self.rot("sm", 4)
                self.ts(imp[:, 0:32], oapf(0)[:, 129:161], sm[:, 0:1], None, ALU.mult, None, okeys + [smk], [impk])
                for i in range(1, 4):
                    self.stt(imp[:, 0:32], oapf(i)[:, 129:161], sm[:, i:i + 1], imp[:, 0:32], ALU.mult, ALU.add,
                             okeys + [smk, impk], [impk])
                self.tt(imp[:, 0:32], imp[:, 0:32], keep[:, j, :], ALU.mult, [impk, "keep"], [impk])
                self.tt(imp[:, 0:32], imp[:, 0:32], addt[:, j, :], ALU.add, [impk, "addt"], [impk])
                self.s.add("dve", (lambda e, o_=imp[:, 32:40], i_=imp[:, 0:32]: e.max(o_, i_)), [impk], [impk])
                self.ts(imp[:, 0:32], imp[:, 0:32], imp[:, 39:40], -BIG, ALU.is_lt, ALU.mult, [impk], [impk])
                self.tr(self.ps[bt][0:32, 0:128], imp[:, 0:32], self.identf, [impk, "identf"], [("ps", bt)])
                for i in range(4):
                    self.cp(Rs[0:32, j, g, i * 128:(i + 1) * 128], self.ps[bt][0:32, 0:128], [("ps", bt), "Rs"], [("Rsel", j, g)])
            return score, finish

        tasks = []
        for j in range(NJ):
            for g in range(2):
                n = len(tasks)
                tasks.append(cmp_task(j, g, n % 2, 2 + 2 * (n % 2), 6 + n % 2))
        self.run_pipeline(tasks, 1)

        def sw_task(j, g, br, kb, sb, obase, first, last, store):
            qr = qT[:, j, 4 * g:4 * g + 4, :]
            oapf = mk_oap(obase, 129)
            okeys = [("ps", obase), ("ps", obase + 1)]

            def score():
                S_ = self.ps[sb]
                if br == 1:
                    self.mm(S_, kselT[:, g, kb * 128:(kb + 1) * 128], qr, True, False, ["kselT", "qT"], [("ps", sb)])
                    msk = kb >= 2 * j
                    self.mm(S_, Ltab[0:36, kb, :], Rs[0:36, j, g, :], False, not msk, ["Ltab", "Rs", ("Rsel", j, g)], [("ps", sb)])
                    if msk:
                        self.mm(S_, self.identb, mAB[:, kb - 2 * j, :], False, True, ["identb", "mAB"], [("ps", sb)])
                else:
                    o = kb - (2 * j - 4)
                    self.mm(S_, kwinT[:, g, kb * 128:(kb + 1) * 128], qr, True, False, ["kwinT", "qT"], [("ps", sb)])
                    self.mm(S_, Ltab[32:36, kb, :], Rs[32:36, j, g, :], False, False, ["Ltab", "Rs"], [("ps", sb)])
                    self.mm(S_, self.identb, mW[:, o, :], False, True, ["identb", "mW"], [("ps", sb)])

            def finish():
                vt, vk = (vsel, "vsel") if br == 1 else (vwin, "vwin")
                pvs = [(oapf(i), obase + i // 2, i, vt[:, kb, g, 0:129], [vk]) for i in range(4)]
                self.att_exp_pv(sb, 128, pvs, first, last)
                if last:
                    sm, smk = self.fin_norm(oapf, okeys)
                    gate_update(j, g, br, oapf, okeys, sm, smk)
                if store:
                    self.dma("sp", D["oa_s"][j * 128:(j + 1) * 128, :], oa_all[:, j, :], [("oa", j, 0), ("oa", j, 1)],
                             [("oa_s", j)], sem="so%d" % (j % 2))
            return score, finish

        tasks = []
        grp = 0
        for j in range(NJ):
            for g in range(2):
                for br in (1, 2):
                    if br == 1:
                        kbs = list(range(2 * j + 2))
                    else:
                        kbs = [2 * j - 4 + o for o in range(6) if 2 * j - 4 + o >= 0]
                    obase = 2 + 2 * (grp % 2)
                    grp += 1
                    for ki, kb in enumerate(kbs):
                        last = ki == len(kbs) - 1
                        tasks.append(sw_task(j, g, br, kb, SB4[len(tasks) % 4], obase, ki == 0, last,
                                             last and g == 1 and br == 2))
        self.run_pipeline(tasks, 2)

    def mla(self, qnT, qrT, mAB, base):
        D, S, SO, NJ, NB = self.D, self.S, self.SO, self.NJ, self.NB
        self.s.barrier()
        self.top = base
        ckvn = self.take([2, S], BF16)
        wukv = self.take([2, 2048], BF16)
        kraw = self.take([S], BF16)
        krot = self.take([S], F32)
        knT = self.take([4, S], BF16)
        krT = self.take([4, S], BF16)
        vaug = self.take([NB, 4, 130], BF16)
        self.dma("sp", ckvn, D["ckvnT_s"].rearrange("p (k n) -> p k n", n=S), [], ["ckvn"], sem="l0")
        self.dma("pool", wukv, D["w_ukv"].rearrange("(k p) c -> p k c", p=128), [], ["wukv"], sem="l2")
        self.dma("sp", kraw[0:64, :], D["kraw_s"], [], ["kraw"], sem="l1")
        self.dma("sp", krot[0:64, :], D["krot_s"], [], ["krot"], sem="l3")
        self.mset(vaug, 1.0, ["vaug"])
        self.act(kraw[0:64, :], kraw[0:64, :], AF.Square, ["kraw"], ["kraw"])
        SB4 = (0, 1, 6, 7)
        wv = wukv.rearrange("p k (h c) -> p k h c", c=256)
        grp = 0
        for hg in range(2):
            self.bankc = 2
            for i in range(4):
                h = 4 * hg + i
                for nb in range(S // 512):
                    cs = slice(nb * 512, (nb + 1) * 512)
                    bn = self.bank()
                    for k in range(2):
                        self.mm(self.ps[bn], wukv[:, k, h * 256:h * 256 + 128], ckvn[:, k, cs], k == 0, k == 1,
                                ["wukv", "ckvn"], [("ps", bn)])
                    sq, sqk = self.rot("sq", 3)
                    self.act(sq, self.ps[bn], AF.Square, [("ps", bn)], [sqk])
                    bss = self.bank()
                    self.mm(self.ps[bss], self.onesb, sq, True, False, [sqk, "onesb"], [("ps", bss)])
                    self.mm(self.ps[bss], self.onesb[0:64, :], kraw[0:64, cs], False, True, ["kraw", "onesb"], [("ps", bss)])
                    rs, rsk = self.rot("rs", 2)
                    self.act(rs, self.ps[bss], AF.Sqrt, [("ps", bss)], [rsk], bias=EPS, scale=1.0 / 192.0)
                    self.recip(rs, rs, [rsk], [rsk])
                    self.stt(knT[:, i, cs], self.ps[bn], self.tcol(74), rs, ALU.mult, ALU.mult, [("ps", bn), "tab", rsk], [("knT", i)])
                    self.tt(krT[0:64, i, cs], krot[0:64, cs], rs[0:64, :], ALU.mult, ["krot", rsk], [("krT", i)])
            for c in range(NB):
                bv = self.bank()
                for k in range(2):
                    self.mm(self.ps[bv], ckvn[:, k, c * 128:(c + 1) * 128], wv[:, k, 4 * hg:4 * hg + 4, 128:256], k == 0, k == 1,
                            ["wukv", "ckvn"], [("ps", bv)])
                self.cp(vaug[:, c, :, 0:128], self.ps[bv].rearrange("p (h d) -> p h d", d=128), [("ps", bv)], ["vaug"])

            def mla_task(j, kb, sb, obase, first, last):
                oapf = lambda i: self.ps[obase + i // 2][:, (i % 2) * 256:(i % 2) * 256 + 129]
                okeys = [("ps", obase), ("ps", obase + 1)]

                def score():
                    S_ = self.ps[sb]
                    msk = kb >= 2 * j
                    for i in range(4):
                        h = 4 * hg + i
                        self.mm(S_[:, i * 128:(i + 1) * 128], knT[:, i, kb * 128:(kb + 1) * 128], qnT[:, h, j * 128:(j + 1) * 128],
                                i == 0, False, [("knT", i), ("qnT", h)], [("ps", sb)], sgc=True)
                        self.mm(S_[:, i * 128:(i + 1) * 128], krT[0:64, i, kb * 128:(kb + 1) * 128], qrT[0:64, h, j * 128:(j + 1) * 128],
                                False, (i == 3) and not msk, [("krT", i), ("qrT", h)], [("ps", sb)], sgc=True)
                    if msk:
                        self.mm(S_, self.identb, mAB[:, kb - 2 * j, :], False, True, ["identb", "mAB"], [("ps", sb)], sgc=True)

                def finish():
                    pvs = [(oapf(i), obase + i // 2, i, vaug[:, kb, i, 0:129], ["vaug"]) for i in range(4)]
                    self.att_exp_pv(sb, 128, pvs, first, last)
                    if last:
                        sm, smk = self.fin_norm(oapf, okeys)
                        ob, obk = self.rot("of", 2)
                        for i in range(4):
                            self.ts(ob[:, i * 128:(i + 1) * 128], oapf(i)[:, 0:128], sm[:, i:i + 1], None, ALU.mult, None,
                                    okeys + [smk], [obk])
                        self.dma("sp", D["ob_s"][j * 128:(j + 1) * 128, hg * 512:(hg + 1) * 512], ob[:, 0:512], [obk],
                                 [("ob_s", j, hg)], sem=self.ksem(obk))
                return score, finish

            tasks = []
            for j in range(NJ):
                obase = 2 + 2 * (grp % 2)
                grp += 1
                nkb = 2 * j + 2
                for kb in range(nkb):
                    tasks.append(mla_task(j, kb, SB4[len(tasks) % 4], obase, kb == 0, kb == nkb - 1))
            self.run_pipeline(tasks, 2)

    def outnorm(self, base):
        D, SO, NJ = self.D, self.SO, self.NJ
        self.s.barrier()
        self.top = base
        xns = [self.take([1024], F32) for _ in range(2)]
        oin = [self.take([1024], F32) for _ in range(4)]
        mx = [self.take([16, 128], BF16) for _ in range(2)]
        mv = D["mixT_s"].rearrange("p (k n) -> p k n", n=SO)
        for j in range(NJ):
            m = mx[j % 2]
            mk = ("mx", j % 2)
            for half, (name, gcol) in enumerate((("oa_s", 48), ("ob_s", 56))):
                oi = 2 * (j % 2) + half
                o = oin[oi]
                ok = ("oin", oi)
                self.dma("sp", o, D[name][j * 128:(j + 1) * 128, :], [], [ok], sem=f"lo{oi}")
                self.norm_T(o, ok, 1024, gcol, (lambda k, m=m, half=half: m[:, half * 8 + k, :]),
                            (lambda k, j=j, half=half: ("mx", j % 2, half * 8 + k)), xns[half], 6 if half == 0 else 0,
                            xnkey=("xn", half))
            self.dma("sp", mv[:, :, j * 128:(j + 1) * 128], m, [("mx", j % 2, kk_) for kk_ in range(16)], [("mixT_s", j)],
                     sem=self.ksem(mk))

    def phase3(self):
        D, SO = self.D, self.SO
        self.s.barrier()
        self.top = self.p13_top
        self.bankc = 0
        self.ssb, self.rbuf, self.junk = self.p1save
        mixT = self.take([KC, 512], BF16)
        NT3 = SO // 512
        pieces, A2, B2 = self.ffn_slabs(D["f2g"], D["f2u"], D["f2d"])
        wov = D["w_out"].rearrange("(k p) c -> p k c", p=128)
        Ao = [((lambda buf: buf), wov[:, :, dc * 512:(dc + 1) * 512]) for dc in range(4)]
        A, Bl = [], []
        for t in range(NT3):
            A += Ao + A2
            Bl += B2
        stA = Stream(self, "WA", self.WA, A, live=2)
        stB = Stream(self, "WB", self.WB, Bl)
        xres = self.xres
        mv = D["mixT_s"].rearrange("p (k n) -> p k n", n=SO)
        for t in range(NT3):
            a0 = t * (len(Ao) + len(A2))
            b0 = t * len(B2)
            for c in range(4):
                r0 = (4 * t + c) * 128
                self.dma("sp", xres[:, c, :], D["x1own"][r0:r0 + 128, :], [], [("xres", c)], sem=f"xr{c}")
            self.dma("sp", mixT, mv[:, :, t * 512:(t + 1) * 512], [], ["mixT"], sem="lm")
            for dc in range(4):
                buf, key = stA.use(a0 + dc)
                for tc in range(4):
                    bnk = 4 + tc
                    for k in range(KC):
                        self.mm(self.ps[bnk], mixT[:, k, tc * 128:(tc + 1) * 128], buf[:, k, :], k == 0, k == KC - 1,
                                ["mixT", key], [("ps", bnk)])
                    xs = xres[:, tc, dc * 512:(dc + 1) * 512]
                    self.tt(xs, self.ps[bnk], xs, ALU.add, [("ps", bnk), ("xres", tc)], [("xres", tc)])
            self.ffn(stA, a0 + 4, stB, b0, pieces, 32)
            for c in range(4):
                r0 = (4 * t + c) * 128
                self.dma("sp", D["out"][r0:r0 + 128, :], xres[:, c, :], [("xres", c)], [("out", t, c)], sem=f"o{c}")
        self.s.barrier()


def build_nc(S, DFF):
    nc = bass.Bass("TRN2", target_bir_lowering=False)
    kb = KB(nc, S, DFF)
    kb.declare()
    kb.setup_mem()
    kb.consts()
    kb.phase1()
    kb.phase2()
    kb.phase3()
    kb.s.emit(kb.es)
    kb.es.close()
    return nc


def host_tables(S, p):
    NB = S // 128
    NJ = NB // 2
    f = np.float32
    slopes = (2.0 ** (-8.0 * np.arange(1, 9, dtype=np.float64) / 8)).astype(f)
    ql = np.arange(128)
    kl = np.arange(128)
    diag = np.where(kl[:, None] <= ql[None, :], 0.0, -BIG).astype(f)
    far = np.where(kl[:, None] > ql[None, :], 0.0, -BIG).astype(f)
    full = np.zeros((128, 128), f)
    empty = np.full((128, 128), -BIG, f)
    rep4 = lambda m: np.tile(m, (1, 4))
    mAB = np.stack([rep4(diag if p == 0 else full), rep4(empty if p == 0 else diag)], 1).reshape(128, 2 * 512)
    mw = []
    for o in range(6):
        d = p + 4 - o
        m = far if d == 4 else (full if 1 <= d <= 3 else (diag if d == 0 else empty))
        mw.append(rep4(m))
    mW = np.stack(mw, 1).reshape(128, 6 * 512)
    n = np.arange(128)
    mc = []
    for j in range(NJ):
        qb = 2 * j + p
        vis = (16 * n[:, None] + 31 <= 128 * qb + ql[None, :]) & (n[:, None] < 127)
        mc.append(rep4(np.where(vis, 0.0, -BIG).astype(f)))
    mC = np.stack(mc, 1).reshape(128, NJ * 512)
    Ltab = np.zeros((36, NB, 128), f)
    for kb in range(NB):
        for s in range(32):
            Ltab[s, kb, :] = (s == 2 * kb + kl // 64)
        Ltab[32, kb, :] = 1.0
        Ltab[33, kb, :] = 1.0
        Ltab[34, kb, :] = kl
        Ltab[35, kb, :] = 128.0 * kb
    Rs = np.zeros((68, NJ, 2, 512), f)
    col = np.arange(512)
    for j in range(NJ):
        qb = 2 * j + p
        for g in range(2):
            sl = slopes[4 * g + col // 128]
            qq = (col % 128).astype(f)
            Rs[32, j, g] = -sl * qq
            Rs[33, j, g] = -sl * f(128.0 * qb)
            Rs[34, j, g] = sl
            Rs[35, j, g] = sl
            Rs[64, j, g] = -sl * qq
            Rs[65, j, g] = -sl * f(128.0 * qb)
            Rs[66, j, g] = sl
            Rs[67, j, g] = f(15.5) * sl
    Lc = np.zeros((68, 128), f)
    Lc[64] = 1.0
    Lc[65] = 1.0
    Lc[66] = 16.0 * n
    Lc[67] = 1.0
    keep = np.zeros((128, NJ, 32), f)
    addt = np.zeros((128, NJ, 32), f)
    s = np.arange(32)
    for j in range(NJ):
        t = 128 * (2 * j + p) + ql
        cur = t // 64
        forced = (s[None, :] == 0) | (s[None, :] == cur[:, None])
        future = 64 * s[None, :] > t[:, None]
        keep[:, j, :] = 1.0 - forced - future
        addt[:, j, :] = 1e4 * forced + NEG * future
    ov = np.zeros((128, 32), f)
    ncmp = (S - 32) // 16 + 1
    for nn in range(ncmp):
        ov[nn] = (16 * nn < 64 * s + 64) & (16 * nn + 31 >= 64 * s)
    ov[:, S // 64:] = 0.0
    inv = (np.float32(10000.0) ** (-np.arange(32, dtype=f) / f(32))).astype(f)
    pos = np.arange(S).astype(f)
    ang = (pos[None, :] * inv[:, None]).astype(f)
    cs, sn = np.cos(ang).astype(f), np.sin(ang).astype(f)
    cosk = np.concatenate([cs, cs], 0)
    sink = np.concatenate([-sn, sn], 0)
    own = np.concatenate([128 * (2 * j + p) + ql for j in range(NJ)])
    cosq = cosk[:, own]
    sinq = sink[:, own]
    c = np.ascontiguousarray
    return dict(mAB=c(mAB), mW=c(mW), mC=c(mC), Ltab=c(Ltab.reshape(36, -1)), Rs=c(Rs.reshape(68, -1)), Lc=c(Lc),
                keep=c(keep.reshape(128, -1)), addt=c(addt.reshape(128, -1)), ov=c(ov),
                cosk=c(cosk), sink=c(sink), cosq=c(cosq), sinq=c(sinq))


def run(inputs, S, DFF, B):
    f = np.float32
    c = np.ascontiguousarray
    g = lambda k: np.asarray(inputs[k], dtype=f)[0]
    tab = np.zeros((128, NTAB), f)
    tab[:, 0:16] = g("ffn1_norm").reshape(16, 128).T
    tab[:, 16:32] = g("mix_norm").reshape(16, 128).T
    tab[:, 32:48] = g("ffn2_norm").reshape(16, 128).T
    tab[:, 48:56] = g("out_norm_nsa").reshape(8, 128).T
    tab[:, 56:64] = g("out_norm_mla").reshape(8, 128).T
    tab[:, 64] = g("nsa_q_norm")
    tab[:, 65:68] = g("nsa_k_norm").T
    tab[:, 68:71] = g("mla_q_a_norm").reshape(3, 128).T
    tab[:, 71:73] = g("mla_kv_a_norm").reshape(2, 128).T
    qn, kn = g("mla_q_norm"), g("mla_k_norm")
    tab[:, 73] = qn[0:128]
    tab[:, 74] = kn[0:128]
    tab[0:64, 75] = qn[128:192]
    tab[0:64, 76] = np.concatenate([qn[160:192], qn[128:160]])
    tab[0:64, 77] = kn[128:192]
    tab[0:64, 78] = np.concatenate([kn[160:192], kn[128:160]])
    shared = dict(
        f1g=c(g("ffn1_w_gate")), f1u=c(g("ffn1_w_up")), f1d=c(g("ffn1_w_down")),
        f2g=c(g("ffn2_w_gate")), f2u=c(g("ffn2_w_up")), f2d=c(g("ffn2_w_down")),
        w_in=c(g("w_in")), w_out=c(g("w_out")),
        cw1k=c(g("nsa_cmp_w1_k")), cw1v=c(g("nsa_cmp_w1_v")), cw2k=c(g("nsa_cmp_w2_k")), cw2v=c(g("nsa_cmp_w2_v")),
        w_uq=c(g("mla_w_uq")), w_ukv=c(g("mla_w_ukv")),
        posk=c(g("nsa_cmp_pos_k").T), posv=c(g("nsa_cmp_pos_v").T),
        ident=np.eye(128, dtype=f),
    )
    x = np.asarray(inputs["x"], dtype=f)
    tables = [host_tables(S, p) for p in range(2)]
    in_maps = []
    for core in range(2 * B):
        b, p = core // 2, core % 2
        m = dict(shared)
        m.update(tables[p])
        tb = tab.copy()
        tb[:, 79] = 1.0 - p
        tb[:, 80] = float(p)
        m["tab"] = tb
        m["x"] = c(x[b])
        in_maps.append(m)
    nc = build_nc(S, DFF)
    res = run_bass_kernel_spmd(nc, in_maps, core_ids=list(range(2 * B)))
    if DEBUG:
        LAST["res"] = res.results
    out = np.zeros((B, S, DM), f)
    NJ = S // 256
    for core in range(2 * B):
        b, p = core // 2, core % 2
        o = np.asarray(res.results[core]["out"]).reshape(NJ, 128, DM)
        for j in range(NJ):
            qb = 2 * j + p
            out[b, qb * 128:(qb + 1) * 128] = o[j]
    return out


def kernel(**inputs):
    return run(inputs, 2048, 5632, 4)
```

```python
import numpy as np
from contextlib import ExitStack
import concourse.bass as bass
import concourse.mybir as mybir
from concourse.bass_utils import run_bass_kernel_spmd

F32 = mybir.dt.float32
BF16 = mybir.dt.bfloat16
AF = mybir.ActivationFunctionType
ALU = mybir.AluOpType
AX = mybir.AxisListType

DM = 2048
KC = 16
IN_DIM = 3288
EPS = 1e-6
BIG = 30000.0
NEG = -1e30
GCH = 11
NTAB = 96
DEBUG = False
LAST = {}


class Op:
    __slots__ = ("eng", "fn", "deps", "signal", "val", "dma", "idx")


class Sched:
    CE = ("pe", "act", "dve", "pool")
    ALL = ("pe", "act", "dve", "pool", "sp")

    def __init__(self, nc):
        self.nc = nc
        self.ops = {e: [] for e in self.ALL}
        self.lastw = {}
        self.readers = {}
        self.dcount = {}
        self.last = {}
        self.n = 0

    def add(self, eng, fn, reads=(), writes=(), dma=None):
        op = Op()
        op.eng, op.fn, op.signal, op.dma, op.val = eng, fn, False, dma, None
        op.idx = self.n
        self.n += 1
        deps = {}

        def dep(d):
            if d.dma is None and d.eng == "pe" and eng == "pe":
                return
            k = ("d", d.dma) if d.dma is not None else ("e", d.eng)
            c = deps.get(k)
            if c is None or c.idx < d.idx:
                deps[k] = d

        for k in reads:
            w = self.lastw.get(k)
            if w is not None:
                dep(w)
        for k in writes:
            w = self.lastw.get(k)
            if w is not None:
                dep(w)
            for r in self.readers.get(k, ()):
                dep(r)
        op.deps = list(deps.values())
        for d in op.deps:
            d.signal = True
        for k in reads:
            self.readers.setdefault(k, []).append(op)
        for k in writes:
            self.lastw[k] = op
            self.readers[k] = []
        if dma is not None:
            self.dcount[dma] = self.dcount.get(dma, 0) + 16
            op.val = (dma, self.dcount[dma])
            self.last[("d", dma)] = op
        else:
            self.last[("e", eng)] = op
        self.ops[eng].append(op)
        return op

    def barrier(self):
        lasts = list(self.last.values())
        for e in self.ALL:
            op = Op()
            op.eng, op.fn, op.signal, op.dma, op.val = e, None, False, None, None
            op.idx = self.n
            self.n += 1
            op.deps = [d for d in lasts if not (d.dma is None and d.eng == e)]
            for d in op.deps:
                d.signal = True
            self.ops[e].append(op)
        self.lastw = {}
        self.readers = {}

    def emit(self, es):
        nc = self.nc
        for e in self.CE:
            c = 0
            for op in self.ops[e]:
                if op.fn is not None and op.dma is None and op.signal:
                    c += 1
                    op.val = ("prog_" + e, c)
        names = ["prog_" + e for e in self.CE] + sorted(self.dcount.keys())
        sems = {n: es.enter_context(nc.semaphore(n)) for n in names}
        block = es.enter_context(nc.Block())

        def mk(e):
            def body(eng):
                known = {}
                for op in self.ops[e]:
                    need = {}
                    for d in op.deps:
                        n, v = d.val
                        if need.get(n, 0) < v:
                            need[n] = v
                    for n, v in need.items():
                        if known.get(n, 0) < v:
                            eng.wait_ge(sems[n], v)
                            known[n] = v
                    if op.fn is None:
                        continue
                    ins = op.fn(eng)
                    if op.dma is not None:
                        ins.then_inc(sems[op.dma], 16)
                    elif op.signal:
                        ins.then_inc(sems["prog_" + e], 1)
            return body

        block.tensor(mk("pe"))
        block.scalar(mk("act"))
        block.vector(mk("dve"))
        block.gpsimd(mk("pool"))
        block.sync(mk("sp"))


class Stream:
    def __init__(self, kb, name, bufs, slabs, live=1):
        self.kb, self.name, self.bufs, self.slabs = kb, name, bufs, slabs
        self.nxt = 0
        self.live = live

    def _issue(self, i):
        dstf, src = self.slabs[i]
        b = i % len(self.bufs)
        self.kb.dma("pool", dstf(self.bufs[b]), src, [], [(self.name, b)], sem=f"{self.name}{b}")

    def use(self, i):
        nb = len(self.bufs)
        while self.nxt < len(self.slabs) and self.nxt <= i + nb - self.live:
            self._issue(self.nxt)
            self.nxt += 1
        b = i % nb
        return self.slabs[i][0](self.bufs[b]), (self.name, b)


class KB:
    def __init__(self, nc, S, DFF):
        self.nc = nc
        self.S, self.DFF = S, DFF
        self.NB = S // 128
        self.NJ = self.NB // 2
        self.SO = S // 2
        self.FC = DFF // 128
        self.NG = self.FC // GCH
        assert self.FC % GCH == 0 and S % 512 == 0 and self.SO % 512 == 0
        self.s = Sched(nc)
        self.es = ExitStack()
        self.rotc = {}
        self.bankc = 0
        self.ssi = 0

    def setup_mem(self):
        nc = self.nc
        self.AW = 52600
        self.arena = self.es.enter_context(nc.sbuf_tensor("arena", [128, self.AW], F32))
        self.pst = self.es.enter_context(nc.psum_tensor("ps", [128, 8 * 512], F32))
        self.ps = [self.pst[:, b * 512:(b + 1) * 512] for b in range(8)]
        self.top = 0

    def take(self, shape, dt):
        n = int(np.prod(shape))
        nbytes = n * (4 if dt == F32 else 2)
        n32 = (nbytes + 3) // 4
        n32 = (n32 + 7) // 8 * 8
        off = self.top
        self.top += n32
        assert self.top <= self.AW, f"arena overflow {self.top}"
        ap = self.arena[:, off:off + n32]
        if dt != F32:
            ap = ap.bitcast(dt)
        ap = ap[:, 0:n]
        if len(shape) == 2:
            ap = ap.rearrange("p (a b) -> p a b", b=shape[1])
        elif len(shape) == 3:
            ap = ap.rearrange("p (a b c) -> p a b c", b=shape[1], c=shape[2])
        return ap

    def bank(self):
        b = self.bankc % 8
        self.bankc += 1
        return b

    def mm(self, out, lhsT, rhs, start, stop, r, w, sgc=False):
        if sgc:
            self.s.add("pe", lambda e: e.matmul(out, lhsT, rhs, start=start, stop=stop, skip_group_check=True), r, w)
        else:
            self.s.add("pe", lambda e: e.matmul(out, lhsT, rhs, start=start, stop=stop), r, w)

    def tr(self, out, in_, ident, r, w):
        self.s.add("pe", lambda e: e.transpose(out, in_, ident), r, w)

    def act(self, out, in_, func, r, w, bias=None, scale=None):
        kw = {}
        if bias is not None:
            kw["bias"] = bias
        if scale is not None:
            kw["scale"] = scale
        self.s.add("act", lambda e: e.activation(out, in_, func, **kw), r, w)

    def ts(self, out, in0, s1, s2, op0, op1, r, w, eng="dve"):
        if op1 is None:
            self.s.add(eng, lambda e: e.tensor_scalar(out, in0, s1, None, op0), r, w)
        else:
            self.s.add(eng, lambda e: e.tensor_scalar(out, in0, s1, s2, op0, op1), r, w)

    def tt(self, out, in0, in1, op, r, w, eng="dve"):
        self.s.add(eng, lambda e: e.tensor_tensor(out, in0, in1, op), r, w)

    def stt(self, out, in0, sc, in1, op0, op1, r, w):
        self.s.add("dve", lambda e: e.scalar_tensor_tensor(out, in0, sc, in1, op0, op1), r, w)

    def cp(self, out, in_, r, w, eng="dve"):
        self.s.add(eng, lambda e: e.tensor_copy(out, in_), r, w)

    def red(self, out, in_, r, w):
        self.s.add("dve", lambda e: e.tensor_reduce(out, in_, AX.X, ALU.add), r, w)

    def recip(self, out, in_, r, w):
        self.s.add("dve", lambda e: e.reciprocal(out, in_), r, w)

    def mset(self, ap, val, w, eng="dve"):
        self.s.add(eng, lambda e: e.memset(ap, val), [], w)

    def dma(self, eng, out, in_, r, w, sem, cap=False):
        sem = ("g_" if eng == "pool" else "h_") + sem
        if cap:
            self.s.add(eng, lambda e: e.dma_start(out=out, in_=in_, max_dma_last_dim=4096), r, w, dma=sem)
        else:
            self.s.add(eng, lambda e: e.dma_start(out=out, in_=in_), r, w, dma=sem)

    def declare(self):
        nc, S, DFF, SO, NJ, NB = self.nc, self.S, self.DFF, self.SO, self.NJ, self.NB
        D = {}

        def inp(name, shape):
            D[name] = nc.dram_tensor(name, list(shape), F32, kind="ExternalInput").ap()

        def scr(name, shape, dt):
            D[name] = nc.dram_tensor(name, list(shape), dt, kind=("ExternalOutput" if DEBUG else "Internal")).ap()

        inp("x", [S, DM])
        for f in ("f1", "f2"):
            inp(f + "g", [DM, DFF]); inp(f + "u", [DM, DFF]); inp(f + "d", [DFF, DM])
        inp("w_in", [DM, IN_DIM]); inp("w_out", [DM, DM])
        inp("cw1k", [4096, 256]); inp("cw1v", [4096, 256]); inp("cw2k", [256, 128]); inp("cw2v", [256, 128])
        inp("w_uq", [384, 1536]); inp("w_ukv", [256, 2048])
        inp("tab", [128, NTAB]); inp("posk", [128, 32]); inp("posv", [128, 32]); inp("ident", [128, 128])
        inp("mAB", [128, 2 * 512]); inp("mW", [128, 6 * 512]); inp("mC", [128, NJ * 512])
        inp("Ltab", [36, NB * 128]); inp("Rs", [68, NJ * 2 * 512]); inp("Lc", [68, 128])
        inp("keep", [128, NJ * 32]); inp("addt", [128, NJ * 32]); inp("ov", [128, 32])
        inp("cosk", [64, S]); inp("sink", [64, S]); inp("cosq", [64, SO]); inp("sinq", [64, SO])
        D["out"] = nc.dram_tensor("out", [SO, DM], F32, kind="ExternalOutput").ap()
        scr("x1own", [SO, DM], F32)
        scr("qT_s", [128, NJ * 8 * 128], BF16)
        scr("kcmpT_s", [128, 2 * S], BF16); scr("vcmpT_s", [128, 2 * S], BF16)
        scr("kselT_s", [128, 2 * S], BF16); scr("kwinT_s", [128, 2 * S], BF16)
        scr("vsel_s", [S, 256], BF16); scr("vwin_s", [S, 256], BF16)
        scr("gates_s", [SO, 24], F32)
        scr("cqnT_s", [128, 3 * SO], BF16); scr("ckvnT_s", [128, 2 * S], BF16)
        scr("kraw_s", [64, S], BF16); scr("krot_s", [64, S], F32)
        scr("oa_s", [SO, 1024], F32); scr("ob_s", [SO, 1024], F32)
        scr("mixT_s", [128, 16 * SO], BF16)
        self.D = D

    def consts(self):
        D = self.D
        self.tab = self.take([NTAB], F32)
        self.identf = self.take([128], F32)
        self.identb = self.take([128], BF16)
        self.onesb = self.take([128], BF16)
        self.dma("sp", self.tab, D["tab"], [], ["tab"], sem="c0")
        self.dma("sp", self.identf, D["ident"], [], ["identf"], sem="c1")
        self.cp(self.identb, self.identf, ["identf"], ["identb"])
        self.mset(self.onesb, 1.0, ["onesb"])
        self.cbase = self.top

    def tcol(self, c, n=1, p=128):
        return self.tab[0:p, c:c + n]

    def norm_T(self, src, srckey, F, gcol, dst, dstkey, xn, tb0, xnkey="xn", dst4=None):
        nk = F // 128
        si = self.ssi % 4
        self.ssi += 1
        ss = self.ssb[:, 4 * si:4 * si + 4]
        k0_, k1_, k2_ = ("ss", si, 0), ("ss", si, 1), ("ss", si, 2)
        jk = self.junk
        self.act(jk[:, 0:F], src, AF.Square, [srckey], ["jk"])
        self.red(ss[:, 0:1], jk[:, 0:F], ["jk"], [k0_])
        self.act(ss[:, 1:2], ss[:, 0:1], AF.Sqrt, [k0_], [k1_], bias=EPS, scale=1.0 / F)
        self.recip(ss[:, 2:3], ss[:, 1:2], [k1_], [k2_])
        self.ts(xn[:, 0:F], src, ss[:, 2:3], None, ALU.mult, None, [srckey, k2_], [xnkey])
        for k0 in range(0, nk, 4):
            b = tb0 + (k0 // 4) % 2
            for kk in range(4):
                k = k0 + kk
                self.tr(self.ps[b][:, kk * 128:(kk + 1) * 128], xn[:, k * 128:(k + 1) * 128], self.identf,
                        [xnkey, "identf"], [("ps", b)])
            if dst4 is not None:
                gb = self.tab[:, gcol + k0:gcol + k0 + 4].to_broadcast([128, 4, 128]) if False else \
                    self.tab[:, gcol + k0:gcol + k0 + 4].unsqueeze(2).to_broadcast([128, 4, 128])
                self.tt(dst4(k0), self.ps[b].rearrange("p (a n) -> p a n", n=128), gb, ALU.mult,
                        [("ps", b), "tab"], [dstkey(k0 + kk) for kk in range(4)])
            else:
                for kk in range(4):
                    k = k0 + kk
                    self.ts(dst(k), self.ps[b][:, kk * 128:(kk + 1) * 128], self.tcol(gcol + k), None,
                            ALU.mult, None, [("ps", b), "tab"], [dstkey(k)])

    def ffn_slabs(self, wg, wu, wd):
        A, B = [], []
        wgv = wg.rearrange("(k p) c -> p k c", p=128)
        wuv = wu.rearrange("(k p) c -> p k c", p=128)
        wdv = wd.rearrange("(c p) n -> p c n", p=128)
        pieces = []
        for g in range(self.NG):
            c = 0
            while c < GCH:
                n = min(4, GCH - c)
                pieces.append((g, c, n))
                c += n
        for (g, c, n) in pieces:
            c0 = (g * GCH + c) * 128
            for wv in (wgv, wuv):
                A.append(((lambda buf, n=n: buf[:, :, 0:n * 128]), wv[:, :, c0:c0 + n * 128]))
        for g in range(self.NG):
            for dc in range(4):
                B.append(((lambda buf: buf), wdv[:, g * GCH:(g + 1) * GCH, dc * 512:(dc + 1) * 512]))
        return pieces, A, B

    def ffn(self, stA, a0, stB, b0, pieces, gcol):
        xres, hT, actT = self.xres, self.hT, self.actT
        for c in range(4):
            self.norm_T(xres[:, c, :], ("xres", c), DM, gcol,
                        (lambda k, c=c: hT[:, k, c * 128:(c + 1) * 128]), (lambda k: ("hT", k)), self.xn, 4,
                        dst4=(lambda k0, c=c: hT[:, k0:k0 + 4, c * 128:(c + 1) * 128]))
        ai = a0
        bi = b0
        cc = 0
        for g in range(self.NG):
            for (pg, c, n) in pieces:
                if pg != g:
                    continue
                bufG, keyG = stA.use(ai)
                bufU, keyU = stA.use(ai + 1)
                ai += 2
                for i in range(n):
                    ci = c + i
                    bg, bu = 2 * (cc % 2), 2 * (cc % 2) + 1
                    for (bnk, buf, key) in ((bg, bufG, keyG), (bu, bufU, keyU)):
                        for k in range(KC):
                            self.mm(self.ps[bnk], buf[:, k, i * 128:(i + 1) * 128], hT[:, k, :],
                                    k == 0, k == KC - 1, [key, ("hT", k)], [("ps", bnk)])
                    sg = self.sg[cc % 2]
                    self.act(sg, self.ps[bg], AF.Silu, [("ps", bg)], [("sg", cc % 2)])
                    self.tt(actT[:, ci, :], sg, self.ps[bu], ALU.mult, [("sg", cc % 2), ("ps", bu)], [("act", ci)])
                    cc += 1
            for dc in range(4):
                bufD, keyD = stB.use(bi)
                bi += 1
                for tc in range(4):
                    bnk = 4 + tc
                    for ci in range(GCH):
                        self.mm(self.ps[bnk], actT[:, ci, tc * 128:(tc + 1) * 128], bufD[:, ci, :],
                                ci == 0, ci == GCH - 1, [("act", ci), keyD], [("ps", bnk)])
                    xs = xres[:, tc, dc * 512:(dc + 1) * 512]
                    self.stt(xs, self.ps[bnk], 0.5, xs, ALU.mult, ALU.add, [("ps", bnk), ("xres", tc)], [("xres", tc)])
        return ai, bi

    def fm_rstd(self, parts, N, c1, c2, rs, rskey):
        bss = self.bank()
        sqs = []
        for i, (ap, key, P) in enumerate(parts):
            sq, sqk = self.rot("sq", 3)
            self.act(sq[0:P, 0:N], ap, AF.Square, [key], [sqk])
            sqs.append((sq, sqk, P))
        for i, (sq, sqk, P) in enumerate(sqs):
            self.mm(self.ps[bss][:, 0:N], self.onesb[0:P, :], sq[0:P, 0:N], i == 0, i == len(sqs) - 1,
                    [sqk, "onesb"], [("ps", bss)])
        self.act(rs[:, 0:N], self.ps[bss][:, 0:N], AF.Sqrt, [("ps", bss)], [rskey], bias=c2, scale=c1)
        self.recip(rs[:, 0:N], rs[:, 0:N], [rskey], [rskey])

    def ksem(self, key):
        return "s_%s%d" % (key[0], key[1])

    def rot(self, name, n):
        i = self.rotc.get(name, 0)
        self.rotc[name] = i + 1
        return self.rbuf[name][i % n], (name, i % n)

    def phase1(self):
        D, S = self.D, self.S
        NT = S // 512
        self.top = self.cbase
        self.xres = self.take([4, DM], F32)
        self.hT = self.take([KC, 512], BF16)
        self.actT = self.take([GCH, 512], BF16)
        self.WA = [self.take([KC, 512], BF16) for _ in range(4)]
        self.WB = [self.take([GCH, 512], BF16) for _ in range(3)]
        self.xn = self.take([DM], F32)
        self.sg = [self.take([512], F32) for _ in range(2)]
        self.ssb = self.take([16], F32)
        self.junk = self.take([DM], BF16)
        self.p13_top = self.top
        self.hoT = self.take([KC, 256], BF16)
        self.rbuf = {
            "sq": [self.take([512], BF16) for _ in range(3)],
            "rs": [self.take([512], F32) for _ in range(2)],
            "stb": [self.take([512], BF16) for _ in range(4)],
            "stf": [self.take([512], F32) for _ in range(3)],
        }
        cosb = self.take([512], F32)
        sinb = self.take([512], F32)
        self.p1save = (self.ssb, self.rbuf, self.junk)

        pieces, A1, B1 = self.ffn_slabs(D["f1g"], D["f1u"], D["f1d"])
        wv = D["w_in"].rearrange("(k p) c -> p k c", p=128)
        wslabs = [(0, 512), (512, 1024), (1024, 1536), (1536, 2048), (2048, 2560), (2560, 2968), (2968, 3288)]
        Aw = [((lambda buf, n=(b - a): buf[:, :, 0:n]), wv[:, :, a:b]) for (a, b) in wslabs]
        A = []
        Bl = []
        for t in range(NT):
            A += A1 + Aw
            Bl += B1
        stA = Stream(self, "WA", self.WA, A, live=2)
        stB = Stream(self, "WB", self.WB, Bl)
        nA1 = len(A1)
        m0, m1 = self.tcol(79), self.tcol(80)
        hT, hoT, xres = self.hT, self.hoT, self.xres
        scale_n = 128.0 ** -0.5

        for t in range(NT):
            a0 = t * (nA1 + len(Aw))
            b0 = t * len(B1)
            if t == 0:
                for c in range(4):
                    self.dma("sp", xres[:, c, :], D["x"][c * 128:(c + 1) * 128, :], [], [("xres", c)], sem=f"xr{c}")
            self.ffn(stA, a0, stB, b0, pieces, 0)
            for i in range(2):
                self.ts(self.xn, xres[:, 2 * i + 1, :], m1, None, ALU.mult, None, [("xres", 2 * i + 1), "tab"], ["xn"])
                self.stt(self.xn, xres[:, 2 * i, :], m0, self.xn, ALU.mult, ALU.add, [("xres", 2 * i), "tab", "xn"], ["xn"])
                r0 = (2 * t + i) * 128
                self.dma("sp", D["x1own"][r0:r0 + 128, :], self.xn, ["xn"], [("x1own", 2 * t + i)], sem="x1o")
            for c in range(4):
                self.norm_T(xres[:, c, :], ("xres", c), DM, 16,
                            (lambda k, c=c: hT[:, k, c * 128:(c + 1) * 128]), (lambda k: ("hT", k)), self.xn, 4,
                        dst4=(lambda k0, c=c: hT[:, k0:k0 + 4, c * 128:(c + 1) * 128]))
            if t + 1 < NT:
                for c in range(4):
                    r0 = (4 * (t + 1) + c) * 128
                    self.dma("sp", xres[:, c, :], D["x"][r0:r0 + 128, :], [], [("xres", c)], sem=f"xr{c}")
            tmp = self.xn.bitcast(BF16)[:, 0:KC * 128].rearrange("p (k n) -> p k n", n=128)
            for i in range(2):
                ev = hT[:, :, (2 * i) * 128:(2 * i + 1) * 128]
                od = hT[:, :, (2 * i + 1) * 128:(2 * i + 2) * 128]
                self.ts(tmp, od, m1, None, ALU.mult, None, [("hT", kk_) for kk_ in range(KC)] + ["tab"], ["xn"])
                self.stt(hoT[:, :, i * 128:(i + 1) * 128], ev, m0, tmp, ALU.mult, ALU.add, [("hT", kk_) for kk_ in range(KC)] + ["tab", "xn"], ["hoT"])
            ai = a0 + nA1
            tc0 = t * 512
            oc0 = t * 256
            qv = D["qT_s"].rearrange("p (j h n) -> p j h n", h=8, n=128)
            for sl in range(2):
                buf, key = stA.use(ai); ai += 1
                for hh in range(4):
                    h = sl * 4 + hh
                    bq = self.bank()
                    for k in range(KC):
                        self.mm(self.ps[bq][:, 0:256], buf[:, k, hh * 128:(hh + 1) * 128], hoT[:, k, :],
                                k == 0, k == KC - 1, [key, "hoT"], [("ps", bq)])
                    rs, rsk = self.rot("rs", 2)
                    self.fm_rstd([(self.ps[bq][:, 0:256], ("ps", bq), 128)], 256, 1.0, EPS * 128.0, rs, rsk)
                    st, stk = self.rot("stb", 4)
                    self.stt(st[:, 0:256], self.ps[bq][:, 0:256], self.tcol(64), rs[:, 0:256], ALU.mult, ALU.mult,
                             [("ps", bq), "tab", rsk], [stk])
                    self.dma("sp", qv[:, 2 * t:2 * t + 2, h, :], st[:, 0:256].rearrange("p (j n) -> p j n", n=128),
                             [stk], [("qT_s", t, h)], sem=self.ksem(stk))
            buf, key = stA.use(ai); ai += 1
            for idx, name in enumerate(("kcmpT_s", "kcmpT_s", "vcmpT_s", "vcmpT_s")):
                g = idx % 2
                bq = self.bank()
                for k in range(KC):
                    self.mm(self.ps[bq], buf[:, k, idx * 128:(idx + 1) * 128], hT[:, k, :], k == 0, k == KC - 1,
                            [key, ("hT", k)], [("ps", bq)])
                st, stk = self.rot("stb", 4)
                self.act(st, self.ps[bq], AF.Copy, [("ps", bq)], [stk])
                self.dma("sp", D[name][:, g * S + tc0:g * S + tc0 + 512], st, [stk], [(name, g, t)], sem=self.ksem(stk))
            for (kname, vname, gc) in (("kselT_s", "vsel_s", 66), ("kwinT_s", "vwin_s", 67)):
                buf, key = stA.use(ai); ai += 1
                for g in range(2):
                    bq = self.bank()
                    for k in range(KC):
                        self.mm(self.ps[bq], buf[:, k, g * 128:(g + 1) * 128], hT[:, k, :], k == 0, k == KC - 1,
                                [key, ("hT", k)], [("ps", bq)])
                    rs, rsk = self.rot("rs", 2)
                    self.fm_rstd([(self.ps[bq], ("ps", bq), 128)], 512, 1.0 / 128.0, EPS, rs, rsk)
                    st, stk = self.rot("stb", 4)
                    self.stt(st, self.ps[bq], self.tcol(gc), rs, ALU.mult, ALU.mult, [("ps", bq), "tab", rsk], [stk])
                    self.dma("sp", D[kname][:, g * S + tc0:g * S + tc0 + 512], st, [stk], [(kname, g, t)], sem=self.ksem(stk))
                for tc in range(4):
                    bq = self.bank()
                    for k in range(KC):
                        self.mm(self.ps[bq][:, 0:256], hT[:, k, tc * 128:(tc + 1) * 128], buf[:, k, 256:512],
                                k == 0, k == KC - 1, [key, ("hT", k)], [("ps", bq)])
                    st, stk = self.rot("stb", 4)
                    self.act(st[:, 0:256], self.ps[bq][:, 0:256], AF.Copy, [("ps", bq)], [stk])
                    r0 = tc0 + tc * 128
                    self.dma("sp", D[vname][r0:r0 + 128, :], st[:, 0:256], [stk], [(vname, 4 * t + tc)], sem=self.ksem(stk))
            buf, key = stA.use(ai); ai += 1
            for oc in range(2):
                bq = self.bank()
                for k in range(KC):
                    self.mm(self.ps[bq][:, 0:24], hoT[:, k, oc * 128:(oc + 1) * 128], buf[:, k, 0:24],
                            k == 0, k == KC - 1, [key, "hoT"], [("ps", bq)])
                st, stk = self.rot("stf", 3)
                self.act(st[:, 0:24], self.ps[bq][:, 0:24], AF.Sigmoid, [("ps", bq)], [stk])
                r0 = oc0 + oc * 128
                self.dma("sp", D["gates_s"][r0:r0 + 128, :], st[:, 0:24], [stk], [("gates_s", 2 * t + oc)], sem=self.ksem(stk))
            bqs = []
            for kq in range(3):
                bq = self.bank()
                bqs.append(bq)
                for k in range(KC):
                    self.mm(self.ps[bq][:, 0:256], buf[:, k, 24 + kq * 128:24 + (kq + 1) * 128], hoT[:, k, :],
                            k == 0, k == KC - 1, [key, "hoT"], [("ps", bq)])
            rs, rsk = self.rot("rs", 2)
            self.fm_rstd([(self.ps[b][:, 0:256], ("ps", b), 128) for b in bqs], 256, 1.0 / 384.0, EPS, rs, rsk)
            for kq in range(3):
                st, stk = self.rot("stb", 4)
                self.stt(st[:, 0:256], self.ps[bqs[kq]][:, 0:256], self.tcol(68 + kq), rs[:, 0:256], ALU.mult, ALU.mult,
                         [("ps", bqs[kq]), "tab", rsk], [stk])
                self.dma("sp", D["cqnT_s"][:, kq * self.SO + oc0:kq * self.SO + oc0 + 256], st[:, 0:256], [stk],
                         [("cqnT_s", kq, t)], sem=self.ksem(stk))
            buf, key = stA.use(ai); ai += 1
            bqs = []
            for kq in range(2):
                bq = self.bank()
                bqs.append(bq)
                for k in range(KC):
                    self.mm(self.ps[bq], buf[:, k, kq * 128:(kq + 1) * 128], hT[:, k, :], k == 0, k == KC - 1,
                            [key, ("hT", k)], [("ps", bq)])
            rs, rsk = self.rot("rs", 2)
            self.fm_rstd([(self.ps[b], ("ps", b), 128) for b in bqs], 512, 1.0 / 256.0, EPS, rs, rsk)
            for kq in range(2):
                st, stk = self.rot("stb", 4)
                self.stt(st, self.ps[bqs[kq]], self.tcol(71 + kq), rs, ALU.mult, ALU.mult,
                         [("ps", bqs[kq]), "tab", rsk], [stk])
                self.dma("sp", D["ckvnT_s"][:, kq * S + tc0:kq * S + tc0 + 512], st, [stk], [("ckvnT_s", kq, t)], sem=self.ksem(stk))
            br = self.bank()
            bs = self.bank()
            for k in range(KC):
                self.mm(self.ps[br][0:64, :], buf[:, k, 256:320], hT[:, k, :], k == 0, k == KC - 1, [key, ("hT", k)], [("ps", br)])
            for k in range(KC):
                self.mm(self.ps[bs][0:32, :], buf[:, k, 288:320], hT[:, k, :], k == 0, k == KC - 1, [key, ("hT", k)], [("ps", bs)])
            for k in range(KC):
                self.mm(self.ps[bs][32:64, :], buf[:, k, 256:288], hT[:, k, :], k == 0, k == KC - 1, [key, ("hT", k)], [("ps", bs)])
            st, stk = self.rot("stb", 4)
            self.act(st[0:64, :], self.ps[br][0:64, :], AF.Copy, [("ps", br)], [stk])
            self.dma("sp", D["kraw_s"][:, tc0:tc0 + 512], st[0:64, :], [stk], [("kraw_s", t)], sem=self.ksem(stk))
            self.dma("sp", cosb[0:64, :], D["cosk"][:, tc0:tc0 + 512], [], ["cosb"], sem="cs0")
            self.dma("sp", sinb[0:64, :], D["sink"][:, tc0:tc0 + 512], [], ["sinb"], sem="cs1")
            t1, t1k = self.rot("stf", 3)
            t2, t2k = self.rot("stf", 3)
            self.stt(t1[0:64, :], self.ps[br][0:64, :], self.tcol(77, 1, 64), cosb[0:64, :], ALU.mult, ALU.mult,
                     [("ps", br), "tab", "cosb"], [t1k])
            self.stt(t2[0:64, :], self.ps[bs][0:64, :], self.tcol(78, 1, 64), sinb[0:64, :], ALU.mult, ALU.mult,
                     [("ps", bs), "tab", "sinb"], [t2k])
            self.tt(t1[0:64, :], t1[0:64, :], t2[0:64, :], ALU.add, [t1k, t2k], [t1k])
            self.dma("sp", D["krot_s"][:, tc0:tc0 + 512], t1[0:64, :], [t1k], [("krot_s", t)], sem=self.ksem(t1k))

    def phase2(self):
        D, S, SO, NJ, NB = self.D, self.S, self.SO, self.NJ, self.NB
        self.s.barrier()
        self.top = self.cbase
        self.bankc = 0
        self.ssb = self.take([16], F32)
        self.junk = self.take([1024], BF16)
        self.rbuf = {
            "sq": [self.take([512], BF16) for _ in range(3)],
            "rs": [self.take([512], F32) for _ in range(2)],
            "stb": [self.take([512], BF16) for _ in range(4)],
            "stf": [self.take([512], F32) for _ in range(3)],
            "pt": [self.take([512], BF16) for _ in range(4)],
            "of": [self.take([1024], F32) for _ in range(2)],
            "sm": [self.take([64], F32) for _ in range(4)],
        }
        kcT = self.take([2, 128], BF16)
        vcaug = self.take([2, 162], BF16)
        qnT = self.take([8, SO], BF16)
        qrT = self.take([8, SO], BF16)
        mAB = self.take([2, 512], BF16)
        self.dma("pool", mAB, D["mAB"].rearrange("p (a n) -> p a n", n=512), [], ["mAB"], sem="t0")
        p2base = self.top

        kcA = self.take([2, S], BF16)
        vcA = self.take([2, S], BF16)
        w1k = self.take([32, 256], BF16)
        w1v = self.take([32, 256], BF16)
        w2k = self.take([2, 128], BF16)
        w2v = self.take([2, 128], BF16)
        posk = self.take([32], BF16)
        posv = self.take([32], BF16)
        ovb = self.take([32], F32)
        hid = [self.take([128], BF16) for _ in range(2)]
        self.dma("sp", kcA, D["kcmpT_s"].rearrange("p (g n) -> p g n", n=S), [("kcmpT_s", g, t) for g in range(2) for t in range(S // 512)], ["kcA"], sem="l0")
        self.dma("sp", vcA, D["vcmpT_s"].rearrange("p (g n) -> p g n", n=S), [("vcmpT_s", g, t) for g in range(2) for t in range(S // 512)], ["vcA"], sem="l1")
        self.dma("pool", w1k, D["cw1k"].rearrange("(l d) h -> d l h", d=128), [], ["w1k"], sem="l2")
        self.dma("pool", w1v, D["cw1v"].rearrange("(l d) h -> d l h", d=128), [], ["w1v"], sem="l3")
        self.dma("pool", w2k, D["cw2k"].rearrange("(c p) d -> p c d", p=128), [], ["w2k"], sem="l4")
        self.dma("pool", w2v, D["cw2v"].rearrange("(c p) d -> p c d", p=128), [], ["w2v"], sem="l5")
        self.dma("pool", posk, D["posk"], [], ["posk"], sem="l6")
        self.dma("pool", posv, D["posv"], [], ["posv"], sem="l7")
        self.dma("sp", ovb, D["ov"], [], ["ovb"], sem="l8")
        self.mset(vcaug, 0.0, ["vcaug"])
        self.mset(kcT, 0.0, ["kcT"])
        NC_ = (S - 32) // 16 + 1
        GA = 2.0 * (2.0 / np.pi) ** 0.5
        for (isv, srcA, srck, w1, w1key, w2, w2key, pos, poskey) in (
                (0, kcA, "kcA", w1k, "w1k", w2k, "w2k", posk, "posk"),
                (1, vcA, "vcA", w1v, "w1v", w2v, "w2v", posv, "posv")):
            sv = srcA.rearrange("p g (n r) -> p g n r", r=16)
            for g in range(2):
                for hc in range(2):
                    bh = self.bank()
                    bb = self.bank()
                    for l in range(32):
                        self.mm(self.ps[bh][:, 0:NC_], w1[:, l, hc * 128:(hc + 1) * 128],
                                sv[:, g, (l // 16):(l // 16) + NC_, l % 16], l == 0, l == 31,
                                [w1key, srck], [("ps", bh)])
                    for l in range(32):
                        self.mm(self.ps[bb][:, 0:1], w1[:, l, hc * 128:(hc + 1) * 128], pos[:, l:l + 1], l == 0, l == 31,
                                [w1key, poskey], [("ps", bb)])
                    sm, smk = self.rot("sm", 4)
                    self.cp(sm[:, 0:1], self.ps[bb][:, 0:1], [("ps", bb)], [smk])
                    u, uk = self.rot("stf", 3)
                    w_, wk = self.rot("stf", 3)
                    self.ts(u[:, 0:NC_], self.ps[bh][:, 0:NC_], sm[:, 0:1], None, ALU.add, None, [("ps", bh), smk], [uk])
                    self.tt(w_[:, 0:NC_], u[:, 0:NC_], u[:, 0:NC_], ALU.mult, [uk], [wk])
                    self.ts(w_[:, 0:NC_], w_[:, 0:NC_], 0.044715, 1.0, ALU.mult, ALU.add, [wk], [wk])
                    self.tt(w_[:, 0:NC_], w_[:, 0:NC_], u[:, 0:NC_], ALU.mult, [wk, uk], [wk])
                    self.act(w_[:, 0:NC_], w_[:, 0:NC_], AF.Sigmoid, [wk], [wk], scale=GA)
                    self.tt(hid[hc][:, 0:NC_], u[:, 0:NC_], w_[:, 0:NC_], ALU.mult, [uk, wk], [("hid", hc)])
                bo = self.bank()
                if not isv:
                    for hc in range(2):
                        self.mm(self.ps[bo][:, 0:NC_], w2[:, hc, :], hid[hc][:, 0:NC_], hc == 0, hc == 1,
                                [w2key, ("hid", hc)], [("ps", bo)])
                    rs, rsk = self.rot("rs", 2)
                    self.fm_rstd([(self.ps[bo][:, 0:NC_], ("ps", bo), 128)], NC_, 1.0 / 128.0, EPS, rs, rsk)
                    self.stt(kcT[:, g, 0:NC_], self.ps[bo][:, 0:NC_], self.tcol(65), rs[:, 0:NC_], ALU.mult, ALU.mult,
                             [("ps", bo), "tab", rsk], ["kcT"])
                else:
                    for hc in range(2):
                        self.mm(self.ps[bo][0:NC_, 0:128], hid[hc][:, 0:NC_], w2[:, hc, :], hc == 0, hc == 1,
                                [w2key, ("hid", hc)], [("ps", bo)])
                    self.cp(vcaug[0:NC_, g, 0:128], self.ps[bo][0:NC_, 0:128], [("ps", bo)], ["vcaug"])
                    self.mset(vcaug[0:NC_, g, 128:129], 1.0, ["vcaug"])
                    self.cp(vcaug[0:NC_, g, 129:161], ovb[0:NC_, :], ["ovb"], ["vcaug"])

        cqn = self.take([3, SO], BF16)
        wuq = self.take([3, 1536], BF16)
        cosq = self.take([SO], F32)
        sinq = self.take([SO], F32)
        self.dma("sp", cqn, D["cqnT_s"].rearrange("p (k n) -> p k n", n=SO), [], ["cqn"], sem="q0")
        self.dma("pool", wuq, D["w_uq"].rearrange("(k p) c -> p k c", p=128), [], ["wuq"], sem="q2")
        self.dma("sp", cosq[0:64, :], D["cosq"], [], ["cosq"], sem="q1")
        self.dma("sp", sinq[0:64, :], D["sinq"], [], ["sinq"], sem="q8")
        for h in range(8):
            for nb in range(SO // 512):
                cs = slice(nb * 512, (nb + 1) * 512)
                c0 = h * 192
                bn, br, bs = self.bank(), self.bank(), self.bank()
                for k in range(3):
                    self.mm(self.ps[bn], wuq[:, k, c0:c0 + 128], cqn[:, k, cs], k == 0, k == 2, ["wuq", "cqn"], [("ps", bn)])
                for k in range(3):
                    self.mm(self.ps[br][0:64, :], wuq[:, k, c0 + 128:c0 + 192], cqn[:, k, cs], k == 0, k == 2, ["wuq", "cqn"], [("ps", br)])
                for k in range(3):
                    self.mm(self.ps[bs][0:32, :], wuq[:, k, c0 + 160:c0 + 192], cqn[:, k, cs], k == 0, k == 2, ["wuq", "cqn"], [("ps", bs)])
                for k in range(3):
                    self.mm(self.ps[bs][32:64, :], wuq[:, k, c0 + 128:c0 + 160], cqn[:, k, cs], k == 0, k == 2, ["wuq", "cqn"], [("ps", bs)])
                rs, rsk = self.rot("rs", 2)
                self.fm_rstd([(self.ps[bn], ("ps", bn), 128), (self.ps[br][0:64, :], ("ps", br), 64)], 512,
                             1.0, EPS * 192.0, rs, rsk)
                self.stt(qnT[:, h, cs], self.ps[bn], self.tcol(73), rs, ALU.mult, ALU.mult, [("ps", bn), "tab", rsk], [("qnT", h)])
                t1, t1k = self.rot("stf", 3)
                t2, t2k = self.rot("stf", 3)
                self.stt(t1[0:64, :], self.ps[br][0:64, :], self.tcol(75, 1, 64), cosq[0:64, cs], ALU.mult, ALU.mult,
                         [("ps", br), "tab", "cosq"], [t1k])
                self.stt(t2[0:64, :], self.ps[bs][0:64, :], self.tcol(76, 1, 64), sinq[0:64, cs], ALU.mult, ALU.mult,
                         [("ps", bs), "tab", "sinq"], [t2k])
                self.tt(t1[0:64, :], t1[0:64, :], t2[0:64, :], ALU.add, [t1k, t2k], [t1k])
                self.tt(qrT[0:64, h, cs], t1[0:64, :], rs[0:64, :], ALU.mult, [t1k, rsk], [("qrT", h)], eng="pool")

        self.nsa(kcT, vcaug, mAB, p2base)
        self.mla(qnT, qrT, mAB, p2base)
        self.outnorm(p2base)

    def att_exp_pv(self, sb, nk, pvs, first, last):
        pt, ptk = self.rot("pt", 4)
        self.act(pt[0:nk, :], self.ps[sb][0:nk, :], AF.Exp, [("ps", sb)], [ptk])
        seen = set()
        for (oap, ob, h, rhs, rkeys) in pvs:
            st = first and (ob not in seen)
            seen.add(ob)
            self.mm(oap, pt[0:nk, h * 128:(h + 1) * 128], rhs, st, bool(last), [ptk] + rkeys, [("ps", ob)], sgc=True)

    def run_pipeline(self, tasks, skew):
        n = len(tasks)
        for i in range(n + skew):
            if i < n:
                tasks[i][0]()
            if i - skew >= 0:
                tasks[i - skew][1]()

    def fin_norm(self, oapf, okeys):
        sm, smk = self.rot("sm", 4)
        for i in range(4):
            self.ts(sm[:, i:i + 1], oapf(i)[:, 128:129], 1e-30, None, ALU.add, None, okeys, [smk])
        self.recip(sm[:, 0:4], sm[:, 0:4], [smk], [smk])
        return sm, smk

    def nsa(self, kcT, vcaug, mAB, base):
        D, S, SO, NJ, NB = self.D, self.S, self.SO, self.NJ, self.NB
        self.s.barrier()
        self.top = base
        kselT = self.take([2, S], BF16)
        kwinT = self.take([2, S], BF16)
        vsel = self.take([NB, 2, 130], BF16)
        vwin = self.take([NB, 2, 130], BF16)
        qT = self.take([NJ, 8, 128], BF16)
        gates = self.take([NJ, 24], F32)
        mW = self.take([6, 512], BF16)
        mC = self.take([NJ, 512], BF16)
        Ltab = self.take([NB, 128], BF16)
        Rs = self.take([NJ, 2, 512], BF16)
        Lc = self.take([128], BF16)
        keep = self.take([NJ, 32], F32)
        addt = self.take([NJ, 32], F32)
        oa_all = self.take([NJ, 1024], F32)
        self.dma("sp", kselT, D["kselT_s"].rearrange("p (g n) -> p g n", n=S), [], ["kselT"], sem="l0")
        self.dma("sp", kwinT, D["kwinT_s"].rearrange("p (g n) -> p g n", n=S), [], ["kwinT"], sem="l1")
        self.mset(vsel, 1.0, ["vsel"])
        self.mset(vwin, 1.0, ["vwin"])
        for g in range(2):
            self.dma("sp", vsel[:, :, g, 0:128], D["vsel_s"][:, g * 128:(g + 1) * 128].rearrange("(c p) d -> p c d", p=128),
                     [], ["vsel"], sem="l2")
            self.dma("sp", vwin[:, :, g, 0:128], D["vwin_s"][:, g * 128:(g + 1) * 128].rearrange("(c p) d -> p c d", p=128),
                     [], ["vwin"], sem="l3")
        self.dma("sp", qT, D["qT_s"].rearrange("p (j h n) -> p j h n", h=8, n=128), [], ["qT"], sem="l4")
        self.dma("sp", gates, D["gates_s"].rearrange("(j p) c -> p j c", p=128), [], ["gates"], sem="l5")
        self.dma("pool", mW.rearrange("p a n -> p (a n)"), D["mW"], [], ["mW"], sem="l6", cap=True)
        self.dma("pool", mC.rearrange("p a n -> p (a n)"), D["mC"], [], ["mC"], sem="l7", cap=True)
        self.dma("pool", Ltab[0:36], D["Ltab"].rearrange("p (a n) -> p a n", n=128), [], ["Ltab"], sem="l8")
        self.dma("pool", Rs[0:68].rearrange("p j g n -> p (j g n)"), D["Rs"], [], ["Rs"], sem="t1", cap=True)
        self.dma("pool", Lc[0:68], D["Lc"], [], ["Lc"], sem="t2")
        self.dma("sp", keep, D["keep"].rearrange("p (j n) -> p j n", n=32), [], ["keep"], sem="t3")
        self.dma("sp", addt, D["addt"].rearrange("p (j n) -> p j n", n=32), [], ["addt"], sem="t4")
        NC_ = (S - 32) // 16 + 1
        SB4 = (0, 1, 6, 7)

        def mk_oap(obase, ncol):
            return lambda i: self.ps[obase + i // 2][:, (i % 2) * 256:(i % 2) * 256 + ncol]

        def gate_update(j, g, br, oapf, okeys, sm, smk):
            gv = gates[:, j, :].rearrange("p (h b) -> p h b", b=3)[:, 4 * g:4 * g + 4, br]
            self.tt(sm[:, 4:8], sm[:, 0:4], gv, ALU.mult, [smk, "gates"], [smk])
            for i in range(4):
                h = 4 * g + i
                dst = oa_all[:, j, h * 128:(h + 1) * 128]
                if br == 0:
                    self.ts(dst, oapf(i)[:, 0:128], sm[:, 4 + i:5 + i], None, ALU.mult, None, okeys + [smk], [("oa", j, g)])
                else:
                    self.stt(dst, oapf(i)[:, 0:128], sm[:, 4 + i:5 + i], dst, ALU.mult, ALU.add,
                             okeys + [smk, ("oa", j, g)], [("oa", j, g)])

        def cmp_task(j, g, sb, obase, bt):
            qr = qT[:, j, 4 * g:4 * g + 4, :]
            oapf = mk_oap(obase, 161)
            okeys = [("ps", obase), ("ps", obase + 1)]
            nk = NC_

            def score():
                S_ = self.ps[sb]
                self.mm(S_[0:nk, :], kcT[:, g, 0:nk], qr, True, False, ["kcT", "qT"], [("ps", sb)])
                self.mm(S_[0:nk, :], Lc[64:68, 0:nk], Rs[64:68, j, g, :], False, False, ["Lc", "Rs"], [("ps", sb)])
                self.mm(S_[0:nk, :], self.identb[0:nk, 0:nk], mC[0:nk, j, :], False, True, ["identb", "mC"], [("ps", sb)])

            def finish():
                pvs = [(oapf(i), obase + i // 2, i, vcaug[0:nk, g, 0:161], ["vcaug"]) for i in range(4)]
                self.att_exp_pv(sb, nk, pvs, True, True)
                sm, smk = self.fin_norm(oapf, okeys)
                gate_update(j, g, 0, oapf, okeys, sm, smk)
                imp, impk = self.rot("sm", 4)
                self.ts(imp[:, 0:32], oapf(0)[:, 129:161], sm[:, 0:1], None, ALU.mult, None, okeys + [smk], [impk])
                for i in range(1, 4):
                    self.stt(imp[:, 0:32], oapf(i)[:, 129:161], sm[:, i:i + 1], imp[:, 0:32], ALU.mult, ALU.add,
                             okeys + [smk, impk], [impk])
                self.tt(imp[:, 0:32], imp[:, 0:32], keep[:, j, :], ALU.mult, [impk, "keep"], [impk])
                self.tt(imp[:, 0:32], imp[:, 0:32], addt[:, j, :], ALU.add, [impk, "addt"], [impk])
                self.s.add("dve", (lambda e, o_=imp[:, 32:40], i_=imp[:, 0:32]: e.max(o_, i_)), [impk], [impk])
                self.ts(imp[:, 0:32], imp[:, 0:32], imp[:, 39:40], -BIG, ALU.is_lt, ALU.mult, [impk], [impk])
                self.tr(self.ps[bt][0:32, 0:128], imp[:, 0:32], self.identf, [impk, "identf"], [("ps", bt)])
                for i in range(4):
                    self.cp(Rs[0:32, j, g, i * 128:(i + 1) * 128], self.ps[bt][0:32, 0:128], [("ps", bt), "Rs"], [("Rsel", j, g)])
            return score, finish

        tasks = []
        for j in range(NJ):
            for g in range(2):
                n = len(tasks)
                tasks.append(cmp_task(j, g, n % 2, 2 + 2 * (n % 2), 6 + n % 2))
        self.run_pipeline(tasks, 1)

        def sw_task(j, g, br, kb, sb, obase, first, last, store):
            qr = qT[:, j, 4 * g:4 * g + 4, :]
            oapf = mk_oap(obase, 129)
            okeys = [("ps", obase), ("ps", obase + 1)]

            def score():
                S_ = self.ps[sb]
                if br == 1:
                    self.mm(S_, kselT[:, g, kb * 128:(kb + 1) * 128], qr, True, False, ["kselT", "qT"], [("ps", sb)])
                    msk = kb >= 2 * j
                    self.mm(S_, Ltab[0:36, kb, :], Rs[0:36, j, g, :], False, not msk, ["Ltab", "Rs", ("Rsel", j, g)], [("ps", sb)])
                    if msk:
                        self.mm(S_, self.identb, mAB[:, kb - 2 * j, :], False, True, ["identb", "mAB"], [("ps", sb)])
                else:
                    o = kb - (2 * j - 4)
                    self.mm(S_, kwinT[:, g, kb * 128:(kb + 1) * 128], qr, True, False, ["kwinT", "qT"], [("ps", sb)])
                    self.mm(S_, Ltab[32:36, kb, :], Rs[32:36, j, g, :], False, False, ["Ltab", "Rs"], [("ps", sb)])
                    self.mm(S_, self.identb, mW[:, o, :], False, True, ["identb", "mW"], [("ps", sb)])

            def finish():
                vt, vk = (vsel, "vsel") if br == 1 else (vwin, "vwin")
                pvs = [(oapf(i), obase + i // 2, i, vt[:, kb, g, 0:129], [vk]) for i in range(4)]
                self.att_exp_pv(sb, 128, pvs, first, last)
                if last:
                    sm, smk = self.fin_norm(oapf, okeys)
                    gate_update(j, g, br, oapf, okeys, sm, smk)
                if store:
                    self.dma("sp", D["oa_s"][j * 128:(j + 1) * 128, :], oa_all[:, j, :], [("oa", j, 0), ("oa", j, 1)],
                             [("oa_s", j)], sem="so%d" % (j % 2))
            return score, finish

        tasks = []
        grp = 0
        for j in range(NJ):
            for g in range(2):
                for br in (1, 2):
                    if br == 1:
                        kbs = list(range(2 * j + 2))
                    else:
                        kbs = [2 * j - 4 + o for o in range(6) if 2 * j - 4 + o >= 0]
                    obase = 2 + 2 * (grp % 2)
                    grp += 1
                    for ki, kb in enumerate(kbs):
                        last = ki == len(kbs) - 1
                        tasks.append(sw_task(j, g, br, kb, SB4[len(tasks) % 4], obase, ki == 0, last,
                                             last and g == 1 and br == 2))
        self.run_pipeline(tasks, 2)

    def mla(self, qnT, qrT, mAB, base):
        D, S, SO, NJ, NB = self.D, self.S, self.SO, self.NJ, self.NB
        self.s.barrier()
        self.top = base
        ckvn = self.take([2, S], BF16)
        wukv = self.take([2, 2048], BF16)
        kraw = self.take([S], BF16)
        krot = self.take([S], F32)
        knT = self.take([4, S], BF16)
        krT = self.take([4, S], BF16)
        vaug = self.take([NB, 4, 130], BF16)
        self.dma("sp", ckvn, D["ckvnT_s"].rearrange("p (k n) -> p k n", n=S), [], ["ckvn"], sem="l0")
        self.dma("pool", wukv, D["w_ukv"].rearrange("(k p) c -> p k c", p=128), [], ["wukv"], sem="l2")
        self.dma("sp", kraw[0:64, :], D["kraw_s"], [], ["kraw"], sem="l1")
        self.dma("sp", krot[0:64, :], D["krot_s"], [], ["krot"], sem="l3")
        self.mset(vaug, 1.0, ["vaug"])
        self.act(kraw[0:64, :], kraw[0:64, :], AF.Square, ["kraw"], ["kraw"])
        SB4 = (0, 1, 6, 7)
        wv = wukv.rearrange("p k (h c) -> p k h c", c=256)
        grp = 0
        for hg in range(2):
            self.bankc = 2
            for i in range(4):
                h = 4 * hg + i
                for nb in range(S // 512):
                    cs = slice(nb * 512, (nb + 1) * 512)
                    bn = self.bank()
                    for k in range(2):
                        self.mm(self.ps[bn], wukv[:, k, h * 256:h * 256 + 128], ckvn[:, k, cs], k == 0, k == 1,
                                ["wukv", "ckvn"], [("ps", bn)])
                    sq, sqk = self.rot("sq", 3)
                    self.act(sq, self.ps[bn], AF.Square, [("ps", bn)], [sqk])
                    bss = self.bank()
                    self.mm(self.ps[bss], self.onesb, sq, True, False, [sqk, "onesb"], [("ps", bss)])
                    self.mm(self.ps[bss], self.onesb[0:64, :], kraw[0:64, cs], False, True, ["kraw", "onesb"], [("ps", bss)])
                    rs, rsk = self.rot("rs", 2)
                    self.act(rs, self.ps[bss], AF.Sqrt, [("ps", bss)], [rsk], bias=EPS, scale=1.0 / 192.0)
                    self.recip(rs, rs, [rsk], [rsk])
                    self.stt(knT[:, i, cs], self.ps[bn], self.tcol(74), rs, ALU.mult, ALU.mult, [("ps", bn), "tab", rsk], [("knT", i)])
                    self.tt(krT[0:64, i, cs], krot[0:64, cs], rs[0:64, :], ALU.mult, ["krot", rsk], [("krT", i)], eng="pool")
            for c in range(NB):
                bv = self.bank()
                for k in range(2):
                    self.mm(self.ps[bv], ckvn[:, k, c * 128:(c + 1) * 128], wv[:, k, 4 * hg:4 * hg + 4, 128:256], k == 0, k == 1,
                            ["wukv", "ckvn"], [("ps", bv)])
                self.cp(vaug[:, c, :, 0:128], self.ps[bv].rearrange("p (h d) -> p h d", d=128), [("ps", bv)], ["vaug"])

            def mla_task(j, kb, sb, obase, first, last):
                oapf = lambda i: self.ps[obase + i // 2][:, (i % 2) * 256:(i % 2) * 256 + 129]
                okeys = [("ps", obase), ("ps", obase + 1)]

                def score():
                    S_ = self.ps[sb]
                    msk = kb >= 2 * j
                    for i in range(4):
                        h = 4 * hg + i
                        self.mm(S_[:, i * 128:(i + 1) * 128], knT[:, i, kb * 128:(kb + 1) * 128], qnT[:, h, j * 128:(j + 1) * 128],
                                i == 0, False, [("knT", i), ("qnT", h)], [("ps", sb)], sgc=True)
                        self.mm(S_[:, i * 128:(i + 1) * 128], krT[0:64, i, kb * 128:(kb + 1) * 128], qrT[0:64, h, j * 128:(j + 1) * 128],
                                False, (i == 3) and not msk, [("krT", i), ("qrT", h)], [("ps", sb)], sgc=True)
                    if msk:
                        self.mm(S_, self.identb, mAB[:, kb - 2 * j, :], False, True, ["identb", "mAB"], [("ps", sb)], sgc=True)

                def finish():
                    pvs = [(oapf(i), obase + i // 2, i, vaug[:, kb, i, 0:129], ["vaug"]) for i in range(4)]
                    self.att_exp_pv(sb, 128, pvs, first, last)
                    if last:
                        sm, smk = self.fin_norm(oapf, okeys)
                        ob, obk = self.rot("of", 2)
                        for i in range(4):
                            self.ts(ob[:, i * 128:(i + 1) * 128], oapf(i)[:, 0:128], sm[:, i:i + 1], None, ALU.mult, None,
                                    okeys + [smk], [obk])
                        self.dma("sp", D["ob_s"][j * 128:(j + 1) * 128, hg * 512:(hg + 1) * 512], ob[:, 0:512], [obk],
                                 [("ob_s", j, hg)], sem=self.ksem(obk))
                return score, finish

            tasks = []
            for j in range(NJ):
                obase = 2 + 2 * (grp % 2)
                grp += 1
                nkb = 2 * j + 2
                for kb in range(nkb):
                    tasks.append(mla_task(j, kb, SB4[len(tasks) % 4], obase, kb == 0, kb == nkb - 1))
            self.run_pipeline(tasks, 2)

    def outnorm(self, base):
        D, SO, NJ = self.D, self.SO, self.NJ
        self.s.barrier()
        self.top = base
        xns = [self.take([1024], F32) for _ in range(2)]
        oin = [self.take([1024], F32) for _ in range(4)]
        mx = [self.take([16, 128], BF16) for _ in range(2)]
        mv = D["mixT_s"].rearrange("p (k n) -> p k n", n=SO)
        for j in range(NJ):
            m = mx[j % 2]
            mk = ("mx", j % 2)
            for half, (name, gcol) in enumerate((("oa_s", 48), ("ob_s", 56))):
                oi = 2 * (j % 2) + half
                o = oin[oi]
                ok = ("oin", oi)
                self.dma("sp", o, D[name][j * 128:(j + 1) * 128, :], [], [ok], sem=f"lo{oi}")
                self.norm_T(o, ok, 1024, gcol, (lambda k, m=m, half=half: m[:, half * 8 + k, :]),
                            (lambda k, j=j, half=half: ("mx", j % 2, half * 8 + k)), xns[half], 6 if half == 0 else 0,
                            xnkey=("xn", half))
            self.dma("sp", mv[:, :, j * 128:(j + 1) * 128], m, [("mx", j % 2, kk_) for kk_ in range(16)], [("mixT_s", j)],
                     sem=self.ksem(mk))

    def phase3(self):
        D, SO = self.D, self.SO
        self.s.barrier()
        self.top = self.p13_top
        self.bankc = 0
        self.ssb, self.rbuf, self.junk = self.p1save
        mixT = self.take([KC, 512], BF16)
        NT3 = SO // 512
        pieces, A2, B2 = self.ffn_slabs(D["f2g"], D["f2u"], D["f2d"])
        wov = D["w_out"].rearrange("(k p) c -> p k c", p=128)
        Ao = [((lambda buf: buf), wov[:, :, dc * 512:(dc + 1) * 512]) for dc in range(4)]
        A, Bl = [], []
        for t in range(NT3):
            A += Ao + A2
            Bl += B2
        stA = Stream(self, "WA", self.WA, A, live=2)
        stB = Stream(self, "WB", self.WB, Bl)
        xres = self.xres
        mv = D["mixT_s"].rearrange("p (k n) -> p k n", n=SO)
        for t in range(NT3):
            a0 = t * (len(Ao) + len(A2))
            b0 = t * len(B2)
            for c in range(4):
                r0 = (4 * t + c) * 128
                self.dma("sp", xres[:, c, :], D["x1own"][r0:r0 + 128, :], [], [("xres", c)], sem=f"xr{c}")
            self.dma("sp", mixT, mv[:, :, t * 512:(t + 1) * 512], [], ["mixT"], sem="lm")
            for dc in range(4):
                buf, key = stA.use(a0 + dc)
                for tc in range(4):
                    bnk = 4 + tc
                    for k in range(KC):
                        self.mm(self.ps[bnk], mixT[:, k, tc * 128:(tc + 1) * 128], buf[:, k, :], k == 0, k == KC - 1,
                                ["mixT", key], [("ps", bnk)])
                    xs = xres[:, tc, dc * 512:(dc + 1) * 512]
                    self.tt(xs, self.ps[bnk], xs, ALU.add, [("ps", bnk), ("xres", tc)], [("xres", tc)])
            self.ffn(stA, a0 + 4, stB, b0, pieces, 32)
            for c in range(4):
                r0 = (4 * t + c) * 128
                self.dma("sp", D["out"][r0:r0 + 128, :], xres[:, c, :], [("xres", c)], [("out", t, c)], sem=f"o{c}")
        self.s.barrier()


def build_nc(S, DFF):
    nc = bass.Bass("TRN2", target_bir_lowering=False)
    kb = KB(nc, S, DFF)
    kb.declare()
    kb.setup_mem()
    kb.consts()
    kb.phase1()
    kb.phase2()
    kb.phase3()
    kb.s.emit(kb.es)
    kb.es.close()
    return nc


def host_tables(S, p):
    NB = S // 128
    NJ = NB // 2
    f = np.float32
    slopes = (2.0 ** (-8.0 * np.arange(1, 9, dtype=np.float64) / 8)).astype(f)
    ql = np.arange(128)
    kl = np.arange(128)
    diag = np.where(kl[:, None] <= ql[None, :], 0.0, -BIG).astype(f)
    far = np.where(kl[:, None] > ql[None, :], 0.0, -BIG).astype(f)
    full = np.zeros((128, 128), f)
    empty = np.full((128, 128), -BIG, f)
    rep4 = lambda m: np.tile(m, (1, 4))
    mAB = np.stack([rep4(diag if p == 0 else full), rep4(empty if p == 0 else diag)], 1).reshape(128, 2 * 512)
    mw = []
    for o in range(6):
        d = p + 4 - o
        m = far if d == 4 else (full if 1 <= d <= 3 else (diag if d == 0 else empty))
        mw.append(rep4(m))
    mW = np.stack(mw, 1).reshape(128, 6 * 512)
    n = np.arange(128)
    mc = []
    for j in range(NJ):
        qb = 2 * j + p
        vis = (16 * n[:, None] + 31 <= 128 * qb + ql[None, :]) & (n[:, None] < 127)
        mc.append(rep4(np.where(vis, 0.0, -BIG).astype(f)))
    mC = np.stack(mc, 1).reshape(128, NJ * 512)
    Ltab = np.zeros((36, NB, 128), f)
    for kb in range(NB):
        for s in range(32):
            Ltab[s, kb, :] = (s == 2 * kb + kl // 64)
        Ltab[32, kb, :] = 1.0
        Ltab[33, kb, :] = 1.0
        Ltab[34, kb, :] = kl
        Ltab[35, kb, :] = 128.0 * kb
    Rs = np.zeros((68, NJ, 2, 512), f)
    col = np.arange(512)
    for j in range(NJ):
        qb = 2 * j + p
        for g in range(2):
            sl = slopes[4 * g + col // 128]
            qq = (col % 128).astype(f)
            Rs[32, j, g] = -sl * qq
            Rs[33, j, g] = -sl * f(128.0 * qb)
            Rs[34, j, g] = sl
            Rs[35, j, g] = sl
            Rs[64, j, g] = -sl * qq
            Rs[65, j, g] = -sl * f(128.0 * qb)
            Rs[66, j, g] = sl
            Rs[67, j, g] = f(15.5) * sl
    Lc = np.zeros((68, 128), f)
    Lc[64] = 1.0
    Lc[65] = 1.0
    Lc[66] = 16.0 * n
    Lc[67] = 1.0
    keep = np.zeros((128, NJ, 32), f)
    addt = np.zeros((128, NJ, 32), f)
    s = np.arange(32)
    for j in range(NJ):
        t = 128 * (2 * j + p) + ql
        cur = t // 64
        forced = (s[None, :] == 0) | (s[None, :] == cur[:, None])
        future = 64 * s[None, :] > t[:, None]
        keep[:, j, :] = 1.0 - forced - future
        addt[:, j, :] = 1e4 * forced + NEG * future
    ov = np.zeros((128, 32), f)
    ncmp = (S - 32) // 16 + 1
    for nn in range(ncmp):
        ov[nn] = (16 * nn < 64 * s + 64) & (16 * nn + 31 >= 64 * s)
    ov[:, S // 64:] = 0.0
    inv = (np.float32(10000.0) ** (-np.arange(32, dtype=f) / f(32))).astype(f)
    pos = np.arange(S).astype(f)
    ang = (pos[None, :] * inv[:, None]).astype(f)
    cs, sn = np.cos(ang).astype(f), np.sin(ang).astype(f)
    cosk = np.concatenate([cs, cs], 0)
    sink = np.concatenate([-sn, sn], 0)
    own = np.concatenate([128 * (2 * j + p) + ql for j in range(NJ)])
    cosq = cosk[:, own]
    sinq = sink[:, own]
    c = np.ascontiguousarray
    return dict(mAB=c(mAB), mW=c(mW), mC=c(mC), Ltab=c(Ltab.reshape(36, -1)), Rs=c(Rs.reshape(68, -1)), Lc=c(Lc),
                keep=c(keep.reshape(128, -1)), addt=c(addt.reshape(128, -1)), ov=c(ov),
                cosk=c(cosk), sink=c(sink), cosq=c(cosq), sinq=c(sinq))


def run(inputs, S, DFF, B):
    f = np.float32
    c = np.ascontiguousarray
    g = lambda k: np.asarray(inputs[k], dtype=f)[0]
    tab = np.zeros((128, NTAB), f)
    tab[:, 0:16] = g("ffn1_norm").reshape(16, 128).T
    tab[:, 16:32] = g("mix_norm").reshape(16, 128).T
    tab[:, 32:48] = g("ffn2_norm").reshape(16, 128).T
    tab[:, 48:56] = g("out_norm_nsa").reshape(8, 128).T
    tab[:, 56:64] = g("out_norm_mla").reshape(8, 128).T
    tab[:, 64] = g("nsa_q_norm")
    tab[:, 65:68] = g("nsa_k_norm").T
    tab[:, 68:71] = g("mla_q_a_norm").reshape(3, 128).T
    tab[:, 71:73] = g("mla_kv_a_norm").reshape(2, 128).T
    qn, kn = g("mla_q_norm"), g("mla_k_norm")
    tab[:, 73] = qn[0:128]
    tab[:, 74] = kn[0:128]
    tab[0:64, 75] = qn[128:192]
    tab[0:64, 76] = np.concatenate([qn[160:192], qn[128:160]])
    tab[0:64, 77] = kn[128:192]
    tab[0:64, 78] = np.concatenate([kn[160:192], kn[128:160]])
    shared = dict(
        f1g=c(g("ffn1_w_gate")), f1u=c(g("ffn1_w_up")), f1d=c(g("ffn1_w_down")),
        f2g=c(g("ffn2_w_gate")), f2u=c(g("ffn2_w_up")), f2d=c(g("ffn2_w_down")),
        w_in=c(g("w_in")), w_out=c(g("w_out")),
        cw1k=c(g("nsa_cmp_w1_k")), cw1v=c(g("nsa_cmp_w1_v")), cw2k=c(g("nsa_cmp_w2_k")), cw2v=c(g("nsa_cmp_w2_v")),
        w_uq=c(g("mla_w_uq")), w_ukv=c(g("mla_w_ukv")),
        posk=c(g("nsa_cmp_pos_k").T), posv=c(g("nsa_cmp_pos_v").T),
        ident=np.eye(128, dtype=f),
    )
    x = np.asarray(inputs["x"], dtype=f)
    tables = [host_tables(S, p) for p in range(2)]
    in_maps = []
    for core in range(2 * B):
        b, p = core // 2, core % 2
        m = dict(shared)
        m.update(tables[p])
        tb = tab.copy()
        tb[:, 79] = 1.0 - p
        tb[:, 80] = float(p)
        m["tab"] = tb
        m["x"] = c(x[b])
        in_maps.append(m)
    nc = build_nc(S, DFF)
    res = run_bass_kernel_spmd(nc, in_maps, core_ids=list(range(2 * B)))
    if DEBUG:
        LAST["res"] = res.results
    out = np.zeros((B, S, DM), f)
    NJ = S // 256
    for core in range(2 * B):
        b, p = core // 2, core % 2
        o = np.asarray(res.results[core]["out"]).reshape(NJ, 128, DM)
        for j in range(NJ):
            qb = 2 * j + p
            out[b, qb * 128:(qb + 1) * 128] = o[j]
    return out


def kernel(**inputs):
    return run(inputs, 2048, 5632, 4)
```
